# Optimizing a Trainium2 kernel written in Bass

```python
import math
import jax
import jax.numpy as jnp
from jax import lax
import numpy as np


D_MODEL = 1024
BATCH = 8
SEQ = 4096
DEPTH = 2

GRID_W = 64
CTX_LEN = 256
HEAD_DIM = 64
ATTN_W = D_MODEL // 2
N_Q_HEADS = ATTN_W // HEAD_DIM
N_KV_HEADS = max(1, N_Q_HEADS // 4)
GQA_GROUP = N_Q_HEADS // N_KV_HEADS
KV_W = N_KV_HEADS * HEAD_DIM
ROPE_AXIS_DIM = HEAD_DIM // 2
ROPE_THETA = 10000.0
Q_BLOCK = 128
FOURIER_W = D_MODEL // 4
FOURIER_GROUPS = 4
FOURIER_GROUP_DIM = FOURIER_W // FOURIER_GROUPS
HYENA_W = D_MODEL - ATTN_W - FOURIER_W
HYENA_BANDS = 16
HYENA_EMB = 1 + 2 * HYENA_BANDS
HYENA_FILTER_W = 64
HYENA_FAST_DECAY = 0.3
HYENA_SLOW_DECAY = 1.5
HYENA_TARGET = 1e-2
MIX_W = ATTN_W + FOURIER_W + HYENA_W
Q0 = 0
K0 = ATTN_W
V0 = K0 + KV_W
F0 = V0 + KV_W
H0 = F0 + FOURIER_W
IN_W = H0 + 3 * HYENA_W
D_FF = int(math.ceil(8 * D_MODEL / 3 / 128)) * 128
NORM_EPS = 1e-6

kernel_name = 'hybrid_attn_fourier_hyena_dit'


def rmsnorm(x, g):
    xf = x.astype(jnp.float32)
    y = xf * lax.rsqrt(jnp.mean(xf * xf, axis=-1, keepdims=True) + NORM_EPS)
    return (y * g.astype(jnp.float32)).astype(x.dtype)


def dwconv3(u, w, b):
    up = jnp.pad(u, ((0, 0), (1, 1), (0, 0)))
    return up[:, :-2] * w[0] + up[:, 1:-1] * w[1] + up[:, 2:] * w[2] + b


def rope2d(x, ang_r, ang_c):
    def rot(xa, ang):
        cos = jnp.cos(ang)[:, None, :].astype(xa.dtype)
        sin = jnp.sin(ang)[:, None, :].astype(xa.dtype)
        x1, x2 = jnp.split(xa, 2, axis=-1)
        return jnp.concatenate([x1 * cos - x2 * sin, x2 * cos + x1 * sin], axis=-1)
    xr, xc = jnp.split(x, 2, axis=-1)
    return jnp.concatenate([rot(xr, ang_r), rot(xc, ang_c)], axis=-1)


def attend(q, k, v):
    s = jnp.einsum('bqhgd,bkhd->bhgqk', q, k).astype(jnp.float32) * (HEAD_DIM ** -0.5)
    p = jax.nn.softmax(s, axis=-1).astype(v.dtype)
    return jnp.einsum('bhgqk,bkhd->bqhgd', p, v)


def blocked_attention(q, k, v):
    B, L = q.shape[0], q.shape[1]
    nb = L // Q_BLOCK
    qb = q.reshape(B, nb, Q_BLOCK, N_KV_HEADS, GQA_GROUP, HEAD_DIM).swapaxes(0, 1)
    ob = lax.map(lambda qq: attend(qq, k, v), qb)
    return ob.swapaxes(0, 1).reshape(B, L, ATTN_W)


def fourier_mix(f, w):
    B, L, _ = f.shape
    fg = f.astype(jnp.float32).reshape(B, L, FOURIER_GROUPS, FOURIER_GROUP_DIM)
    y = jnp.fft.fft2(fg, axes=(1, 3), norm='ortho').real.astype(f.dtype)
    return y.reshape(B, L, FOURIER_W) @ w


def hyena_filter(L, w1, b1, fr1, w2, b2, fr2, w3):
    f32 = jnp.float32
    t = jnp.linspace(0.0, 1.0, L, dtype=f32)[:, None]
    bands = jnp.linspace(1e-4, HYENA_BANDS - 1, HYENA_BANDS, dtype=f32)
    ang = (2.0 * math.pi) * jnp.arange(L, dtype=f32)[:, None] / L * bands[None, :]
    z = jnp.concatenate([t, jnp.cos(ang), -jnp.sin(ang)], axis=-1)
    h = jnp.sin(fr1.astype(f32) * (z @ w1.astype(f32) + b1.astype(f32)))
    h = jnp.sin(fr2.astype(f32) * (h @ w2.astype(f32) + b2.astype(f32)))
    h = h @ w3.astype(f32)
    deltas = jnp.abs(jnp.linspace(math.log(HYENA_TARGET) / HYENA_SLOW_DECAY,
                                  math.log(HYENA_TARGET) / HYENA_FAST_DECAY, HYENA_W, dtype=f32))
    decay = jnp.exp(-t * deltas[None, :])
    h_fwd = h[:, :HYENA_W] * decay
    h_bwd = h[:, HYENA_W:] * decay
    two = jnp.concatenate([h_fwd, jnp.zeros((1, HYENA_W), f32), h_bwd[:0:-1]], axis=0)
    return two / jnp.sum(jnp.abs(two), axis=0, keepdims=True)


def hyena_longconv(u, filt, bias):
    L = u.shape[1]
    n = 2 * L
    uf = jnp.fft.rfft(u.astype(jnp.float32), n=n, axis=1)
    kf = jnp.fft.rfft(filt, n=n, axis=0)
    y = jnp.fft.irfft(uf * kf[None], n=n, axis=1)[:, :L]
    return (y + u.astype(jnp.float32) * bias.astype(jnp.float32)).astype(u.dtype)


def hyena_mix(p, conv_w, conv_b, filt, bias):
    uc = dwconv3(p, conv_w, conv_b)
    x0, x1, v = jnp.split(uc, 3, axis=-1)
    return x0 * hyena_longconv(x1 * v, filt, bias)


def conv_ffn(h, w_up, w_c, b_c, w_down):
    u = dwconv3(h @ w_up, w_c, b_c)
    gate, val = jnp.split(u, 2, axis=-1)
    return (jax.nn.silu(gate) * val) @ w_down


def setup_inputs(seed: int = 0) -> dict:
    key = jax.random.key(seed)
    ks = jax.random.split(key, 32)
    f32 = jnp.float32

    def nrm(k, shape, scale):
        return jax.random.normal(k, shape, f32) * scale

    def gain(k, shape):
        return 1.0 + 0.05 * jax.random.normal(k, shape, f32)

    return {
        'x': nrm(ks[0], (BATCH, SEQ, D_MODEL), 1.0),
        'c': nrm(ks[1], (BATCH, D_MODEL), 1.0),
        'ctx': nrm(ks[2], (BATCH, CTX_LEN, D_MODEL), 1.0),
        'c_ctx': nrm(ks[3], (D_MODEL,), 1.0),
        'w_mod': nrm(ks[4], (DEPTH, D_MODEL, 6 * D_MODEL), 0.5 * D_MODEL ** -0.5),
        'b_mod': nrm(ks[5], (DEPTH, 6 * D_MODEL), 0.02),
        'g_pre_mix': gain(ks[6], (DEPTH, D_MODEL)),
        'g_post_mix': gain(ks[7], (DEPTH, D_MODEL)),
        'g_pre_ffn': gain(ks[8], (DEPTH, D_MODEL)),
        'g_post_ffn': gain(ks[9], (DEPTH, D_MODEL)),
        'w_in': nrm(ks[10], (DEPTH, D_MODEL, IN_W), D_MODEL ** -0.5),
        'g_q': gain(ks[11], (DEPTH, HEAD_DIM)),
        'g_k': gain(ks[12], (DEPTH, HEAD_DIM)),
        'w_fourier': nrm(ks[13], (DEPTH, FOURIER_W, FOURIER_W), FOURIER_W ** -0.5),
        'w_hy_conv': nrm(ks[14], (DEPTH, 3, 3 * HYENA_W), 3 ** -0.5),
        'b_hy_conv': nrm(ks[15], (DEPTH, 3 * HYENA_W), 0.02),
        'hy_w1': nrm(ks[16], (DEPTH, HYENA_EMB, HYENA_FILTER_W), HYENA_EMB ** -0.5),
        'hy_b1': nrm(ks[17], (DEPTH, HYENA_FILTER_W), 0.1),
        'hy_fr1': gain(ks[18], (DEPTH, HYENA_FILTER_W)),
        'hy_w2': nrm(ks[19], (DEPTH, HYENA_FILTER_W, HYENA_FILTER_W), HYENA_FILTER_W ** -0.5),
        'hy_b2': nrm(ks[20], (DEPTH, HYENA_FILTER_W), 0.1),
        'hy_fr2': gain(ks[21], (DEPTH, HYENA_FILTER_W)),
        'hy_w3': nrm(ks[22], (DEPTH, HYENA_FILTER_W, 2 * HYENA_W), HYENA_FILTER_W ** -0.5),
        'hy_bias': nrm(ks[23], (DEPTH, HYENA_W), 0.1),
        'w_out': nrm(ks[24], (DEPTH, MIX_W, D_MODEL), MIX_W ** -0.5),
        'w_up': nrm(ks[25], (DEPTH, D_MODEL, 2 * D_FF), D_MODEL ** -0.5),
        'w_ffn_conv': nrm(ks[26], (DEPTH, 3, 2 * D_FF), 3 ** -0.5),
        'b_ffn_conv': nrm(ks[27], (DEPTH, 2 * D_FF), 0.02),
        'w_down': nrm(ks[28], (DEPTH, D_FF, D_MODEL), D_FF ** -0.5),
    }


def reference(x, c, ctx, c_ctx, w_mod, b_mod, g_pre_mix, g_post_mix, g_pre_ffn, g_post_ffn,
              w_in, g_q, g_k, w_fourier, w_hy_conv, b_hy_conv, hy_w1, hy_b1, hy_fr1,
              hy_w2, hy_b2, hy_fr2, hy_w3, hy_bias, w_out, w_up, w_ffn_conv, b_ffn_conv, w_down):
    B, L, _ = x.shape
    C = ctx.shape[1]
    ROWS = L // GRID_W
    row = jnp.repeat(jnp.arange(ROWS, dtype=jnp.float32), GRID_W)
    col = jnp.tile(jnp.arange(GRID_W, dtype=jnp.float32), ROWS)
    inv = ROPE_THETA ** (-jnp.arange(0, ROPE_AXIS_DIM, 2, dtype=jnp.float32) / ROPE_AXIS_DIM)
    ang_r = row[:, None] * inv[None, :]
    ang_c = col[:, None] * inv[None, :]

    for i in range(DEPTH):
        last = i == DEPTH - 1
        mod_x = (jax.nn.silu(c) @ w_mod[i] + b_mod[i])[:, None, :]
        mod_c = (jax.nn.silu(c_ctx) @ w_mod[i] + b_mod[i])[None, None, :]
        sh1, sc1, ga1, sh2, sc2, ga2 = jnp.split(mod_x, 6, axis=-1)
        csh1, csc1, cga1, csh2, csc2, cga2 = jnp.split(mod_c, 6, axis=-1)
        hy_params = (hy_w1[i], hy_b1[i], hy_fr1[i], hy_w2[i], hy_b2[i], hy_fr2[i], hy_w3[i])

        hx = rmsnorm(x, g_pre_mix[i]) * (1.0 + sc1) + sh1
        hc = rmsnorm(ctx, g_pre_mix[i]) * (1.0 + csc1) + csh1
        px = hx @ w_in[i]
        pc = hc @ (w_in[i][:, K0:F0] if last else w_in[i])
        pc_kv = pc if last else pc[..., K0:F0]

        kc = rmsnorm(pc_kv[..., :KV_W].reshape(B, C, N_KV_HEADS, HEAD_DIM), g_k[i])
        vc = pc_kv[..., KV_W:].reshape(B, C, N_KV_HEADS, HEAD_DIM)
        q = rope2d(rmsnorm(px[..., Q0:K0].reshape(B, L, N_Q_HEADS, HEAD_DIM), g_q[i]), ang_r, ang_c)
        k = rope2d(rmsnorm(px[..., K0:V0].reshape(B, L, N_KV_HEADS, HEAD_DIM), g_k[i]), ang_r, ang_c)
        v = px[..., V0:F0].reshape(B, L, N_KV_HEADS, HEAD_DIM)
        k_all = jnp.concatenate([kc, k], axis=1)
        v_all = jnp.concatenate([vc, v], axis=1)
        attn_x = blocked_attention(q.reshape(B, L, N_KV_HEADS, GQA_GROUP, HEAD_DIM), k_all, v_all)
        four_x = fourier_mix(px[..., F0:H0], w_fourier[i])
        hy_x = hyena_mix(px[..., H0:], w_hy_conv[i], b_hy_conv[i], hyena_filter(L, *hy_params), hy_bias[i])
        mix_x = jnp.concatenate([attn_x, four_x, hy_x], axis=-1) @ w_out[i]
        x = x + ga1 * rmsnorm(mix_x, g_post_mix[i])

        if not last:
            qc = rmsnorm(pc[..., Q0:K0].reshape(B, C, N_Q_HEADS, HEAD_DIM), g_q[i])
            attn_c = attend(qc.reshape(B, C, N_KV_HEADS, GQA_GROUP, HEAD_DIM), kc, vc).reshape(B, C, ATTN_W)
            four_c = fourier_mix(pc[..., F0:H0], w_fourier[i])
            hy_c = hyena_mix(pc[..., H0:], w_hy_conv[i], b_hy_conv[i], hyena_filter(C, *hy_params), hy_bias[i])
            mix_c = jnp.concatenate([attn_c, four_c, hy_c], axis=-1) @ w_out[i]
            ctx = ctx + cga1 * rmsnorm(mix_c, g_post_mix[i])

        fx = rmsnorm(x, g_pre_ffn[i]) * (1.0 + sc2) + sh2
        yx = conv_ffn(fx, w_up[i], w_ffn_conv[i], b_ffn_conv[i], w_down[i])
        x = x + ga2 * rmsnorm(yx, g_post_ffn[i])
        if not last:
            fc = rmsnorm(ctx, g_pre_ffn[i]) * (1.0 + csc2) + csh2
            yc = conv_ffn(fc, w_up[i], w_ffn_conv[i], b_ffn_conv[i], w_down[i])
            ctx = ctx + cga2 * rmsnorm(yc, g_post_ffn[i])

    return x
```

```python
import math
import contextlib
import numpy as np
import ml_dtypes
import concourse.bass as bass
import concourse.mybir as mybir
from concourse.bass_utils import run_bass_kernel_spmd

F32 = mybir.dt.float32
BF16 = mybir.dt.bfloat16
AF = mybir.ActivationFunctionType
ALU = mybir.AluOpType
AX = mybir.AxisListType

D = 1024
L = 4096
C = 256
T = L + C
NT = T // 128
DFF = 2816
INW = 1792
EPS = 1e-6
PL = 270
NPP = 16 + 2 * PL
TWO_PI = 2.0 * math.pi
ATT_DUP = 1


class _Op:
    __slots__ = ("eng", "fn", "deps", "idx", "dma", "marked", "mark_no", "sem", "val", "qn")


class Sched:
    ENGS = ["pe", "act", "dve", "pool", "sp"]
    KDMA = 8

    def __init__(self):
        self.ops = []
        self.last_w = {}
        self.readers = {}
        self.bar = None
        self.last_eng = {}
        self.dmas_since = []

    def add(self, eng, fn, reads=(), writes=(), dma=False):
        op = _Op()
        op.eng, op.fn, op.dma = eng, fn, dma
        op.idx = len(self.ops)
        op.marked = False
        deps = {}
        for k in reads:
            w = self.last_w.get(k)
            if w is not None:
                deps[w.idx] = w
        for k in writes:
            w = self.last_w.get(k)
            if w is not None:
                deps[w.idx] = w
            for r in self.readers.get(k, ()):
                deps[r.idx] = r
        if eng == "pe" and not dma:
            deps = {i: d for i, d in deps.items() if not (d.eng == "pe" and not d.dma)}
        if self.bar is not None:
            deps[self.bar.idx] = self.bar
        op.deps = list(deps.values())
        for k in reads:
            self.readers.setdefault(k, []).append(op)
        for k in writes:
            self.last_w[k] = op
            self.readers[k] = []
        self.ops.append(op)
        if dma:
            self.dmas_since.append(op)
        else:
            self.last_eng[eng] = op
        return op

    def barrier(self, fn):
        op = _Op()
        op.eng, op.fn, op.dma = "pool", fn, False
        op.idx = len(self.ops)
        op.marked = False
        deps = list(self.last_eng.values()) + list(self.dmas_since)
        if self.bar is not None:
            deps.append(self.bar)
        op.deps = deps
        self.ops.append(op)
        self.bar = op
        self.last_eng = {"pool": op}
        self.dmas_since = []
        self.last_w = {}
        self.readers = {}
        return op

    def emit(self, nc, final_deps):
        ops = self.ops
        for op in ops:
            for d in op.deps:
                d.marked = True
        for d in final_deps:
            d.marked = True
        cnt = {e: 0 for e in self.ENGS}
        qcnt = {e: 0 for e in self.ENGS}
        for op in ops:
            if op.dma:
                op.qn = qcnt[op.eng]
                qcnt[op.eng] += 1
            elif op.marked:
                cnt[op.eng] += 1
                op.mark_no = cnt[op.eng]
        with contextlib.ExitStack() as st:
            csem = {e: st.enter_context(nc.semaphore("c_" + e)) for e in ["pe", "act", "dve", "pool"]}
            dsem = {}
            for e in self.ENGS:
                if qcnt[e] > 0:
                    dsem[e] = [st.enter_context(nc.semaphore("d_%s_%d" % (e, i))) for i in range(self.KDMA)]
            for op in ops:
                if op.dma:
                    op.sem = dsem[op.eng][op.qn % self.KDMA]
                    op.val = 16 * (op.qn // self.KDMA + 1)
                elif op.marked:
                    op.sem = csem[op.eng]
                    op.val = op.mark_no
            block = st.enter_context(nc.Block())
            engobj = {"pe": nc.tensor, "act": nc.scalar, "dve": nc.vector, "pool": nc.gpsimd, "sp": nc.sync}
            KD = self.KDMA

            def make(e):
                def body(_h):
                    seen = {}
                    eo = engobj[e]

                    def wait(sem, val):
                        key = id(sem)
                        if seen.get(key, 0) >= val:
                            return
                        seen[key] = val
                        eo.wait_ge(sem, val)

                    for op in ops:
                        if op.eng != e:
                            continue
                        need = {}
                        for d in op.deps:
                            k = id(d.sem)
                            if k not in need or need[k][1] < d.val:
                                need[k] = (d.sem, d.val)
                        if op.dma and op.qn >= KD:
                            k = id(op.sem)
                            v = op.val - 16
                            if k not in need or need[k][1] < v:
                                need[k] = (op.sem, v)
                        for sem, val in need.values():
                            wait(sem, val)
                        if op.fn is None:
                            continue
                        ins = op.fn(eo)
                        if op.dma:
                            ins.then_inc(op.sem, 16)
                        elif op.marked:
                            ins.then_inc(op.sem, 1)
                    if e == "sp":
                        for d in final_deps:
                            wait(d.sem, d.val)
                return body

            block.tensor(make("pe"))
            block.scalar(make("act"))
            block.vector(make("dve"))
            block.gpsimd(make("pool"))
            block.sync(make("sp"))


def _fft_tables(kind, n_in):
    HI = n_in // 64
    lo = np.arange(64, dtype=np.float64)[:, None]
    hi = np.arange(HI, dtype=np.float64)[:, None]
    out = {}
    if kind == "four":
        NR = HI
        r = np.arange(NR, dtype=np.float64)[None, :]
        al = TWO_PI * r * hi / HI
        out["D0"] = np.concatenate([np.cos(al), -np.sin(al)], axis=1)
        k = np.arange(n_in, dtype=np.float64)[None, :]
        ph = TWO_PI * k * lo / n_in
        out["EA"] = np.stack([np.cos(ph), np.sin(ph)], axis=1)
        out["EB"] = np.stack([np.sin(ph), -np.cos(ph)], axis=1)
    elif kind == "hfwd":
        NR = 2 * HI
        r = np.arange(NR, dtype=np.float64)[None, :]
        al = TWO_PI * (2 * r + 1) * hi * 64 / (4 * n_in)
        out["D0"] = np.concatenate([np.cos(al), -np.sin(al)], axis=1)
        f = np.arange(n_in, dtype=np.float64)[None, :]
        ph = TWO_PI * (2 * f + 1) * lo / (4 * n_in)
        out["EA"] = np.stack([np.cos(ph), np.sin(ph)], axis=1)
        out["EB"] = np.stack([np.sin(ph), -np.cos(ph)], axis=1)
    elif kind == "hinv":
        NR = 2 * HI
        r = np.arange(NR, dtype=np.float64)[None, :]
        be = TWO_PI * hi * r / NR
        out["D0"] = np.concatenate([np.cos(be), np.sin(be)], axis=1)
        out["D1"] = np.concatenate([np.sin(be), -np.cos(be)], axis=1)
        t = np.arange(n_in, dtype=np.float64)[None, :]
        ps = TWO_PI * (2 * lo + 1) * t / (4 * n_in)
        sc = 2.0 / (2 * n_in)
        out["EA"] = (sc * np.cos(ps))[:, None, :]
        out["EB"] = (-sc * np.sin(ps))[:, None, :]
    return {k: np.ascontiguousarray(v, dtype=np.float32) for k, v in out.items()}, NR


_CONST = None


def host_consts():
    global _CONST
    if _CONST is not None:
        return _CONST
    cs = {}
    f32 = np.float32
    inv = (np.float32(10000.0) ** (-(np.arange(0, 32, 2, dtype=f32)) / f32(32))).astype(f32)
    row = np.repeat(np.arange(64, dtype=f32), 64)
    col = np.tile(np.arange(64, dtype=f32), 64)
    ar = (row[:, None] * inv[None, :]).astype(f32).astype(np.float64)
    ac = (col[:, None] * inv[None, :]).astype(f32).astype(np.float64)
    cos64 = np.concatenate([np.cos(ar), np.cos(ar), np.cos(ac), np.cos(ac)], axis=1)
    sin64 = np.concatenate([-np.sin(ar), np.sin(ar), -np.sin(ac), np.sin(ac)], axis=1)
    tab = np.zeros((T, 128), f32)
    tab[:C, :64] = 1.0
    tab[C:, :64] = cos64
    tab[C:, 64:] = sin64
    cs["ropetab"] = tab
    for nm, kind, n in [("fl", "four", L), ("fc", "four", C), ("hfl", "hfwd", L), ("hfc", "hfwd", C),
                        ("hil", "hinv", L), ("hic", "hinv", C)]:
        tb, NR = _fft_tables(kind, n)
        for k, v in tb.items():
            cs[nm + "_" + k] = v.astype(ml_dtypes.bfloat16)
    for nm, n in [("L", L), ("C", C)]:
        t = np.linspace(0.0, 1.0, n, dtype=f32)[:, None]
        bands = np.linspace(1e-4, 15, 16, dtype=f32)
        ang = ((f32(TWO_PI) * np.arange(n, dtype=f32)[:, None]) / f32(n) * bands[None, :]).astype(f32)
        z = np.concatenate([t, np.cos(ang.astype(np.float64)), -np.sin(ang.astype(np.float64))], axis=-1)
        cs["zT_" + nm] = np.ascontiguousarray(z.T, dtype=f32)
        deltas = np.abs(np.linspace(math.log(1e-2) / 1.5, math.log(1e-2) / 0.3, 256, dtype=f32))
        dec = np.exp(-(t * deltas[None, :]).astype(f32).astype(np.float64))
        cs["decT_" + nm] = np.ascontiguousarray(dec.T, dtype=f32)
    j = np.arange(64, dtype=np.float64)
    b = TWO_PI * np.outer(j, j) / 64.0
    blk = np.zeros((128, 2, 128), np.float64)
    for g in range(2):
        blk[g * 64:(g + 1) * 64, 0, g * 64:(g + 1) * 64] = np.cos(b) / 512.0
        blk[g * 64:(g + 1) * 64, 1, g * 64:(g + 1) * 64] = -np.sin(b) / 512.0
    cs["dcds"] = blk.astype(f32)
    _CONST = cs
    return cs


def pack_pp(inp, b):
    pp = np.zeros((128, NPP), np.float32)
    pp[:, 0:16:2] = inp["c"][b].reshape(8, 128).T
    pp[:, 1:16:2] = inp["c_ctx"].reshape(8, 128).T
    for l in range(2):
        o = 16 + l * PL
        pp[:, o:o + 48] = inp["b_mod"][l].reshape(48, 128).T; o += 48
        pp[:, o:o + 8] = inp["g_pre_mix"][l].reshape(8, 128).T; o += 8
        pp[:, o:o + 8] = inp["g_pre_ffn"][l].reshape(8, 128).T; o += 8
        pp[:, o:o + 18] = inp["w_hy_conv"][l].reshape(3, 6, 128).transpose(2, 1, 0).reshape(128, 18); o += 18
        pp[:, o:o + 6] = inp["b_hy_conv"][l].reshape(6, 128).T; o += 6
        pp[:, o:o + 2] = inp["hy_bias"][l].reshape(2, 128).T; o += 2
        pp[:, o:o + 132] = inp["w_ffn_conv"][l].reshape(3, 44, 128).transpose(2, 1, 0).reshape(128, 132); o += 132
        pp[:, o:o + 44] = inp["b_ffn_conv"][l].reshape(44, 128).T; o += 44
        pp[0:64, o] = inp["hy_b1"][l]; pp[0:64, o + 1] = inp["hy_fr1"][l]
        pp[0:64, o + 2] = inp["hy_b2"][l]; pp[0:64, o + 3] = inp["hy_fr2"][l]
    return pp


def pack_rows(inp):
    rows = np.zeros((2, 6, 1024), np.float32)
    for l in range(2):
        rows[l, 0] = inp["b_mod"][l][2048:3072]
        rows[l, 1] = inp["b_mod"][l][5120:6144]
        rows[l, 2] = inp["g_post_mix"][l]
        rows[l, 3] = inp["g_post_ffn"][l]
        rows[l, 4, 0:512] = np.tile(inp["g_q"][l], 8)
        rows[l, 4, 512:640] = np.tile(inp["g_k"][l], 2)
    return rows


class KB:
    AW = 52900

    def __init__(self, dbg=()):
        self.nc = bass.Bass("TRN2", target_bir_lowering=False)
        nc = self.nc
        self.S = Sched()
        self.arena = nc.alloc_sbuf_tensor("arena", [128, self.AW], F32)
        self.top = 0
        self.psall = nc.alloc_psum_tensor("psall", [128, 4096], F32)[:]
        self.ps = [self.psall[:, i * 512:(i + 1) * 512] for i in range(8)]
        self.psi = 0
        self.dbg = set(dbg)
        self.drams = {}
        self.nbar = 0

    def alloc(self, shape, dtype=F32, parts=128):
        n = 1
        for s in shape:
            n *= s
        esz = 4 if dtype == F32 else 2
        words = (n * esz + 3) // 4
        words = (words + 7) // 8 * 8
        w0 = self.top
        self.top += words
        assert self.top <= self.AW, "SBUF arena overflow %d" % self.top
        ap = self.arena[0:parts, w0:w0 + words]
        if dtype != F32:
            ap = ap.bitcast(dtype)
        ap = ap[:, 0:n]
        if len(shape) == 2:
            ap = ap.rearrange("p (a b) -> p a b", a=shape[0])
        elif len(shape) == 3:
            ap = ap.rearrange("p (a b c) -> p a b c", a=shape[0], b=shape[1])
        elif len(shape) == 4:
            ap = ap.rearrange("p (a b c d) -> p a b c d", a=shape[0], b=shape[1], c=shape[2])
        return ap

    def dram_in(self, name, shape, dtype=F32):
        t = self.nc.dram_tensor(name, list(shape), dtype, kind="ExternalInput").ap()
        self.drams[name] = t
        return t

    def dram(self, name, shape, dtype=F32, out=False):
        kind = "ExternalOutput" if (out or name in self.dbg) else "Internal"
        t = self.nc.dram_tensor(name, list(shape), dtype, kind=kind).ap()
        self.drams[name] = t
        return t

    def psn(self):
        i = self.psi
        self.psi = (self.psi + 1) % 8
        return i

    def op(self, eng, fn, r=(), w=()):
        return self.S.add(eng, fn, reads=r, writes=w)

    def dma(self, eng, out, in_, r=(), w=()):
        return self.S.add(eng, lambda e: e.dma_start(out=out, in_=in_), reads=r, writes=w, dma=True)

    def mm(self, out, lhsT, rhs, start, stop, r=(), w=()):
        return self.S.add("pe", lambda e: e.matmul(out, lhsT=lhsT, rhs=rhs, start=start, stop=stop), reads=r, writes=w)

    def barrier(self, reset_to=None):
        d = self.bdummy
        self.S.barrier(lambda e: e.memset(d, 0.0))
        if reset_to is not None:
            self.top = reset_to


def _act(out, in_, func, bias=None, scale=None, accum=None):
    kw = {}
    if bias is not None:
        kw["bias"] = bias
    if scale is not None:
        kw["scale"] = scale
    if accum is not None:
        kw["accum_out"] = accum
    return lambda e: e.activation(out=out, in_=in_, func=func, **kw)


def build(nlayers=2, dbg=(), stop_after=None):
    kb = KB(dbg)
    nc, S = kb.nc, kb.S
    op, dma, mm, alloc = kb.op, kb.dma, kb.mm, kb.alloc
    PS = kb.ps
    cs = host_consts()

    xb = kb.dram_in("xb", [L, D])
    ctxb = kb.dram_in("ctxb", [C, D])
    ppd = kb.dram_in("pp", [128, NPP])
    rowsd = kb.dram_in("rows", [2, 6, 1024])
    w_mod = kb.dram_in("w_mod", [2, D, 6 * D])
    w_in = kb.dram_in("w_in", [2, D, INW])
    w_out = kb.dram_in("w_out", [2, D, D])
    w_up = kb.dram_in("w_up", [2, D, 2 * DFF])
    w_down = kb.dram_in("w_down", [2, DFF, D])
    w_four = kb.dram_in("w_fourier", [2, 256, 256])
    hy_w1 = kb.dram_in("hy_w1", [2, 33, 64])
    hy_w2 = kb.dram_in("hy_w2", [2, 64, 64])
    hy_w3 = kb.dram_in("hy_w3", [2, 64, 512])
    cd = {k: kb.dram_in(k, v.shape, BF16 if v.dtype == ml_dtypes.bfloat16 else F32) for k, v in cs.items()}
    yout = kb.dram("y", [L, D], out=True)
    res = [kb.dram("res%d" % i, [T, D]) for i in range(3)]
    mixT = kb.dram("mixT", [D, T], BF16)
    FT = kb.dram("FT", [256, T])
    uT = kb.dram("uT", [256, T])
    x0T = kb.dram("x0T", [256, T])
    hpm = {"L": kb.dram("hpmL", [2, 256, L]), "C": kb.dram("hpmC", [2, 256, C])}
    Gd = {"L": kb.dram("GL", [2, 256, L]), "C": kb.dram("GC", [2, 256, C])}
    Yd = {"L": kb.dram("YL", [2, 256, L]), "C": kb.dram("YC", [2, 256, C])}
    gT = kb.dram("gT", [DFF, T], BF16)

    ident = alloc([128], F32)
    kb.bdummy = alloc([8], F32)
    pp = alloc([NPP], F32)
    epsc = alloc([4], F32)
    base_top = kb.top

    op("pool", lambda e: e.memset(ident, 0.0), w=["ident"])
    op("pool", lambda e: e.affine_select(out=ident, in_=ident, pattern=[[-1, 128]], compare_op=ALU.not_equal,
                                         fill=1.0, base=0, channel_multiplier=1), r=["ident"], w=["ident"])
    op("pool", lambda e: e.memset(epsc[:, 0:1], EPS), w=["epsc"])
    op("pool", lambda e: e.memset(epsc[:, 1:2], -math.pi), w=["epsc"])
    dma("sp", pp, ppd, w=["pp"])
    kb.barrier()

    def tile_src(srcs, t):
        if srcs == "in":
            return ctxb[t * 128:(t + 1) * 128, :] if t < 2 else xb[(t - 2) * 128:(t - 1) * 128, :]
        return srcs[t * 128:(t + 1) * 128, :]

    groups = [(0, 256)] + [(256 + 512 * i, 512) for i in range(8)]

    def padcol(tok):
        return 1 + tok if tok < 256 else 3 + tok

    cur_res = "in"
    def do_layer(l, cur_res):
        last = (l == 1)
        po = 16 + l * PL
        P_bmod, P_gpm, P_gpf = po, po + 48, po + 56
        P_hcw, P_hcb, P_hb, P_fcw, P_fcb, P_hy = po + 64, po + 82, po + 88, po + 90, po + 222, po + 266
        kb.top = base_top
        modc = alloc([48, 2], F32)
        AB = alloc([2, 2, 2, 8], F32)
        GA = alloc([2, 2, 1024], F32)
        layer_top = kb.top

        sc = alloc([8, 2], BF16)
        rep = alloc([2, 8, 128], BF16)
        ones = alloc([128], F32)
        rowsb = alloc([4, 1024], F32)
        wm = [alloc([8, 1024], BF16), alloc([8, 1024], BF16), alloc([8, 1024], BF16)]
        scf = alloc([8, 2], F32)
        op("act", _act(scf, pp[:, 0:16].rearrange("p (j w) -> p j w", w=2), AF.Silu), r=["pp"], w=["scf"])
        op("dve", lambda e: e.tensor_copy(out=sc, in_=scf), r=["scf"], w=["sc"])
        op("pool", lambda e: e.memset(ones, 1.0), w=["ones"])
        for wh in range(2):
            for j in range(8):
                op("act", _act(rep[:, wh, j, :], ones, AF.Identity, scale=scf[:, j, wh:wh + 1]), r=["scf", "ones"], w=["rep"])
        dma("sp", rowsb.rearrange("p a b -> p (a b)"), rowsd[l, 0:4, :].rearrange("r f -> (r f)").partition_broadcast(128), w=["rowsb"])
        wmv = w_mod[l].rearrange("(k p) f -> p k f", p=128)
        pm = 0
        gab = [1]
        for pc in range(6):
            wb = wm[pc % 3]
            dma("pool", wb, wmv[:, :, pc * 1024:(pc + 1) * 1024], w=[("wm", pc % 3)])
            for f in range(8):
                fc = pc * 8 + f
                for k in range(8):
                    mm(PS[pm][:, 2 * fc:2 * fc + 2], wb[:, k, f * 128:(f + 1) * 128], sc[:, k, :], k == 0, k == 7,
                       r=[("wm", pc % 3), "sc"], w=[("ps", pm)])
            if pc in (2, 5):
                sub = 0 if pc == 2 else 1
                for wh in range(2):
                    for hf in range(2):
                        pb = gab[0]
                        gab[0] = gab[0] % 7 + 1
                        for k in range(8):
                            mm(PS[pb], rep[:, wh, k, :], wb[:, k, hf * 512:(hf + 1) * 512], k == 0, k == 7,
                               r=[("wm", pc % 3), "rep"], w=[("ps", pb)])
                        gsl = GA[:, sub, wh, hf * 512:(hf + 1) * 512]
                        op("dve", lambda e, gsl=gsl, pb=pb, sub=sub, hf=hf: e.tensor_tensor(
                            out=gsl, in0=PS[pb], in1=rowsb[:, sub, hf * 512:(hf + 1) * 512], op=ALU.add),
                           r=[("ps", pb), "rowsb"], w=[("GA", sub, wh, hf)])
                        op("dve", lambda e, gsl=gsl, sub=sub, hf=hf: e.tensor_tensor(
                            out=gsl, in0=gsl, in1=rowsb[:, 2 + sub, hf * 512:(hf + 1) * 512], op=ALU.mult),
                           r=["rowsb", ("GA", sub, wh, hf)], w=[("GA", sub, wh, hf)])
        op("dve", lambda e: e.tensor_tensor(out=modc, in0=PS[pm][:, 0:96].rearrange("p (f w) -> p f w", w=2),
                                            in1=pp[:, P_bmod:P_bmod + 48].unsqueeze(2).to_broadcast([128, 48, 2]),
                                            op=ALU.add), r=[("ps", pm), "pp"], w=["modc"])
        for sub in range(2):
            gcol = P_gpm if sub == 0 else P_gpf
            shc, scc = (0, 8) if sub == 0 else (24, 32)
            for wh in range(2):
                a_ap = AB[:, sub, wh, 0, :]
                b_ap = AB[:, sub, wh, 1, :]
                op("dve", lambda e, a_ap=a_ap, scc=scc, wh=wh: e.tensor_scalar(
                    out=a_ap, in0=modc[:, scc:scc + 8, wh], scalar1=1.0, scalar2=None, op0=ALU.add),
                   r=["modc"], w=[("AB", sub, wh, 0)])
                op("dve", lambda e, a_ap=a_ap, gcol=gcol: e.tensor_tensor(out=a_ap, in0=a_ap, in1=pp[:, gcol:gcol + 8], op=ALU.mult),
                   r=[("AB", sub, wh, 0), "pp"], w=[("AB", sub, wh, 0)])
                op("dve", lambda e, b_ap=b_ap, shc=shc, wh=wh: e.tensor_copy(out=b_ap, in_=modc[:, shc:shc + 8, wh]),
                   r=["modc"], w=[("AB", sub, wh, 1)])
        kb.barrier(reset_to=layer_top)
        if stop_after == ("mod", l):
            return None

        def norm_phase(src, sub, hT, tiles):
            xt = [alloc([1024], F32), alloc([1024], F32), alloc([1024], F32)]
            xs = [alloc([1024], F32), alloc([1024], F32), alloc([1024], F32)]
            junk = alloc([1024], F32)
            st = alloc([34, 4], F32)

            def stage_a(i, t):
                b = i % 3
                dma("sp", xt[b], tile_src(src, t), w=[("xt", b)])
                op("pool", lambda e, t=t: e.memset(st[:, t, 0:1], 0.0), w=[("st", t)])
                op("act", _act(junk, xt[b], AF.Square, accum=st[:, t, 0:1]), r=[("xt", b), ("st", t)], w=["junk", ("st", t)])
                op("act", _act(st[:, t, 1:2], st[:, t, 0:1], AF.Sqrt, bias=epsc[:, 0:1], scale=1.0 / D),
                   r=[("st", t)], w=[("st1", t)])
                op("dve", lambda e, t=t: e.reciprocal(out=st[:, t, 2:3], in_=st[:, t, 1:2]), r=[("st1", t)], w=[("st2", t)])
                op("dve", lambda e, t=t, b=b: e.tensor_scalar(out=xs[b], in0=xt[b], scalar1=st[:, t, 2:3], scalar2=None,
                                                              op0=ALU.mult), r=[("xt", b), ("st2", t)], w=[("xs", b)])

            def stage_b(i, t):
                wh = 1 if t < 2 else 0
                b = i % 3
                for hb in range(2):
                    pb = kb.psn()
                    for jj in range(4):
                        j = hb * 4 + jj
                        op("pe", lambda e, pb=pb, jj=jj, j=j, b=b: e.transpose(PS[pb][:, jj * 128:(jj + 1) * 128],
                                                                               xs[b][:, j * 128:(j + 1) * 128], ident),
                           r=[("xs", b), "ident"], w=[("ps", pb)])
                    for jj in range(4):
                        j = hb * 4 + jj
                        if hb == 0:
                            op("act", _act(hT[:, j, t * 128:(t + 1) * 128], PS[pb][:, jj * 128:(jj + 1) * 128], AF.Identity,
                                           bias=AB[:, sub, wh, 1, j:j + 1], scale=AB[:, sub, wh, 0, j:j + 1]),
                               r=[("ps", pb)], w=[("hT", t, j)])
                        else:
                            op("dve", lambda e, pb=pb, jj=jj, j=j, t=t, wh=wh: e.tensor_scalar(
                                out=hT[:, j, t * 128:(t + 1) * 128], in0=PS[pb][:, jj * 128:(jj + 1) * 128],
                                scalar1=AB[:, sub, wh, 0, j:j + 1], scalar2=AB[:, sub, wh, 1, j:j + 1], op0=ALU.mult, op1=ALU.add),
                               r=[("ps", pb)], w=[("hT", t, j)])

            stage_a(0, tiles[0])
            if len(tiles) > 1:
                stage_a(1, tiles[1])
            for i, t in enumerate(tiles):
                if i + 2 < len(tiles):
                    stage_a(i + 2, tiles[i + 2])
                stage_b(i, t)

        def epilogue_steps(pbs, t, sub, src, dst_ap, bufs):
            xt2, tmp, st2, junk2 = bufs
            wh = 1 if t < 2 else 0
            b = t % 2
            steps = []
            steps.append(lambda: dma("sp", xt2[b], tile_src(src, t), w=[("xt2", b)]))
            steps.append(lambda: op("dve", lambda e: e.memset(st2[:, b, 0:2], 0.0), w=[("st2a", b)]))
            for hb in range(2):
                steps.append(lambda hb=hb: op("act", _act(junk2[b], PS[pbs[hb]], AF.Square, accum=st2[:, b, hb:hb + 1]),
                                              r=[("ps", pbs[hb]), ("st2a", b)], w=[("junk2", b), ("st2a", b)]))
            steps.append(lambda: op("dve", lambda e: e.tensor_tensor(out=st2[:, b, 2:3], in0=st2[:, b, 0:1], in1=st2[:, b, 1:2], op=ALU.add),
                                    r=[("st2a", b)], w=[("st2b", b)]))
            steps.append(lambda: op("act", _act(st2[:, b, 3:4], st2[:, b, 2:3], AF.Sqrt, bias=epsc[:, 0:1], scale=1.0 / D),
                                    r=[("st2b", b)], w=[("st2c", b)]))
            steps.append(lambda: op("dve", lambda e: e.reciprocal(out=st2[:, b, 4:5], in_=st2[:, b, 3:4]), r=[("st2c", b)], w=[("st2d", b)]))
            for hb in range(2):
                steps.append(lambda hb=hb: op("dve", lambda e: e.scalar_tensor_tensor(
                    out=tmp[b][:, hb * 512:(hb + 1) * 512], in0=PS[pbs[hb]], scalar=st2[:, b, 4:5],
                    in1=GA[:, sub, wh, hb * 512:(hb + 1) * 512], op0=ALU.mult, op1=ALU.mult),
                    r=[("ps", pbs[hb]), ("st2d", b)], w=[("tmp", b)]))
            steps.append(lambda: op("dve", lambda e: e.tensor_tensor(out=tmp[b], in0=tmp[b], in1=xt2[b], op=ALU.add),
                                    r=[("tmp", b), ("xt2", b)], w=[("tmp", b)]))
            steps.append(lambda: dma("sp", dst_ap, tmp[b], r=[("tmp", b)], w=[("res", t)]))
            return steps

        def run_interleaved(step_lists):
            n = max(len(sl) for sl in step_lists)
            for k in range(n):
                for sl in step_lists:
                    if k < len(sl):
                        sl[k]()

        def diag_build(dg, col0, j):
            for k in range(3):
                op("dve", lambda e, k=k: e.tensor_scalar(out=dg[:, k, :], in0=ident, scalar1=pp[:, col0 + k:col0 + k + 1],
                                                         scalar2=None, op0=ALU.mult), r=["ident", "pp"], w=[("dg", j)])

        def fft_plan_merged(nm, n_in, NO, out_dt, nsrc):
            HI = n_in // 64
            NR = cd[nm + "_D0"].shape[1] // 2
            NF1 = 2 * NR
            n = n_in // NR
            assert HI == 64
            KS = HI * nsrc
            Dts = alloc([NF1], BF16)
            for i in range(nsrc):
                dma("sp", Dts[i * HI:(i + 1) * HI], cd[nm + "_D%d" % i], w=[("Dt", i)])
            E2 = alloc([NO, n_in], BF16)
            dma("sp", E2[0:64], cd[nm + "_EA"], w=["E2a"])
            dma("act", E2[64:128], cd[nm + "_EB"], w=["E2b"])
            XX = [alloc([128, 64], BF16) for _ in range(2)]
            xcnt = [0]
            Z = alloc([NR, 128], BF16)
            O = alloc([NO, n_in], out_dt)
            cpb = min(128, 512 // NF1)
            rpb = min(NR, 512 // (NO * n))
            Zkeys = [("Z", c0, hh) for c0 in range(0, 128, cpb) for hh in range(2)]
            Okeys = [("O", r0) for r0 in range(0, NR, rpb)]
            cnt = [0]

            def run(srcs, consume, srckeys=(), nchunks=2):
                for cc in range(nchunks):
                    xb = xcnt[0] % 2
                    xcnt[0] += 1
                    X = XX[xb]
                    for i in range(nsrc):
                        for q4 in range(4):
                            dma("pool", X[i * HI:(i + 1) * HI, q4 * 32:(q4 + 1) * 32, :],
                                srcs[i][cc * 128 + q4 * 32:cc * 128 + (q4 + 1) * 32, :].rearrange("c (h q) -> h c q", q=64),
                                r=list(srckeys), w=[("X", xb, i)])
                    for c0 in range(0, 128, cpb):
                        pb = kb.psn()
                        for ch in range(c0, c0 + cpb):
                            mm(PS[pb][0:64, (ch - c0) * NF1:(ch - c0 + 1) * NF1], X[0:KS, ch, :], Dts[0:KS, :],
                               True, True, r=[("X", xb, i) for i in range(nsrc)] + [("Dt", i) for i in range(nsrc)], w=[("ps", pb)])
                        psv = PS[pb][0:64, 0:cpb * NF1].rearrange("p (c h f) -> p h f c", h=2, f=NR)
                        op("act", _act(Z[0:64, :, c0:c0 + cpb], psv[:, 0, :, :], AF.Identity), r=[("ps", pb)], w=[("Z", c0, 0)])
                        src_ap = psv[:, 1, :, :]
                        dst_ap = Z[64:128, :, c0:c0 + cpb]
                        op("dve", lambda e, src_ap=src_ap, dst_ap=dst_ap: e.tensor_copy(out=dst_ap, in_=src_ap),
                           r=[("ps", pb), ("Z", c0, 0)], w=[("Z", c0, 1)])
                    Ov = O.rearrange("p o (j r) -> p o j r", r=NR)
                    for r0 in range(0, NR, rpb):
                        pb = kb.psn()
                        for rr in range(rpb):
                            r_ = r0 + rr
                            first = (r0 == 0 and rr == 0)
                            lastm = (r0 + rpb >= NR and rr == rpb - 1)
                            oap = PS[pb][:, rr * NO * n:(rr + 1) * NO * n].rearrange("p (o j) -> p o j", o=NO)
                            mm(oap, Z[:, r_, :], E2[:, :, r_:n_in:NR], True, True,
                               r=(Zkeys if (first or lastm) else []) + ["E2a", "E2b"], w=[("ps", pb)])
                        src_ap = PS[pb][:, 0:rpb * NO * n].rearrange("p (r o j) -> p o j r", o=NO, j=n)
                        dst_ap = Ov[:, :, :, r0:r0 + rpb]
                        op("act", _act(dst_ap, src_ap, AF.Identity), r=[("ps", pb)], w=[("O", r0)])
                    consume(cc, O, Okeys)
            return run

        def fft_plan_split(nm, n_in, NO, out_dt, nsrc):
            HI = n_in // 64
            NR = cd[nm + "_D0"].shape[1] // 2
            NF1 = 2 * NR
            n = n_in // NR
            Dt = []
            for i in range(nsrc):
                dt_ = alloc([NF1], BF16)
                dma("sp", dt_[0:HI], cd[nm + "_D%d" % i], w=[("Dt", i)])
                Dt.append(dt_)
            EA = alloc([NO, n_in], BF16)
            EB = alloc([NO, n_in], BF16)
            dma("sp", EA[0:64], cd[nm + "_EA"], w=["EA"])
            dma("act", EB[0:64], cd[nm + "_EB"], w=["EB"])
            X = [alloc([128, 64], BF16) for _ in range(nsrc)]
            Z = alloc([NF1, 128], BF16)
            O = alloc([NO, n_in], out_dt)
            cpb = min(128, 512 // NF1)
            rpb = min(NR, 512 // (NO * n))
            Zkeys = [("Z", c0) for c0 in range(0, 128, cpb)]
            Okeys = [("O", r0) for r0 in range(0, NR, rpb)]
            cnt = [0]

            def run(srcs, consume, srckeys=(), nchunks=2):
                for cc in range(nchunks):
                    for i in range(nsrc):
                        for q4 in range(4):
                            dma("pool", X[i][0:HI, q4 * 32:(q4 + 1) * 32, :],
                                srcs[i][cc * 128 + q4 * 32:cc * 128 + (q4 + 1) * 32, :].rearrange("c (h q) -> h c q", q=64),
                                r=list(srckeys), w=[("X", i)])
                    for c0 in range(0, 128, cpb):
                        pb = kb.psn()
                        for ch in range(c0, c0 + cpb):
                            for i in range(nsrc):
                                mm(PS[pb][0:64, (ch - c0) * NF1:(ch - c0 + 1) * NF1], X[i][0:HI, ch, :], Dt[i][0:HI, :],
                                   i == 0, i == nsrc - 1, r=[("X", i), ("Dt", i)], w=[("ps", pb)])
                        src_ap = PS[pb][0:64, 0:cpb * NF1].rearrange("p (c f) -> p f c", f=NF1)
                        dst_ap = Z[0:64, :, c0:c0 + cpb]
                        if cnt[0] % 2 == 0:
                            op("act", _act(dst_ap, src_ap, AF.Identity), r=[("ps", pb)], w=[("Z", c0)])
                        else:
                            op("dve", lambda e, src_ap=src_ap, dst_ap=dst_ap: e.tensor_copy(out=dst_ap, in_=src_ap),
                               r=[("ps", pb)], w=[("Z", c0)])
                        cnt[0] += 1
                    Ov = O.rearrange("p o (j r) -> p o j r", r=NR)
                    for r0 in range(0, NR, rpb):
                        pb = kb.psn()
                        for rr in range(rpb):
                            r_ = r0 + rr
                            first = (r0 == 0 and rr == 0)
                            lastm = (r0 + rpb >= NR and rr == rpb - 1)
                            oap = PS[pb][:, rr * NO * n:(rr + 1) * NO * n].rearrange("p (o j) -> p o j", o=NO)
                            mm(oap, Z[0:64, r_, :], EA[0:64, :, r_:n_in:NR], True, False,
                               r=(Zkeys if first else []) + ["EA"], w=[("ps", pb)])
                            mm(oap, Z[0:64, NR + r_, :], EB[0:64, :, r_:n_in:NR], False, True,
                               r=(Zkeys if lastm else []) + ["EB"], w=[("ps", pb)])
                        src_ap = PS[pb][:, 0:rpb * NO * n].rearrange("p (r o j) -> p o j r", o=NO, j=n)
                        dst_ap = Ov[:, :, :, r0:r0 + rpb]
                        if cnt[0] % 2 == 0:
                            op("act", _act(dst_ap, src_ap, AF.Identity), r=[("ps", pb)], w=[("O", r0)])
                        else:
                            op("dve", lambda e, src_ap=src_ap, dst_ap=dst_ap: e.tensor_copy(out=dst_ap, in_=src_ap),
                               r=[("ps", pb)], w=[("O", r0)])
                        cnt[0] += 1
                    consume(cc, O, Okeys)
            return run

        def fft_plan(nm, n_in, NO, out_dt, nsrc):
            if cd[nm + "_D0"].shape[1] // 2 == 128:
                return fft_plan_merged(nm, n_in, NO, out_dt, nsrc)
            return fft_plan_split(nm, n_in, NO, out_dt, nsrc)

        hT = alloc([8, T], BF16)
        mark_hT = kb.top
        norm_phase(cur_res, 0, hT, list(range(NT)))
        kb.barrier(reset_to=mark_hT)
        if stop_after == ("norm", l):
            dbg_hT = kb.dram("dbg_hT", [128, 8 * T], BF16, out=True)
            dma("sp", dbg_hT, hT.rearrange("p a b -> p (a b)"), r=[])
            return None

        mark_wi = kb.top
        wiA = alloc([8, 1024], BF16)
        for c8_ in range(8):
            dma("pool", wiA[:, :, c8_ * 128:(c8_ + 1) * 128],
                w_in[l].rearrange("(k p) f -> p k f", p=128)[:, :, 768 + c8_ * 128:768 + (c8_ + 1) * 128], w=[("wiA", c8_)])
        stage = [alloc([T], F32), alloc([T], F32)]
        x1T = alloc([2, T], F32)
        Ub = alloc([T + 4], BF16)
        dgs = [alloc([3, 128], BF16), alloc([3, 128], BF16)]
        op("dve", lambda e: e.memset(Ub, 0.0), w=["Ub"])
        gl = groups[1:] if last else groups
        for c8 in range(8):
            stg = stage[c8 % 2]
            col = 768 + c8 * 128
            if c8 >= 2:
                diag_build(dgs[c8 % 2], P_hcw + (c8 - 2) * 3, c8 % 2)
            for gi, (t0, n) in enumerate(gl):
                pb = kb.psn()
                for k in range(8):
                    mm(PS[pb][:, 0:n], wiA[:, k, col - 768:col - 768 + 128], hT[:, k, t0:t0 + n], k == 0, k == 7,
                       r=[("wiA", c8)] + [("hT", t) for t in range(t0 // 128, (t0 + n) // 128)], w=[("ps", pb)])
                if c8 < 2:
                    op("act", _act(stg[:, t0:t0 + n], PS[pb][:, 0:n], AF.Identity), r=[("ps", pb)], w=[("stage", c8 % 2)])
                else:
                    pc0 = padcol(t0)
                    op("act", _act(Ub[:, pc0:pc0 + n], PS[pb][:, 0:n], AF.Identity), r=[("ps", pb)], w=["Ub"])
            if c8 < 2:
                dma("sp", FT[c8 * 128:(c8 + 1) * 128, :], stg, r=[("stage", c8 % 2)], w=[("src", "FT")])
                continue
            hc = c8 - 2
            for gi, (t0, n) in enumerate(gl):
                pb = kb.psn()
                pc0 = padcol(t0)
                for k in range(3):
                    mm(PS[pb][:, 0:n], dgs[c8 % 2][:, k, :], Ub[:, pc0 - 1 + k:pc0 - 1 + k + n], k == 0, k == 2,
                       r=[("dg", c8 % 2), "Ub"], w=[("ps", pb)])
                bcol = pp[:, P_hcb + hc:P_hcb + hc + 1]
                if hc < 2:
                    op("act", _act(stg[:, t0:t0 + n], PS[pb][:, 0:n], AF.Identity, bias=bcol, scale=1.0),
                       r=[("ps", pb)], w=[("stage", c8 % 2)])
                elif hc < 4:
                    op("act", _act(x1T[:, hc - 2, t0:t0 + n], PS[pb][:, 0:n], AF.Identity, bias=bcol, scale=1.0),
                       r=[("ps", pb)], w=[("x1T", hc - 2)])
                else:
                    op("dve", lambda e, stg=stg, pb=pb, t0=t0, n=n, bcol=bcol, hc=hc: e.scalar_tensor_tensor(
                        out=stg[:, t0:t0 + n], in0=PS[pb][:, 0:n], scalar=bcol, in1=x1T[:, hc - 4, t0:t0 + n],
                        op0=ALU.add, op1=ALU.mult), r=[("ps", pb), ("x1T", hc - 4)], w=[("stage", c8 % 2)])
            if hc < 2:
                dma("sp", x0T[hc * 128:(hc + 1) * 128, :], stg, r=[("stage", c8 % 2)], w=[("x0T", hc)])
            elif hc >= 4:
                dma("sp", uT[(hc - 4) * 128:(hc - 3) * 128, :], stg, r=[("stage", c8 % 2)], w=[("src", "uT")])
        kb.barrier(reset_to=mark_wi)
        if stop_after == ("wina", l):
            return None

        QT = alloc([2, 2, T], BF16)
        KT = alloc([2, T], BF16)
        Va = alloc([NT, 2, 128], BF16)
        mark_qkv = kb.top
        wi = alloc([8, 768], BF16)
        for kk_ in range(4):
            dma("pool", wi[:, 2 * kk_:2 * kk_ + 2, :], w_in[l].rearrange("(k p) f -> p k f", p=128)[:, 2 * kk_:2 * kk_ + 2, 0:768], w=[("wi", kk_)])
        gqk = alloc([640], F32)
        dma("sp", gqk, rowsd[l, 4, 0:640].partition_broadcast(128), w=["gqk"])
        op("dve", lambda e: e.memset(Va, 1.0), w=["Va1"])
        rt = [alloc([128], F32) for _ in range(4)]
        sq = [alloc([640], F32), alloc([640], F32)]
        ss = alloc([NT, 3, 10], F32)
        qn = [alloc([640], F32), alloc([640], F32)]
        t1 = [alloc([640], F32), alloc([640], F32)]
        t2 = [alloc([640], F32), alloc([640], F32)]
        qkr = [alloc([768], F32), alloc([768], F32)]
        def winb_steps(t):
            b = t % 2
            rtt = rt[t % 4]
            pq, pk = kb.psn(), kb.psn()
            st_ = []

            def s_mm():
                for k in range(8):
                    mm(PS[pq], hT[:, k, t * 128:(t + 1) * 128], wi[:, k, 0:512], k == 0, k == 7, r=[("hT", t), ("wi", k // 2)], w=[("ps", pq)])
                for k in range(8):
                    mm(PS[pk][:, 0:256], hT[:, k, t * 128:(t + 1) * 128], wi[:, k, 512:768], k == 0, k == 7,
                       r=[("hT", t), ("wi", k // 2)], w=[("ps", pk)])
                dma("act", rtt, cd["ropetab"][t * 128:(t + 1) * 128, :], w=[("rt", t % 4)])
            st_.append(s_mm)
            st_.append(lambda: op("act", _act(sq[b][:, 0:512], PS[pq], AF.Square), r=[("ps", pq)], w=[("sq", b)]))
            st_.append(lambda: op("act", _act(sq[b][:, 512:640], PS[pk][:, 0:128], AF.Square), r=[("ps", pk)], w=[("sq", b)]))
            st_.append(lambda: op("dve", lambda e: e.tensor_reduce(out=ss[:, t, 0, :], in_=sq[b].rearrange("p (h d) -> p h d", d=64),
                                                                   axis=AX.X, op=ALU.add), r=[("sq", b)], w=[("ss0", t)]))
            st_.append(lambda: op("act", _act(ss[:, t, 1, :], ss[:, t, 0, :], AF.Sqrt, bias=epsc[:, 0:1], scale=1.0 / 64),
                                  r=[("ss0", t)], w=[("ss1", t)]))
            st_.append(lambda: op("act", _act(Va[:, t, :, 0:64], PS[pk][:, 128:256].rearrange("p (h d) -> p h d", h=2), AF.Identity),
                                  r=[("ps", pk), "Va1"], w=[("Va", t)]))
            st_.append(lambda: op("dve", lambda e: e.reciprocal(out=ss[:, t, 2, :], in_=ss[:, t, 1, :]), r=[("ss1", t)], w=[("ss2", t)]))
            st_.append(lambda: op("dve", lambda e: e.tensor_tensor(
                out=qn[b][:, 0:512].rearrange("p (h d) -> p h d", d=64), in0=PS[pq].rearrange("p (h d) -> p h d", d=64),
                in1=ss[:, t, 2, 0:8].unsqueeze(2).to_broadcast([128, 8, 64]), op=ALU.mult),
                r=[("ps", pq), ("ss2", t)], w=[("qn", b)]))
            st_.append(lambda: op("dve", lambda e: e.tensor_tensor(
                out=qn[b][:, 512:640].rearrange("p (h d) -> p h d", d=64), in0=PS[pk][:, 0:128].rearrange("p (h d) -> p h d", d=64),
                in1=ss[:, t, 2, 8:10].unsqueeze(2).to_broadcast([128, 2, 64]), op=ALU.mult),
                r=[("ps", pk), ("ss2", t), ("Va", t)], w=[("qn", b)]))
            st_.append(lambda: op("dve", lambda e: e.tensor_tensor(out=qn[b], in0=qn[b], in1=gqk, op=ALU.mult),
                                  r=[("qn", b), "gqk"], w=[("qn", b)]))
            qv = qn[b].rearrange("p (h d) -> p h d", d=64)
            st_.append(lambda: op("dve", lambda e: e.tensor_tensor(
                out=t1[b].rearrange("p (h d) -> p h d", d=64), in0=qv,
                in1=rtt[:, 0:64].unsqueeze(1).to_broadcast([128, 10, 64]), op=ALU.mult),
                r=[("qn", b), ("rt", t % 4)], w=[("t1", b)]))
            q4 = qn[b].rearrange("p (h a d) -> p h a d", a=2, d=16)
            t24 = t2[b].rearrange("p (h a d) -> p h a d", a=2, d=16)
            s4 = rtt[:, 64:128].rearrange("p (c a d) -> p c a d", a=2, d=16)

            def s_rope():
                for a in range(2):
                    for rc in range(2):
                        op("dve", lambda e, a=a, rc=rc: e.tensor_tensor(
                            out=t24[:, rc:20:2, a, :], in0=q4[:, rc:20:2, 1 - a, :],
                            in1=s4[:, rc, a, :].unsqueeze(1).to_broadcast([128, 10, 16]), op=ALU.mult),
                           r=[("qn", b), ("rt", t % 4)], w=[("t2", b)])
            st_.append(s_rope)

            def s_add():
                for h in range(2):
                    op("dve", lambda e, h=h: e.tensor_tensor(
                        out=qkr[b][:, h * 256:(h + 1) * 256].rearrange("p (e a d) -> p a e d", e=2, a=2),
                        in0=t1[b][:, h * 256:(h + 1) * 256].rearrange("p (a e d) -> p a e d", a=2, e=2),
                        in1=t2[b][:, h * 256:(h + 1) * 256].rearrange("p (a e d) -> p a e d", a=2, e=2), op=ALU.add),
                       r=[("t1", b), ("t2", b)], w=[("qkr", b)])
                for dup in range(2):
                    op("dve", lambda e, dup=dup: e.tensor_tensor(
                        out=qkr[b][:, 512:768].rearrange("p (h u d) -> p h u d", u=2, d=64)[:, :, dup, :],
                        in0=t1[b][:, 512:640].rearrange("p (h d) -> p h d", d=64),
                        in1=t2[b][:, 512:640].rearrange("p (h d) -> p h d", d=64), op=ALU.add),
                       r=[("t1", b), ("t2", b)], w=[("qkr", b)])
            st_.append(s_add)

            def s_tr():
                p1, p2 = pq, pk
                for slot in range(4):
                    src = qkr[b][:, slot * 128:(slot + 1) * 128]
                    op("pe", lambda e, src=src, slot=slot: e.transpose(PS[p1][:, slot * 128:(slot + 1) * 128], src, ident),
                       r=[("qkr", b), "ident"], w=[("ps", p1)])
                for h in range(2):
                    src = qkr[b][:, 512 + h * 128:512 + (h + 1) * 128]
                    op("pe", lambda e, src=src, h=h: e.transpose(PS[p2][:, h * 128:(h + 1) * 128], src, ident),
                       r=[("qkr", b), "ident"], w=[("ps", p2)])
                op("act", _act(QT.rearrange("p h e t -> p (h e) t")[:, :, t * 128:(t + 1) * 128],
                               PS[p1].rearrange("p (s q) -> p s q", q=128), AF.Identity), r=[("ps", p1)], w=[("QT", t)])
                op("dve", lambda e: e.tensor_copy(out=KT[:, :, t * 128:(t + 1) * 128],
                                                  in_=PS[p2][:, 0:256].rearrange("p (s q) -> p s q", q=128)),
                   r=[("ps", p2)], w=[("KT", t)])
            st_.append(s_tr)
            return st_

        pairs = [[winb_steps(t), winb_steps(t + 1)] for t in range(0, NT, 2)]
        for sl in pairs[0]:
            sl[0]()
        for pi, pr in enumerate(pairs):
            if pi + 1 < len(pairs):
                for sl in pairs[pi + 1]:
                    sl[0]()
            run_interleaved([sl[1:] for sl in pr])
        kb.barrier(reset_to=mark_qkv)
        if stop_after == ("winb", l):
            d1 = kb.dram("dbg_QT", [128, 4 * T], BF16, out=True)
            d2 = kb.dram("dbg_KT", [128, 2 * T], BF16, out=True)
            dma("sp", d1, QT.rearrange("p h e t -> p (h e t)"))
            dma("sp", d2, KT.rearrange("p h t -> p (h t)"))
            return None

        PT = [alloc([1024], BF16) for _ in range(4)]
        rden = [alloc([512], F32) for _ in range(2)]
        accs = [alloc([512], F32) for _ in range(2)]
        ao = [alloc([2, 512], BF16) for _ in range(2)]
        Va2 = alloc([NT, 2, 128], BF16)
        op("act", _act(Va2[:, :, :, 0:64], Va[:, :, :, 64:128], AF.Identity), w=["Va2"])
        op("act", _act(Va2[:, :, :, 64:128], Va[:, :, :, 0:64], AF.Identity), w=["Va2"])
        pti = 0
        si = 0
        acci = 0
        PSA = kb.psall
        qgroups = ([] if last else [[0, 1]]) + [[2 + 4 * i + j for j in range(4)] for i in range(8)]
        for h in range(2):
            for gi, qg in enumerate(qgroups):
                aob = ao[gi % 2]
                for qi, qt in enumerate(qg):
                    ktiles = list(range(0, 2)) if qt < 2 else list(range(0, NT))
                    nsup = len(ktiles) // 2
                    pacc, pacc2 = (0, 1)
                    co = 0
                    acci += 1
                    spair = {}

                    def s_op(ki, qt=qt, h=h):
                        kt0 = ktiles[2 * ki]
                        nonlocal si
                        b0 = 2 + 2 * (si % 3)
                        si += 1
                        spair[ki] = b0
                        for rep_ in range(ATT_DUP):
                            for sub in range(2):
                                kt = kt0 + sub
                                for a in (range(2) if rep_ == 0 else range(ATT_DUPA)):
                                    mm(PS[b0 + a][:, sub * 256:(sub + 1) * 256].rearrange("p (e q) -> p e q", e=2),
                                       KT[a * 64:(a + 1) * 64, h, kt * 128:(kt + 1) * 128],
                                       QT[a * 64:(a + 1) * 64, h, :, qt * 128:(qt + 1) * 128], True, True,
                                       r=[("KT", kt), ("QT", qt)], w=[("ps", b0 + a)])

                    s_op(0)
                    if nsup > 1:
                        s_op(1)
                    for ki in range(nsup):
                        if ki + 2 < nsup:
                            s_op(ki + 2)
                        b0 = spair[ki]
                        pt = PT[pti % 4]
                        op("act", _act(pt.rearrange("p (a c) -> p a c", a=2),
                                       PSA[:, b0 * 512:(b0 + 2) * 512].rearrange("p (a c) -> p a c", a=2),
                                       AF.Exp, scale=0.125),
                           r=[("ps", b0), ("ps", b0 + 1)], w=[("PT", pti % 4)])
                        ptv = pt.rearrange("p (a s e q) -> p a s e q", a=2, s=2, e=2)
                        for sub in range(2):
                            kt = ktiles[2 * ki] + sub
                            fst = (ki == 0 and sub == 0)
                            lst = (ki == nsup - 1 and sub == 1)
                            mm(PS[pacc][:, 0:256].rearrange("p (s q) -> p s q", q=128), Va[:, kt, h, :], ptv[:, :, sub, 0, :],
                               fst, lst, r=[("Va", kt), ("PT", pti % 4)], w=[("acc", pacc, co)])
                            mm(PS[pacc2][:, 0:256].rearrange("p (s q) -> p s q", q=128), Va2[:, kt, h, :], ptv[:, :, sub, 1, :],
                               fst, lst, r=["Va2", ("PT", pti % 4)], w=[("acc", pacc2, co)])
                        pti += 1
                    rd = rden[qi % 2]
                    ac = accs[qi % 2]
                    op("dve", lambda e, ac=ac: e.tensor_copy(out=ac[0:64, 0:256], in_=PS[0][0:64, 0:256]),
                       r=[("acc", 0, 0)], w=[("accs", qi % 2)])
                    op("dve", lambda e, ac=ac: e.tensor_copy(out=ac[0:64, 256:512], in_=PS[0][64:128, 0:256]),
                       r=[("acc", 0, 0)], w=[("accs", qi % 2)])
                    op("dve", lambda e, ac=ac: e.tensor_copy(out=ac[64:128, 0:256], in_=PS[1][64:128, 0:256]),
                       r=[("acc", 1, 0)], w=[("accs", qi % 2)])
                    op("dve", lambda e, ac=ac: e.tensor_copy(out=ac[64:128, 256:512], in_=PS[1][0:64, 0:256]),
                       r=[("acc", 1, 0)], w=[("accs", qi % 2)])
                    op("dve", lambda e, rd=rd, ac=ac: e.reciprocal(out=rd[:, 0:256], in_=ac[:, 256:512]),
                       r=[("accs", qi % 2)], w=[("rden", qi % 2)])
                    op("dve", lambda e, rd=rd, ac=ac, aob=aob, qi=qi: e.tensor_tensor(
                        out=aob[:, :, qi * 128:(qi + 1) * 128],
                        in0=ac[:, 0:256].rearrange("p (s q) -> p s q", q=128),
                        in1=rd[:, 0:256].rearrange("p (s q) -> p s q", q=128), op=ALU.mult),
                       r=[("accs", qi % 2), ("rden", qi % 2)], w=[("ao", gi % 2)])
                ncol = 128 * len(qg)
                tok0 = qg[0] * 128
                dma("sp", mixT[h * 256:(h + 1) * 256, tok0:tok0 + ncol].rearrange("(c p) t -> p c t", p=128),
                    aob[:, :, 0:ncol], r=[("ao", gi % 2)], w=[("mixT", "attn", h, gi)])
        kb.barrier(reset_to=layer_top)
        if stop_after == ("attn", l):
            return None

        segs = [("L", L, C)] if last else [("L", L, C), ("C", C, 0)]
        def do_seg(sn, n_in, tok0):
            small = (sn == "C")

            def seg_barrier():
                if not small:
                    kb.barrier(reset_to=layer_top)
            kb.top = layer_top
            HI = n_in // 64
            zT = alloc([n_in], F32)
            w1 = alloc([64], F32)
            w2 = alloc([64], F32)
            w3 = alloc([512], F32)
            frb = alloc([4], F32)
            h1 = alloc([n_in], F32)
            h2 = alloc([n_in], F32)
            hfb = alloc([4, n_in], F32)
            dec = alloc([2, n_in], F32)
            nrm = alloc([2, 8], F32)
            arg = [alloc([512], F32), alloc([512], F32)]
            arg2 = [alloc([512], F32), alloc([512], F32)]
            dma("sp", zT[0:33], cd["zT_" + sn], w=["zT"])
            dma("sp", w1[0:33], hy_w1[l], w=["w1"])
            dma("sp", w2[0:64], hy_w2[l], w=["w2"])
            dma("sp", w3[0:64], hy_w3[l], w=["w3"])
            dma("act", dec, cd["decT_" + sn].rearrange("(c p) t -> p c t", p=128), w=["dec"])
            op("dve", lambda e: e.tensor_tensor(out=frb[0:64, 0:1], in0=pp[0:64, P_hy:P_hy + 1], in1=pp[0:64, P_hy + 1:P_hy + 2], op=ALU.mult),
               r=["pp"], w=["frb"])
            op("dve", lambda e: e.tensor_tensor(out=frb[0:64, 1:2], in0=pp[0:64, P_hy + 2:P_hy + 3], in1=pp[0:64, P_hy + 3:P_hy + 4], op=ALU.mult),
               r=["pp"], w=["frb"])
            ncg = max(1, n_in // 512)
            cw = min(512, n_in)
            for li, (wt, kdim, src_, dst_, frc) in enumerate([(w1, 33, zT, h1, P_hy + 1), (w2, 64, h1, h2, P_hy + 3)]):
                for g in range(ncg):
                    pb = kb.psn()
                    mm(PS[pb][0:64, 0:cw], wt[0:kdim, 0:64], src_[0:kdim, g * cw:(g + 1) * cw], True, True,
                       r=["w1", "w2", "zT", ("h", li, g)], w=[("ps", pb)])
                    ab = arg[g % 2]
                    op("dve", lambda e, ab=ab, pb=pb, frc=frc, li=li: e.tensor_scalar(
                        out=ab[0:64, 0:cw], in0=PS[pb][0:64, 0:cw], scalar1=pp[0:64, frc:frc + 1], scalar2=frb[0:64, li:li + 1],
                        op0=ALU.mult, op1=ALU.add), r=[("ps", pb), "frb", "pp"], w=[("arg", g % 2)])
                    a2 = arg2[g % 2]
                    MAGIC = 12582912.0
                    op("dve", lambda e, ab=ab, a2=a2: e.tensor_scalar(out=a2[0:64, 0:cw], in0=ab[0:64, 0:cw], scalar1=1.0 / TWO_PI,
                                                                      scalar2=MAGIC, op0=ALU.mult, op1=ALU.add),
                       r=[("arg", g % 2)], w=[("arg2", g % 2)])
                    op("dve", lambda e, a2=a2: e.tensor_scalar(out=a2[0:64, 0:cw], in0=a2[0:64, 0:cw], scalar1=-MAGIC,
                                                               scalar2=-TWO_PI, op0=ALU.add, op1=ALU.mult),
                       r=[("arg2", g % 2)], w=[("arg2", g % 2)])
                    op("dve", lambda e, ab=ab, a2=a2: e.tensor_tensor(out=ab[0:64, 0:cw], in0=ab[0:64, 0:cw], in1=a2[0:64, 0:cw], op=ALU.add),
                       r=[("arg", g % 2), ("arg2", g % 2)], w=[("arg", g % 2)])
                    op("dve", lambda e, ab=ab: e.tensor_scalar(out=ab[0:64, 0:cw], in0=ab[0:64, 0:cw], scalar1=-3.14159,
                                                               scalar2=3.14159, op0=ALU.max, op1=ALU.min),
                       r=[("arg", g % 2)], w=[("arg", g % 2)])
                    op("act", _act(dst_[0:64, g * cw:(g + 1) * cw], ab[0:64, 0:cw], AF.Sin),
                       r=[("arg", g % 2)], w=[("h", li + 1, g)])
            for c4 in range(4):
                for g in range(ncg):
                    pb = kb.psn()
                    mm(PS[pb][:, 0:cw], w3[0:64, c4 * 128:(c4 + 1) * 128], h2[0:64, g * cw:(g + 1) * cw], True, True,
                       r=["w3", ("h", 2, g)], w=[("ps", pb)])
                    op("dve", lambda e, c4=c4, g=g, pb=pb: e.tensor_tensor(
                        out=hfb[:, c4, g * cw:(g + 1) * cw], in0=PS[pb][:, 0:cw], in1=dec[:, c4 % 2, g * cw:(g + 1) * cw], op=ALU.mult),
                       r=[("ps", pb), "dec"], w=[("hfb", c4)])
            for c2 in range(2):
                op("dve", lambda e, c2=c2: e.memset(hfb[:, 2 + c2, 0:1], 0.0), r=[("hfb", 2 + c2)], w=[("hfb", 2 + c2)])
            op("dve", lambda e: e.memset(nrm, 0.0), w=[("nrm", 0), ("nrm", 1)])
            for c2 in range(2):
                for q, c4 in enumerate((c2, 2 + c2)):
                    op("act", _act(zT, hfb[:, c4, :], AF.Abs, accum=nrm[:, c2, q:q + 1]),
                       r=[("hfb", c4), ("nrm", c2)], w=["zT", ("nrm", c2)])
                op("dve", lambda e, c2=c2: e.tensor_tensor(out=nrm[:, c2, 2:3], in0=nrm[:, c2, 0:1], in1=nrm[:, c2, 1:2], op=ALU.add),
                   r=[("nrm", c2)], w=[("nrm", c2)])
                op("dve", lambda e, c2=c2: e.reciprocal(out=nrm[:, c2, 3:4], in_=nrm[:, c2, 2:3]), r=[("nrm", c2)], w=[("nrm", c2)])
                op("dve", lambda e, c2=c2: e.tensor_scalar(out=hfb[:, 2 + c2, :], in0=hfb[:, 2 + c2, :], scalar1=nrm[:, c2, 3:4], scalar2=None,
                                                           op0=ALU.mult), r=[("hfb", 2 + c2), ("nrm", c2)], w=[("hfb", 2 + c2)])
                op("dve", lambda e, c2=c2: e.scalar_tensor_tensor(out=h2, in0=hfb[:, c2, :], scalar=nrm[:, c2, 3:4], in1=hfb[:, 2 + c2, :],
                                                                  op0=ALU.mult, op1=ALU.add),
                   r=[("hfb", c2), ("hfb", 2 + c2), ("nrm", c2)], w=["hps"] + [("h", 2, g) for g in range(ncg)])
                dma("sp", hpm[sn][0, c2 * 128:(c2 + 1) * 128, :], h2, r=["hps"], w=[("src", "hp" + sn)])
                op("dve", lambda e, c2=c2: e.scalar_tensor_tensor(out=h1, in0=hfb[:, c2, :], scalar=nrm[:, c2, 3:4], in1=hfb[:, 2 + c2, :],
                                                                  op0=ALU.mult, op1=ALU.subtract),
                   r=[("hfb", c2), ("hfb", 2 + c2), ("nrm", c2)], w=["h1buf"] + [("h", 1, g) for g in range(ncg)])
                dma("act", hpm[sn][1, c2 * 128:(c2 + 1) * 128, :], h1, r=["h1buf"], w=[("src", "hm" + sn)])
            seg_barrier()
            hw = min(512, n_in)
            gbuf = alloc([2, hw], F32)
            ybuf = alloc([2, hw], F32)
            ytmp = alloc([hw], F32)
            run_hf = fft_plan("hf" + sn.lower(), n_in, 2, F32, 1)
            for which in range(2):
                def consume_g(cc, O, Okeys, which=which):
                    dma("sp", Gd[sn][which, cc * 128:(cc + 1) * 128, :], O[:, which, :], r=Okeys, w=[("G", which, cc)])
                run_hf([hpm[sn][which]], consume_g, srckeys=[("src", ("hp" if which == 0 else "hm") + sn)])

            gbufs = [gbuf, alloc([2, hw], F32)]
            ybufs = [ybuf, alloc([2, hw], F32)]
            nblk = n_in // hw
            gcnt = [0]

            def g_load(cc, hh):
                i = gcnt[0]
                gcnt[0] += 1
                fs = slice(hh * hw, (hh + 1) * hw)
                dma("sp", gbufs[i % 2], Gd[sn][:, cc * 128:(cc + 1) * 128, fs].rearrange("w c f -> c w f"),
                    r=[("G", 0, cc), ("G", 1, cc)], w=[("gbuf", i % 2)])
                return i % 2

            def consume_u(cc, O, Okeys):
                nxt = g_load(cc, 0)
                for hh in range(nblk):
                    fs = slice(hh * hw, (hh + 1) * hw)
                    gi_ = nxt
                    if hh + 1 < nblk:
                        nxt = g_load(cc, hh + 1)
                    gb_ = gbufs[gi_]
                    yb_ = ybufs[gi_]
                    gk = ("gbuf", gi_)
                    op("dve", lambda e, fs=fs, gb_=gb_, yb_=yb_: e.tensor_tensor(out=yb_[:, 0, :], in0=O[:, 0, fs], in1=gb_[:, 0, :], op=ALU.mult),
                       r=Okeys + [gk], w=[("yb0", gi_)])
                    op("dve", lambda e, fs=fs, gb_=gb_: e.tensor_tensor(out=ytmp, in0=O[:, 1, fs], in1=gb_[:, 1, :], op=ALU.mult),
                       r=Okeys + [gk], w=["ytmp"])
                    op("dve", lambda e, yb_=yb_: e.tensor_tensor(out=yb_[:, 0, :], in0=yb_[:, 0, :], in1=ytmp, op=ALU.subtract),
                       r=[("yb0", gi_), "ytmp"], w=[("yb0", gi_)])
                    op("dve", lambda e, fs=fs, gb_=gb_, yb_=yb_: e.tensor_tensor(out=yb_[:, 1, :], in0=O[:, 0, fs], in1=gb_[:, 1, :], op=ALU.mult),
                       r=Okeys + [gk], w=[("yb1", gi_)])
                    op("dve", lambda e, fs=fs, gb_=gb_: e.tensor_tensor(out=ytmp, in0=O[:, 1, fs], in1=gb_[:, 0, :], op=ALU.mult),
                       r=Okeys + [gk, ("yb0", gi_)], w=["ytmp"])
                    op("dve", lambda e, yb_=yb_: e.tensor_tensor(out=yb_[:, 1, :], in0=yb_[:, 1, :], in1=ytmp, op=ALU.add),
                       r=[("yb1", gi_), "ytmp"], w=[("yb1", gi_)])
                    dma("act", Yd[sn][:, cc * 128:(cc + 1) * 128, fs].rearrange("w c f -> c w f"), yb_,
                        r=[("yb0", gi_), ("yb1", gi_)], w=[("Ysrc", cc)])
            run_hf([uT[:, tok0:tok0 + n_in]], consume_u)
            seg_barrier()
            ub = alloc([n_in], F32)
            x0b = alloc([n_in], F32)
            hyo = alloc([n_in], BF16)

            ubs = [ub, alloc([n_in], F32)]
            x0bs = [x0b, alloc([n_in], F32)]
            for cc_ in range(2):
                dma("sp", ubs[cc_], uT[cc_ * 128:(cc_ + 1) * 128, tok0:tok0 + n_in], w=[("ub", cc_)])
                dma("sp", x0bs[cc_], x0T[cc_ * 128:(cc_ + 1) * 128, tok0:tok0 + n_in], w=[("x0b", cc_)])

            def consume_y(cc, O, Okeys):
                ub_, x0_ = ubs[cc], x0bs[cc]
                op("dve", lambda e, cc=cc, ub_=ub_: e.scalar_tensor_tensor(out=ub_, in0=ub_, scalar=pp[:, P_hb + cc:P_hb + cc + 1], in1=O[:, 0, :],
                                                                           op0=ALU.mult, op1=ALU.add), r=[("ub", cc), "pp"] + Okeys, w=[("ub", cc)])
                op("dve", lambda e, ub_=ub_, x0_=x0_: e.tensor_tensor(out=hyo, in0=ub_, in1=x0_, op=ALU.mult), r=[("ub", cc), ("x0b", cc)], w=["hyo"])
                dma("sp", mixT[768 + cc * 128:768 + (cc + 1) * 128, tok0:tok0 + n_in], hyo, r=["hyo"], w=[("mixT", "hy", cc, tok0)])
            fft_plan("hi" + sn.lower(), n_in, 1, F32, 2)([Yd[sn][0], Yd[sn][1]], consume_y, srckeys=[("Ysrc", 0), ("Ysrc", 1)])
            seg_barrier()

            wf = alloc([2, 256], F32)
            dcb = alloc([2, 128], F32)
            Wx = alloc([2, 2, 256], BF16)
            dma("sp", wf, w_four[l].rearrange("(c p) n -> p c n", p=128), w=["wf"])
            dma("sp", dcb, cd["dcds"], w=["dcb"])
            fsc = 1.0 if sn == "L" else 4.0
            for o_ in range(2):
                for cc in range(2):
                    pb = kb.psn()
                    mm(PS[pb][:, 0:256], dcb[:, o_, :], wf[:, cc, :], True, True, r=["wf", "dcb"], w=[("ps", pb)])
                    op("act", _act(Wx[:, o_, cc, :], PS[pb][:, 0:256], AF.Identity, scale=fsc), r=[("ps", pb)], w=["Wx"])
            Ok = alloc([2, 2, n_in], BF16)
            fo = alloc([n_in], BF16)

            def consume_f(cc, O, Okeys):
                op("act", _act(Ok[:, cc, :, :], O, AF.Identity), r=Okeys, w=[("Ok", cc)])
            fft_plan("f" + sn.lower(), n_in, 2, BF16, 1)([FT[:, tok0:tok0 + n_in]], consume_f)
            for nchk in range(2):
                for g in range(ncg):
                    pb = kb.psn()
                    i = 0
                    for cc in range(2):
                        for o_ in range(2):
                            mm(PS[pb][:, 0:cw], Wx[:, o_, cc, nchk * 128:(nchk + 1) * 128], Ok[:, cc, o_, g * cw:(g + 1) * cw],
                               i == 0, i == 3, r=["Wx", ("Ok", cc)], w=[("ps", pb)])
                            i += 1
                    op("act", _act(fo[:, g * cw:(g + 1) * cw], PS[pb][:, 0:cw], AF.Identity), r=[("ps", pb)], w=["fo"])
                dma("sp", mixT[512 + nchk * 128:512 + (nchk + 1) * 128, tok0:tok0 + n_in], fo, r=["fo"], w=[("mixT", "f", nchk, tok0)])
            seg_barrier()
            if small:
                kb.barrier(reset_to=layer_top)

        for (sn_, n_, t_) in segs:
            do_seg(sn_, n_, t_)
        if stop_after == ("mix", l):
            return None

        dst_res = res[(2 * l) % 3]
        wo = alloc([8, D], BF16)
        for kk_ in range(4):
            dma("pool", wo[:, 2 * kk_:2 * kk_ + 2, :], w_out[l].rearrange("(k p) f -> p k f", p=128)[:, 2 * kk_:2 * kk_ + 2, :], w=[("wo", kk_)])
        mx = [alloc([8, 512], BF16), alloc([8, 512], BF16)]
        ebufs = ([alloc([1024], F32), alloc([1024], F32)], [alloc([1024], F32), alloc([1024], F32)],
                 alloc([2, 8], F32), [alloc([512], F32), alloc([512], F32)])
        outs = []
        def mx_load(gi):
            t0, n = gl[gi]
            dma("pool", mx[gi % 2][:, :, 0:n], mixT[:, t0:t0 + n].rearrange("(c p) t -> p c t", p=128), w=[("mx", gi % 2)])
        mx_load(0)
        for gi, (t0, n) in enumerate(gl):
            mb = mx[gi % 2]
            if gi + 1 < len(gl):
                mx_load(gi + 1)
            for tp in range(0, n // 128, 2):
                sls = []
                for ti in (tp, tp + 1):
                    t = t0 // 128 + ti
                    pbs = [kb.psn(), kb.psn()]
                    for hb in range(2):
                        for k in range(8):
                            mm(PS[pbs[hb]], mb[:, k, ti * 128:(ti + 1) * 128], wo[:, k, hb * 512:(hb + 1) * 512], k == 0, k == 7,
                               r=[("mx", gi % 2), ("wo", k // 2)], w=[("ps", pbs[hb])])
                    sls.append(epilogue_steps(pbs, t, 0, cur_res, dst_res[t * 128:(t + 1) * 128, :], ebufs))
                run_interleaved(sls)
        kb.barrier(reset_to=layer_top)
        cur_res = dst_res
        if stop_after == ("wout", l):
            return None

        fT = alloc([8, T], BF16)
        mark_f = kb.top
        tiles = list(range(2, NT)) if last else list(range(NT))
        norm_phase(cur_res, 1, fT, tiles)
        kb.barrier(reset_to=mark_f)
        wu = [alloc([2, 8, 128], BF16), alloc([2, 8, 128], BF16)]
        Ug = [[alloc([T + 4], BF16), alloc([T + 4], BF16)], [alloc([T + 4], BF16), alloc([T + 4], BF16)]]
        Cb = [alloc([T + 4], F32), alloc([T + 4], F32)]
        SG = alloc([T + 4], BF16)
        TAB = [alloc([T + 4], BF16), alloc([T + 4], BF16)]
        gst = [alloc([T + 4], BF16), alloc([T + 4], BF16)]
        for jb_ in range(2):
            for gv_ in range(2):
                op("dve", lambda e, jb_=jb_, gv_=gv_: e.memset(Ug[jb_][gv_], 0.0), w=[("Ug", jb_, gv_)])
        wuv = w_up[l].rearrange("(k p) f -> p k f", p=128)
        W_ = T + 2

        def conv_steps(j):
            jb = j % 2
            cw_ = lambda gv, k: pp[:, P_fcw + (gv * 22 + j) * 3 + k:P_fcw + (gv * 22 + j) * 3 + k + 1]
            cb_ = lambda gv: pp[:, P_fcb + gv * 22 + j:P_fcb + gv * 22 + j + 1]
            st_ = []
            ugk = lambda gv: [("Ug", jb, gv, gi_) for gi_ in range(len(gl))] + [("Ug", jb, gv)]
            for gv in range(2):
                st_.append(lambda gv=gv: op("act", _act(Cb[gv][:, 1:1 + W_], Ug[jb][gv][:, 1:1 + W_], AF.Identity,
                                                        bias=cb_(gv), scale=cw_(gv, 1)),
                                            r=ugk(gv), w=[("Cb", gv)]))
            for gv in range(2):
                for k in (0, 2):
                    st_.append(lambda gv=gv, k=k: op("dve", lambda e: e.scalar_tensor_tensor(
                        out=Cb[gv][:, 1:1 + W_], in0=Ug[jb][gv][:, k:k + W_], scalar=cw_(gv, k), in1=Cb[gv][:, 1:1 + W_],
                        op0=ALU.mult, op1=ALU.add), r=ugk(gv) + [("Cb", gv)], w=[("Cb", gv)]))
            st_.append(lambda: op("act", _act(SG[:, 1:1 + W_], Cb[0][:, 1:1 + W_], AF.Silu), r=[("Cb", 0)], w=["SG"]))
            st_.append(lambda: op("dve", lambda e: e.tensor_tensor(out=gst[jb][:, 1:1 + W_], in0=Cb[1][:, 1:1 + W_], in1=SG[:, 1:1 + W_],
                                                                   op=ALU.mult), r=[("Cb", 1), "SG"], w=[("gst", jb)]))

            def s_out():
                if not last:
                    dma("sp", gT[j * 128:(j + 1) * 128, 0:256], gst[jb][:, 1:257], r=[("gst", jb)], w=[("gT", j, 0)])
                dma("sp", gT[j * 128:(j + 1) * 128, 256:T], gst[jb][:, 259:259 + L], r=[("gst", jb)], w=[("gT", j, 1)])
            st_.append(s_out)
            return st_

        pending = []
        for j in range(22):
            jb = j % 2
            for gv in range(2):
                c0 = gv * DFF + j * 128
                dma("pool", wu[jb][:, gv, :, :], wuv[:, :, c0:c0 + 128], w=[("wu", jb, gv)])
                for gi, (t0, n) in enumerate(gl):
                    pb = kb.psn()
                    for k in range(8):
                        mm(PS[pb][:, 0:n], wu[jb][:, gv, k, :], fT[:, k, t0:t0 + n], k == 0, k == 7,
                           r=[("wu", jb, gv)] + [("hT", t) for t in range(t0 // 128, (t0 + n) // 128)], w=[("ps", pb)])
                    pc0 = padcol(t0)
                    op("act", _act(Ug[jb][gv][:, pc0:pc0 + n], PS[pb][:, 0:n], AF.Identity), r=[("ps", pb), ("Ug", jb, gv)], w=[("Ug", jb, gv, gi)])
                    if pending and gi % 2 == 1:
                        pending.pop(0)()
            while pending:
                pending.pop(0)()
            pending = conv_steps(j)
        while pending:
            pending.pop(0)()
        kb.barrier(reset_to=layer_top)
        wd = alloc([22, D], BF16)
        for kk_ in range(11):
            dma("pool", wd[:, 2 * kk_:2 * kk_ + 2, :], w_down[l].rearrange("(k p) f -> p k f", p=128)[:, 2 * kk_:2 * kk_ + 2, :], w=[("wd", kk_)])
        gb = [alloc([22, 512], BF16), alloc([22, 512], BF16)]
        ebufs = ([alloc([1024], F32), alloc([1024], F32)], [alloc([1024], F32), alloc([1024], F32)],
                 alloc([2, 8], F32), [alloc([512], F32), alloc([512], F32)])
        final_ops = []
        def gb_load(gi):
            t0, n = gl[gi]
            dma("pool", gb[gi % 2][:, :, 0:n], gT[:, t0:t0 + n].rearrange("(c p) t -> p c t", p=128),
                r=[("gT", j, 0) for j in range(22)] + [("gT", j, 1) for j in range(22)], w=[("gb", gi % 2)])
        gb_load(0)
        for gi, (t0, n) in enumerate(gl):
            g_ = gb[gi % 2]
            if gi + 1 < len(gl):
                gb_load(gi + 1)
            for tp in range(0, n // 128, 2):
                sls = []
                for ti in (tp, tp + 1):
                    t = t0 // 128 + ti
                    pbs = [kb.psn(), kb.psn()]
                    for hb in range(2):
                        for k in range(22):
                            mm(PS[pbs[hb]], g_[:, k, ti * 128:(ti + 1) * 128], wd[:, k, hb * 512:(hb + 1) * 512], k == 0, k == 21,
                               r=[("gb", gi % 2), ("wd", k // 2)], w=[("ps", pbs[hb])])
                    if last:
                        dst = yout[(t - 2) * 128:(t - 1) * 128, :]
                    else:
                        dst = res[(2 * l + 1) % 3][t * 128:(t + 1) * 128, :]
                    sls.append(epilogue_steps(pbs, t, 1, cur_res, dst, ebufs))
                run_interleaved(sls)
        kb.barrier(reset_to=layer_top)
        cur_res = res[(2 * l + 1) % 3]

        return cur_res

    for l in range(nlayers):
        cur_res = do_layer(l, cur_res)
        if cur_res is None:
            break
    S.emit(nc, [S.bar])
    return kb


_KB = None


def make_in_maps(inp):
    cs = host_consts()
    rows = pack_rows(inp)
    maps = []
    f = lambda a: np.ascontiguousarray(a, dtype=np.float32)
    shared = {k: f(inp[k]) for k in ["w_mod", "w_in", "w_out", "w_up", "w_down", "w_fourier", "hy_w1", "hy_w2", "hy_w3"]}
    for b in range(8):
        m = {"xb": f(inp["x"][b]), "ctxb": f(inp["ctx"][b]), "pp": pack_pp(inp, b), "rows": rows}
        m.update(shared)
        m.update(cs)
        maps.append(m)
    return maps


def kernel(**inputs):
    global _KB
    inp = {k: np.asarray(v) for k, v in inputs.items()}
    if _KB is None:
        _KB = build()
    maps = make_in_maps(inp)
    r = run_bass_kernel_spmd(_KB.nc, maps, core_ids=list(range(8)))
    return np.stack([np.asarray(r.results[b]["y"], dtype=np.float32) for b in range(8)], axis=0)
```

```python
import math
import contextlib
import numpy as np
import ml_dtypes
import concourse.bass as bass
import concourse.mybir as mybir
from concourse.bass_utils import run_bass_kernel_spmd

F32 = mybir.dt.float32
BF16 = mybir.dt.bfloat16
AF = mybir.ActivationFunctionType
ALU = mybir.AluOpType
AX = mybir.AxisListType

D = 1024
L = 4096
C = 256
T = L + C
NT = T // 128
DFF = 2816
INW = 1792
EPS = 1e-6
PL = 270
NPP = 16 + 2 * PL
TWO_PI = 2.0 * math.pi
ATT_DUP = 1


class _Op:
    __slots__ = ("eng", "fn", "deps", "idx", "dma", "marked", "mark_no", "sem", "val", "qn")


class Sched:
    ENGS = ["pe", "act", "dve", "pool", "sp"]
    KDMA = 8

    def __init__(self):
        self.ops = []
        self.last_w = {}
        self.readers = {}
        self.bar = None
        self.last_eng = {}
        self.dmas_since = []

    def add(self, eng, fn, reads=(), writes=(), dma=False):
        op = _Op()
        op.eng, op.fn, op.dma = eng, fn, dma
        op.idx = len(self.ops)
        op.marked = False
        deps = {}
        for k in reads:
            w = self.last_w.get(k)
            if w is not None:
                deps[w.idx] = w
        for k in writes:
            w = self.last_w.get(k)
            if w is not None:
                deps[w.idx] = w
            for r in self.readers.get(k, ()):
                deps[r.idx] = r
        if eng == "pe" and not dma:
            deps = {i: d for i, d in deps.items() if not (d.eng == "pe" and not d.dma)}
        if self.bar is not None:
            deps[self.bar.idx] = self.bar
        op.deps = list(deps.values())
        for k in reads:
            self.readers.setdefault(k, []).append(op)
        for k in writes:
            self.last_w[k] = op
            self.readers[k] = []
        self.ops.append(op)
        if dma:
            self.dmas_since.append(op)
        else:
            self.last_eng[eng] = op
        return op

    def barrier(self, fn):
        op = _Op()
        op.eng, op.fn, op.dma = "pool", fn, False
        op.idx = len(self.ops)
        op.marked = False
        deps = list(self.last_eng.values()) + list(self.dmas_since)
        if self.bar is not None:
            deps.append(self.bar)
        op.deps = deps
        self.ops.append(op)
        self.bar = op
        self.last_eng = {"pool": op}
        self.dmas_since = []
        self.last_w = {}
        self.readers = {}
        return op

    def emit(self, nc, final_deps):
        ops = self.ops
        for op in ops:
            for d in op.deps:
                d.marked = True
        for d in final_deps:
            d.marked = True
        cnt = {e: 0 for e in self.ENGS}
        qcnt = {e: 0 for e in self.ENGS}
        for op in ops:
            if op.dma:
                op.qn = qcnt[op.eng]
                qcnt[op.eng] += 1
            elif op.marked:
                cnt[op.eng] += 1
                op.mark_no = cnt[op.eng]
        with contextlib.ExitStack() as st:
            csem = {e: st.enter_context(nc.semaphore("c_" + e)) for e in ["pe", "act", "dve", "pool"]}
            dsem = {}
            for e in self.ENGS:
                if qcnt[e] > 0:
                    dsem[e] = [st.enter_context(nc.semaphore("d_%s_%d" % (e, i))) for i in range(self.KDMA)]
            for op in ops:
                if op.dma:
                    op.sem = dsem[op.eng][op.qn % self.KDMA]
                    op.val = 16 * (op.qn // self.KDMA + 1)
                elif op.marked:
                    op.sem = csem[op.eng]
                    op.val = op.mark_no
            block = st.enter_context(nc.Block())
            engobj = {"pe": nc.tensor, "act": nc.scalar, "dve": nc.vector, "pool": nc.gpsimd, "sp": nc.sync}
            KD = self.KDMA

            def make(e):
                def body(_h):
                    seen = {}
                    eo = engobj[e]

                    def wait(sem, val):
                        key = id(sem)
                        if seen.get(key, 0) >= val:
                            return
                        seen[key] = val
                        eo.wait_ge(sem, val)

                    for op in ops:
                        if op.eng != e:
                            continue
                        need = {}
                        for d in op.deps:
                            k = id(d.sem)
                            if k not in need or need[k][1] < d.val:
                                need[k] = (d.sem, d.val)
                        if op.dma and op.qn >= KD:
                            k = id(op.sem)
                            v = op.val - 16
                            if k not in need or need[k][1] < v:
                                need[k] = (op.sem, v)
                        for sem, val in need.values():
                            wait(sem, val)
                        if op.fn is None:
                            continue
                        ins = op.fn(eo)
                        if op.dma:
                            ins.then_inc(op.sem, 16)
                        elif op.marked:
                            ins.then_inc(op.sem, 1)
                    if e == "sp":
                        for d in final_deps:
                            wait(d.sem, d.val)
                return body

            block.tensor(make("pe"))
            block.scalar(make("act"))
            block.vector(make("dve"))
            block.gpsimd(make("pool"))
            block.sync(make("sp"))


def _fft_tables(kind, n_in):
    HI = n_in // 64
    lo = np.arange(64, dtype=np.float64)[:, None]
    hi = np.arange(HI, dtype=np.float64)[:, None]
    out = {}
    if kind == "four":
        NR = HI
        r = np.arange(NR, dtype=np.float64)[None, :]
        al = TWO_PI * r * hi / HI
        out["D0"] = np.concatenate([np.cos(al), -np.sin(al)], axis=1)
        k = np.arange(n_in, dtype=np.float64)[None, :]
        ph = TWO_PI * k * lo / n_in
        out["EA"] = np.stack([np.cos(ph), np.sin(ph)], axis=1)
        out["EB"] = np.stack([np.sin(ph), -np.cos(ph)], axis=1)
    elif kind == "hfwd":
        NR = 2 * HI
        r = np.arange(NR, dtype=np.float64)[None, :]
        al = TWO_PI * (2 * r + 1) * hi * 64 / (4 * n_in)
        out["D0"] = np.concatenate([np.cos(al), -np.sin(al)], axis=1)
        f = np.arange(n_in, dtype=np.float64)[None, :]
        ph = TWO_PI * (2 * f + 1) * lo / (4 * n_in)
        out["EA"] = np.stack([np.cos(ph), np.sin(ph)], axis=1)
        out["EB"] = np.stack([np.sin(ph), -np.cos(ph)], axis=1)
    elif kind == "hinv":
        NR = 2 * HI
        r = np.arange(NR, dtype=np.float64)[None, :]
        be = TWO_PI * hi * r / NR
        out["D0"] = np.concatenate([np.cos(be), np.sin(be)], axis=1)
        out["D1"] = np.concatenate([np.sin(be), -np.cos(be)], axis=1)
        t = np.arange(n_in, dtype=np.float64)[None, :]
        ps = TWO_PI * (2 * lo + 1) * t / (4 * n_in)
        sc = 2.0 / (2 * n_in)
        out["EA"] = (sc * np.cos(ps))[:, None, :]
        out["EB"] = (-sc * np.sin(ps))[:, None, :]
    return {k: np.ascontiguousarray(v, dtype=np.float32) for k, v in out.items()}, NR


_CONST = None


def host_consts():
    global _CONST
    if _CONST is not None:
        return _CONST
    cs = {}
    f32 = np.float32
    inv = (np.float32(10000.0) ** (-(np.arange(0, 32, 2, dtype=f32)) / f32(32))).astype(f32)
    row = np.repeat(np.arange(64, dtype=f32), 64)
    col = np.tile(np.arange(64, dtype=f32), 64)
    ar = (row[:, None] * inv[None, :]).astype(f32).astype(np.float64)
    ac = (col[:, None] * inv[None, :]).astype(f32).astype(np.float64)
    cos64 = np.concatenate([np.cos(ar), np.cos(ar), np.cos(ac), np.cos(ac)], axis=1)
    sin64 = np.concatenate([-np.sin(ar), np.sin(ar), -np.sin(ac), np.sin(ac)], axis=1)
    tab = np.zeros((T, 128), f32)
    tab[:C, :64] = 1.0
    tab[C:, :64] = cos64
    tab[C:, 64:] = sin64
    cs["ropetab"] = tab
    for nm, kind, n in [("fl", "four", L), ("fc", "four", C), ("hfl", "hfwd", L), ("hfc", "hfwd", C),
                        ("hil", "hinv", L), ("hic", "hinv", C)]:
        tb, NR = _fft_tables(kind, n)
        for k, v in tb.items():
            cs[nm + "_" + k] = v.astype(ml_dtypes.bfloat16)
    for nm, n in [("L", L), ("C", C)]:
        t = np.linspace(0.0, 1.0, n, dtype=f32)[:, None]
        bands = np.linspace(1e-4, 15, 16, dtype=f32)
        ang = ((f32(TWO_PI) * np.arange(n, dtype=f32)[:, None]) / f32(n) * bands[None, :]).astype(f32)
        z = np.concatenate([t, np.cos(ang.astype(np.float64)), -np.sin(ang.astype(np.float64))], axis=-1)
        cs["zT_" + nm] = np.ascontiguousarray(z.T, dtype=f32)
        deltas = np.abs(np.linspace(math.log(1e-2) / 1.5, math.log(1e-2) / 0.3, 256, dtype=f32))
        dec = np.exp(-(t * deltas[None, :]).astype(f32).astype(np.float64))
        cs["decT_" + nm] = np.ascontiguousarray(dec.T, dtype=f32)
    j = np.arange(64, dtype=np.float64)
    b = TWO_PI * np.outer(j, j) / 64.0
    blk = np.zeros((128, 2, 128), np.float64)
    for g in range(2):
        blk[g * 64:(g + 1) * 64, 0, g * 64:(g + 1) * 64] = np.cos(b) / 512.0
        blk[g * 64:(g + 1) * 64, 1, g * 64:(g + 1) * 64] = -np.sin(b) / 512.0
    cs["dcds"] = blk.astype(f32)
    _CONST = cs
    return cs


def pack_pp(inp, b):
    pp = np.zeros((128, NPP), np.float32)
    pp[:, 0:16:2] = inp["c"][b].reshape(8, 128).T
    pp[:, 1:16:2] = inp["c_ctx"].reshape(8, 128).T
    for l in range(2):
        o = 16 + l * PL
        pp[:, o:o + 48] = inp["b_mod"][l].reshape(48, 128).T; o += 48
        pp[:, o:o + 8] = inp["g_pre_mix"][l].reshape(8, 128).T; o += 8
        pp[:, o:o + 8] = inp["g_pre_ffn"][l].reshape(8, 128).T; o += 8
        pp[:, o:o + 18] = inp["w_hy_conv"][l].reshape(3, 6, 128).transpose(2, 1, 0).reshape(128, 18); o += 18
        pp[:, o:o + 6] = inp["b_hy_conv"][l].reshape(6, 128).T; o += 6
        pp[:, o:o + 2] = inp["hy_bias"][l].reshape(2, 128).T; o += 2
        pp[:, o:o + 132] = inp["w_ffn_conv"][l].reshape(3, 44, 128).transpose(2, 1, 0).reshape(128, 132); o += 132
        pp[:, o:o + 44] = inp["b_ffn_conv"][l].reshape(44, 128).T; o += 44
        pp[0:64, o] = inp["hy_b1"][l]; pp[0:64, o + 1] = inp["hy_fr1"][l]
        pp[0:64, o + 2] = inp["hy_b2"][l]; pp[0:64, o + 3] = inp["hy_fr2"][l]
    return pp


def pack_rows(inp):
    rows = np.zeros((2, 6, 1024), np.float32)
    for l in range(2):
        rows[l, 0] = inp["b_mod"][l][2048:3072]
        rows[l, 1] = inp["b_mod"][l][5120:6144]
        rows[l, 2] = inp["g_post_mix"][l]
        rows[l, 3] = inp["g_post_ffn"][l]
        rows[l, 4, 0:512] = np.tile(inp["g_q"][l], 8)
        rows[l, 4, 512:640] = np.tile(inp["g_k"][l], 2)
    return rows


class KB:
    AW = 52900

    def __init__(self, dbg=()):
        self.nc = bass.Bass("TRN2", target_bir_lowering=False)
        nc = self.nc
        self.S = Sched()
        self.arena = nc.alloc_sbuf_tensor("arena", [128, self.AW], F32)
        self.top = 0
        self.psall = nc.alloc_psum_tensor("psall", [128, 4096], F32)[:]
        self.ps = [self.psall[:, i * 512:(i + 1) * 512] for i in range(8)]
        self.psi = 0
        self.dbg = set(dbg)
        self.drams = {}
        self.nbar = 0

    def alloc(self, shape, dtype=F32, parts=128):
        n = 1
        for s in shape:
            n *= s
        esz = 4 if dtype == F32 else 2
        words = (n * esz + 3) // 4
        words = (words + 7) // 8 * 8
        w0 = self.top
        self.top += words
        assert self.top <= self.AW, "SBUF arena overflow %d" % self.top
        ap = self.arena[0:parts, w0:w0 + words]
        if dtype != F32:
            ap = ap.bitcast(dtype)
        ap = ap[:, 0:n]
        if len(shape) == 2:
            ap = ap.rearrange("p (a b) -> p a b", a=shape[0])
        elif len(shape) == 3:
            ap = ap.rearrange("p (a b c) -> p a b c", a=shape[0], b=shape[1])
        elif len(shape) == 4:
            ap = ap.rearrange("p (a b c d) -> p a b c d", a=shape[0], b=shape[1], c=shape[2])
        return ap

    def dram_in(self, name, shape, dtype=F32):
        t = self.nc.dram_tensor(name, list(shape), dtype, kind="ExternalInput").ap()
        self.drams[name] = t
        return t

    def dram(self, name, shape, dtype=F32, out=False):
        kind = "ExternalOutput" if (out or name in self.dbg) else "Internal"
        t = self.nc.dram_tensor(name, list(shape), dtype, kind=kind).ap()
        self.drams[name] = t
        return t

    def psn(self):
        i = self.psi
        self.psi = (self.psi + 1) % 8
        return i

    def op(self, eng, fn, r=(), w=()):
        return self.S.add(eng, fn, reads=r, writes=w)

    def dma(self, eng, out, in_, r=(), w=()):
        return self.S.add(eng, lambda e: e.dma_start(out=out, in_=in_), reads=r, writes=w, dma=True)

    def mm(self, out, lhsT, rhs, start, stop, r=(), w=()):
        return self.S.add("pe", lambda e: e.matmul(out, lhsT=lhsT, rhs=rhs, start=start, stop=stop), reads=r, writes=w)

    def barrier(self, reset_to=None):
        d = self.bdummy
        self.S.barrier(lambda e: e.memset(d, 0.0))
        if reset_to is not None:
            self.top = reset_to


def _act(out, in_, func, bias=None, scale=None, accum=None):
    kw = {}
    if bias is not None:
        kw["bias"] = bias
    if scale is not None:
        kw["scale"] = scale
    if accum is not None:
        kw["accum_out"] = accum
    return lambda e: e.activation(out=out, in_=in_, func=func, **kw)


def build(nlayers=2, dbg=(), stop_after=None):
    kb = KB(dbg)
    nc, S = kb.nc, kb.S
    op, dma, mm, alloc = kb.op, kb.dma, kb.mm, kb.alloc
    PS = kb.ps
    cs = host_consts()

    xb = kb.dram_in("xb", [L, D])
    ctxb = kb.dram_in("ctxb", [C, D])
    ppd = kb.dram_in("pp", [128, NPP])
    rowsd = kb.dram_in("rows", [2, 6, 1024])
    w_mod = kb.dram_in("w_mod", [2, D, 6 * D])
    w_in = kb.dram_in("w_in", [2, D, INW])
    w_out = kb.dram_in("w_out", [2, D, D])
    w_up = kb.dram_in("w_up", [2, D, 2 * DFF])
    w_down = kb.dram_in("w_down", [2, DFF, D])
    w_four = kb.dram_in("w_fourier", [2, 256, 256])
    hy_w1 = kb.dram_in("hy_w1", [2, 33, 64])
    hy_w2 = kb.dram_in("hy_w2", [2, 64, 64])
    hy_w3 = kb.dram_in("hy_w3", [2, 64, 512])
    cd = {k: kb.dram_in(k, v.shape, BF16 if v.dtype == ml_dtypes.bfloat16 else F32) for k, v in cs.items()}
    yout = kb.dram("y", [L, D], out=True)
    res = [kb.dram("res%d" % i, [T, D]) for i in range(3)]
    mixT = kb.dram("mixT", [D, T], BF16)
    FT = kb.dram("FT", [256, T])
    uT = kb.dram("uT", [256, T])
    x0T = kb.dram("x0T", [256, T])
    hpm = {"L": kb.dram("hpmL", [2, 256, L]), "C": kb.dram("hpmC", [2, 256, C])}
    Gd = {"L": kb.dram("GL", [2, 256, L]), "C": kb.dram("GC", [2, 256, C])}
    Yd = {"L": kb.dram("YL", [2, 256, L]), "C": kb.dram("YC", [2, 256, C])}
    gT = kb.dram("gT", [DFF, T], BF16)

    ident = alloc([128], F32)
    kb.bdummy = alloc([8], F32)
    pp = alloc([NPP], F32)
    epsc = alloc([4], F32)
    base_top = kb.top

    op("pool", lambda e: e.memset(ident, 0.0), w=["ident"])
    op("pool", lambda e: e.affine_select(out=ident, in_=ident, pattern=[[-1, 128]], compare_op=ALU.not_equal,
                                         fill=1.0, base=0, channel_multiplier=1), r=["ident"], w=["ident"])
    op("pool", lambda e: e.memset(epsc[:, 0:1], EPS), w=["epsc"])
    op("pool", lambda e: e.memset(epsc[:, 1:2], -math.pi), w=["epsc"])
    dma("sp", pp, ppd, w=["pp"])
    kb.barrier()

    def tile_src(srcs, t):
        if srcs == "in":
            return ctxb[t * 128:(t + 1) * 128, :] if t < 2 else xb[(t - 2) * 128:(t - 1) * 128, :]
        return srcs[t * 128:(t + 1) * 128, :]

    groups = [(0, 256)] + [(256 + 512 * i, 512) for i in range(8)]

    def padcol(tok):
        return 1 + tok if tok < 256 else 3 + tok

    cur_res = "in"
    def do_layer(l, cur_res):
        last = (l == 1)
        po = 16 + l * PL
        P_bmod, P_gpm, P_gpf = po, po + 48, po + 56
        P_hcw, P_hcb, P_hb, P_fcw, P_fcb, P_hy = po + 64, po + 82, po + 88, po + 90, po + 222, po + 266
        kb.top = base_top
        modc = alloc([48, 2], F32)
        AB = alloc([2, 2, 2, 8], F32)
        GA = alloc([2, 2, 1024], F32)
        layer_top = kb.top

        sc = alloc([8, 2], BF16)
        rep = alloc([2, 8, 128], BF16)
        ones = alloc([128], F32)
        rowsb = alloc([4, 1024], F32)
        wm = [alloc([8, 1024], BF16), alloc([8, 1024], BF16), alloc([8, 1024], BF16)]
        scf = alloc([8, 2], F32)
        op("act", _act(scf, pp[:, 0:16].rearrange("p (j w) -> p j w", w=2), AF.Silu), r=["pp"], w=["scf"])
        op("dve", lambda e: e.tensor_copy(out=sc, in_=scf), r=["scf"], w=["sc"])
        op("pool", lambda e: e.memset(ones, 1.0), w=["ones"])
        for wh in range(2):
            for j in range(8):
                op("act", _act(rep[:, wh, j, :], ones, AF.Identity, scale=scf[:, j, wh:wh + 1]), r=["scf", "ones"], w=["rep"])
        dma("sp", rowsb.rearrange("p a b -> p (a b)"), rowsd[l, 0:4, :].rearrange("r f -> (r f)").partition_broadcast(128), w=["rowsb"])
        wmv = w_mod[l].rearrange("(k p) f -> p k f", p=128)
        pm = 0
        gab = [1]
        for pc in range(6):
            wb = wm[pc % 3]
            dma("pool", wb, wmv[:, :, pc * 1024:(pc + 1) * 1024], w=[("wm", pc % 3)])
            for f in range(8):
                fc = pc * 8 + f
                for k in range(8):
                    mm(PS[pm][:, 2 * fc:2 * fc + 2], wb[:, k, f * 128:(f + 1) * 128], sc[:, k, :], k == 0, k == 7,
                       r=[("wm", pc % 3), "sc"], w=[("ps", pm)])
            if pc in (2, 5):
                sub = 0 if pc == 2 else 1
                for wh in range(2):
                    for hf in range(2):
                        pb = gab[0]
                        gab[0] = gab[0] % 7 + 1
                        for k in range(8):
                            mm(PS[pb], rep[:, wh, k, :], wb[:, k, hf * 512:(hf + 1) * 512], k == 0, k == 7,
                               r=[("wm", pc % 3), "rep"], w=[("ps", pb)])
                        gsl = GA[:, sub, wh, hf * 512:(hf + 1) * 512]
                        op("dve", lambda e, gsl=gsl, pb=pb, sub=sub, hf=hf: e.tensor_tensor(
                            out=gsl, in0=PS[pb], in1=rowsb[:, sub, hf * 512:(hf + 1) * 512], op=ALU.add),
                           r=[("ps", pb), "rowsb"], w=[("GA", sub, wh, hf)])
                        op("dve", lambda e, gsl=gsl, sub=sub, hf=hf: e.tensor_tensor(
                            out=gsl, in0=gsl, in1=rowsb[:, 2 + sub, hf * 512:(hf + 1) * 512], op=ALU.mult),
                           r=["rowsb", ("GA", sub, wh, hf)], w=[("GA", sub, wh, hf)])
        op("dve", lambda e: e.tensor_tensor(out=modc, in0=PS[pm][:, 0:96].rearrange("p (f w) -> p f w", w=2),
                                            in1=pp[:, P_bmod:P_bmod + 48].unsqueeze(2).to_broadcast([128, 48, 2]),
                                            op=ALU.add), r=[("ps", pm), "pp"], w=["modc"])
        for sub in range(2):
            gcol = P_gpm if sub == 0 else P_gpf
            shc, scc = (0, 8) if sub == 0 else (24, 32)
            for wh in range(2):
                a_ap = AB[:, sub, wh, 0, :]
                b_ap = AB[:, sub, wh, 1, :]
                op("dve", lambda e, a_ap=a_ap, scc=scc, wh=wh: e.tensor_scalar(
                    out=a_ap, in0=modc[:, scc:scc + 8, wh], scalar1=1.0, scalar2=None, op0=ALU.add),
                   r=["modc"], w=[("AB", sub, wh, 0)])
                op("dve", lambda e, a_ap=a_ap, gcol=gcol: e.tensor_tensor(out=a_ap, in0=a_ap, in1=pp[:, gcol:gcol + 8], op=ALU.mult),
                   r=[("AB", sub, wh, 0), "pp"], w=[("AB", sub, wh, 0)])
                op("dve", lambda e, b_ap=b_ap, shc=shc, wh=wh: e.tensor_copy(out=b_ap, in_=modc[:, shc:shc + 8, wh]),
                   r=["modc"], w=[("AB", sub, wh, 1)])
        kb.barrier(reset_to=layer_top)
        if stop_after == ("mod", l):
            return None

        def norm_phase(src, sub, hT, tiles):
            xt = [alloc([1024], F32), alloc([1024], F32), alloc([1024], F32)]
            xs = [alloc([1024], F32), alloc([1024], F32), alloc([1024], F32)]
            junk = alloc([1024], F32)
            st = alloc([34, 4], F32)

            def stage_a(i, t):
                b = i % 3
                dma("sp", xt[b], tile_src(src, t), w=[("xt", b)])
                op("pool", lambda e, t=t: e.memset(st[:, t, 0:1], 0.0), w=[("st", t)])
                op("act", _act(junk, xt[b], AF.Square, accum=st[:, t, 0:1]), r=[("xt", b), ("st", t)], w=["junk", ("st", t)])
                op("act", _act(st[:, t, 1:2], st[:, t, 0:1], AF.Sqrt, bias=epsc[:, 0:1], scale=1.0 / D),
                   r=[("st", t)], w=[("st1", t)])
                op("dve", lambda e, t=t: e.reciprocal(out=st[:, t, 2:3], in_=st[:, t, 1:2]), r=[("st1", t)], w=[("st2", t)])
                op("dve", lambda e, t=t, b=b: e.tensor_scalar(out=xs[b], in0=xt[b], scalar1=st[:, t, 2:3], scalar2=None,
                                                              op0=ALU.mult), r=[("xt", b), ("st2", t)], w=[("xs", b)])

            def stage_b(i, t):
                wh = 1 if t < 2 else 0
                b = i % 3
                for hb in range(2):
                    pb = kb.psn()
                    for jj in range(4):
                        j = hb * 4 + jj
                        op("pe", lambda e, pb=pb, jj=jj, j=j, b=b: e.transpose(PS[pb][:, jj * 128:(jj + 1) * 128],
                                                                               xs[b][:, j * 128:(j + 1) * 128], ident),
                           r=[("xs", b), "ident"], w=[("ps", pb)])
                    for jj in range(4):
                        j = hb * 4 + jj
                        if hb == 0:
                            op("act", _act(hT[:, j, t * 128:(t + 1) * 128], PS[pb][:, jj * 128:(jj + 1) * 128], AF.Identity,
                                           bias=AB[:, sub, wh, 1, j:j + 1], scale=AB[:, sub, wh, 0, j:j + 1]),
                               r=[("ps", pb)], w=[("hT", t, j)])
                        else:
                            op("dve", lambda e, pb=pb, jj=jj, j=j, t=t, wh=wh: e.tensor_scalar(
                                out=hT[:, j, t * 128:(t + 1) * 128], in0=PS[pb][:, jj * 128:(jj + 1) * 128],
                                scalar1=AB[:, sub, wh, 0, j:j + 1], scalar2=AB[:, sub, wh, 1, j:j + 1], op0=ALU.mult, op1=ALU.add),
                               r=[("ps", pb)], w=[("hT", t, j)])

            stage_a(0, tiles[0])
            if len(tiles) > 1:
                stage_a(1, tiles[1])
            for i, t in enumerate(tiles):
                if i + 2 < len(tiles):
                    stage_a(i + 2, tiles[i + 2])
                stage_b(i, t)

        def epilogue_steps(pbs, t, sub, src, dst_ap, bufs):
            xt2, tmp, st2, junk2 = bufs
            wh = 1 if t < 2 else 0
            b = t % 2
            steps = []
            steps.append(lambda: dma("sp", xt2[b], tile_src(src, t), w=[("xt2", b)]))
            steps.append(lambda: op("dve", lambda e: e.memset(st2[:, b, 0:2], 0.0), w=[("st2a", b)]))
            for hb in range(2):
                steps.append(lambda hb=hb: op("act", _act(junk2[b], PS[pbs[hb]], AF.Square, accum=st2[:, b, hb:hb + 1]),
                                              r=[("ps", pbs[hb]), ("st2a", b)], w=[("junk2", b), ("st2a", b)]))
            steps.append(lambda: op("dve", lambda e: e.tensor_tensor(out=st2[:, b, 2:3], in0=st2[:, b, 0:1], in1=st2[:, b, 1:2], op=ALU.add),
                                    r=[("st2a", b)], w=[("st2b", b)]))
            steps.append(lambda: op("act", _act(st2[:, b, 3:4], st2[:, b, 2:3], AF.Sqrt, bias=epsc[:, 0:1], scale=1.0 / D),
                                    r=[("st2b", b)], w=[("st2c", b)]))
            steps.append(lambda: op("dve", lambda e: e.reciprocal(out=st2[:, b, 4:5], in_=st2[:, b, 3:4]), r=[("st2c", b)], w=[("st2d", b)]))
            for hb in range(2):
                steps.append(lambda hb=hb: op("dve", lambda e: e.scalar_tensor_tensor(
                    out=tmp[b][:, hb * 512:(hb + 1) * 512], in0=PS[pbs[hb]], scalar=st2[:, b, 4:5],
                    in1=GA[:, sub, wh, hb * 512:(hb + 1) * 512], op0=ALU.mult, op1=ALU.mult),
                    r=[("ps", pbs[hb]), ("st2d", b)], w=[("tmp", b)]))
            steps.append(lambda: op("dve", lambda e: e.tensor_tensor(out=tmp[b], in0=tmp[b], in1=xt2[b], op=ALU.add),
                                    r=[("tmp", b), ("xt2", b)], w=[("tmp", b)]))
            steps.append(lambda: dma("sp", dst_ap, tmp[b], r=[("tmp", b)], w=[("res", t)]))
            return steps

        def run_interleaved(step_lists):
            n = max(len(sl) for sl in step_lists)
            for k in range(n):
                for sl in step_lists:
                    if k < len(sl):
                        sl[k]()

        def diag_build(dg, col0, j):
            for k in range(3):
                op("dve", lambda e, k=k: e.tensor_scalar(out=dg[:, k, :], in0=ident, scalar1=pp[:, col0 + k:col0 + k + 1],
                                                         scalar2=None, op0=ALU.mult), r=["ident", "pp"], w=[("dg", j)])

        def fft_plan_merged(nm, n_in, NO, out_dt, nsrc):
            HI = n_in // 64
            NR = cd[nm + "_D0"].shape[1] // 2
            NF1 = 2 * NR
            n = n_in // NR
            assert HI == 64
            KS = HI * nsrc
            Dts = alloc([NF1], BF16)
            for i in range(nsrc):
                dma("sp", Dts[i * HI:(i + 1) * HI], cd[nm + "_D%d" % i], w=[("Dt", i)])
            E2 = alloc([NO, n_in], BF16)
            dma("sp", E2[0:64], cd[nm + "_EA"], w=["E2a"])
            dma("act", E2[64:128], cd[nm + "_EB"], w=["E2b"])
            XX = [alloc([128, 64], BF16) for _ in range(2)]
            xcnt = [0]
            Z = alloc([NR, 128], BF16)
            O = alloc([NO, n_in], out_dt)
            cpb = min(128, 512 // NF1)
            rpb = min(NR, 512 // (NO * n))
            Zkeys = [("Z", c0, hh) for c0 in range(0, 128, cpb) for hh in range(2)]
            Okeys = [("O", r0) for r0 in range(0, NR, rpb)]
            cnt = [0]

            def run(srcs, consume, srckeys=(), nchunks=2):
                for cc in range(nchunks):
                    xb = xcnt[0] % 2
                    xcnt[0] += 1
                    X = XX[xb]
                    for i in range(nsrc):
                        for q4 in range(4):
                            dma("pool", X[i * HI:(i + 1) * HI, q4 * 32:(q4 + 1) * 32, :],
                                srcs[i][cc * 128 + q4 * 32:cc * 128 + (q4 + 1) * 32, :].rearrange("c (h q) -> h c q", q=64),
                                r=list(srckeys), w=[("X", xb, i)])
                    for c0 in range(0, 128, cpb):
                        pb = kb.psn()
                        for ch in range(c0, c0 + cpb):
                            mm(PS[pb][0:64, (ch - c0) * NF1:(ch - c0 + 1) * NF1], X[0:KS, ch, :], Dts[0:KS, :],
                               True, True, r=[("X", xb, i) for i in range(nsrc)] + [("Dt", i) for i in range(nsrc)], w=[("ps", pb)])
                        psv = PS[pb][0:64, 0:cpb * NF1].rearrange("p (c h f) -> p h f c", h=2, f=NR)
                        op("act", _act(Z[0:64, :, c0:c0 + cpb], psv[:, 0, :, :], AF.Identity), r=[("ps", pb)], w=[("Z", c0, 0)])
                        src_ap = psv[:, 1, :, :]
                        dst_ap = Z[64:128, :, c0:c0 + cpb]
                        op("dve", lambda e, src_ap=src_ap, dst_ap=dst_ap: e.tensor_copy(out=dst_ap, in_=src_ap),
                           r=[("ps", pb), ("Z", c0, 0)], w=[("Z", c0, 1)])
                    Ov = O.rearrange("p o (j r) -> p o j r", r=NR)
                    for r0 in range(0, NR, rpb):
                        pb = kb.psn()
                        for rr in range(rpb):
                            r_ = r0 + rr
                            first = (r0 == 0 and rr == 0)
                            lastm = (r0 + rpb >= NR and rr == rpb - 1)
                            oap = PS[pb][:, rr * NO * n:(rr + 1) * NO * n].rearrange("p (o j) -> p o j", o=NO)
                            mm(oap, Z[:, r_, :], E2[:, :, r_:n_in:NR], True, True,
                               r=(Zkeys if (first or lastm) else []) + ["E2a", "E2b"], w=[("ps", pb)])
                        src_ap = PS[pb][:, 0:rpb * NO * n].rearrange("p (r o j) -> p o j r", o=NO, j=n)
                        dst_ap = Ov[:, :, :, r0:r0 + rpb]
                        op("act", _act(dst_ap, src_ap, AF.Identity), r=[("ps", pb)], w=[("O", r0)])
                    consume(cc, O, Okeys)
            return run

        def fft_plan_split(nm, n_in, NO, out_dt, nsrc):
            HI = n_in // 64
            NR = cd[nm + "_D0"].shape[1] // 2
            NF1 = 2 * NR
            n = n_in // NR
            Dt = []
            for i in range(nsrc):
                dt_ = alloc([NF1], BF16)
                dma("sp", dt_[0:HI], cd[nm + "_D%d" % i], w=[("Dt", i)])
                Dt.append(dt_)
            EA = alloc([NO, n_in], BF16)
            EB = alloc([NO, n_in], BF16)
            dma("sp", EA[0:64], cd[nm + "_EA"], w=["EA"])
            dma("act", EB[0:64], cd[nm + "_EB"], w=["EB"])
            X = [alloc([128, 64], BF16) for _ in range(nsrc)]
            Z = alloc([NF1, 128], BF16)
            O = alloc([NO, n_in], out_dt)
            cpb = min(128, 512 // NF1)
            rpb = min(NR, 512 // (NO * n))
            Zkeys = [("Z", c0) for c0 in range(0, 128, cpb)]
            Okeys = [("O", r0) for r0 in range(0, NR, rpb)]
            cnt = [0]

            def run(srcs, consume, srckeys=(), nchunks=2):
                for cc in range(nchunks):
                    for i in range(nsrc):
                        for q4 in range(4):
                            dma("pool", X[i][0:HI, q4 * 32:(q4 + 1) * 32, :],
                                srcs[i][cc * 128 + q4 * 32:cc * 128 + (q4 + 1) * 32, :].rearrange("c (h q) -> h c q", q=64),
                                r=list(srckeys), w=[("X", i)])
                    for c0 in range(0, 128, cpb):
                        pb = kb.psn()
                        for ch in range(c0, c0 + cpb):
                            for i in range(nsrc):
                                mm(PS[pb][0:64, (ch - c0) * NF1:(ch - c0 + 1) * NF1], X[i][0:HI, ch, :], Dt[i][0:HI, :],
                                   i == 0, i == nsrc - 1, r=[("X", i), ("Dt", i)], w=[("ps", pb)])
                        src_ap = PS[pb][0:64, 0:cpb * NF1].rearrange("p (c f) -> p f c", f=NF1)
                        dst_ap = Z[0:64, :, c0:c0 + cpb]
                        if cnt[0] % 2 == 0:
                            op("act", _act(dst_ap, src_ap, AF.Identity), r=[("ps", pb)], w=[("Z", c0)])
                        else:
                            op("dve", lambda e, src_ap=src_ap, dst_ap=dst_ap: e.tensor_copy(out=dst_ap, in_=src_ap),
                               r=[("ps", pb)], w=[("Z", c0)])
                        cnt[0] += 1
                    Ov = O.rearrange("p o (j r) -> p o j r", r=NR)
                    for r0 in range(0, NR, rpb):
                        pb = kb.psn()
                        for rr in range(rpb):
                            r_ = r0 + rr
                            first = (r0 == 0 and rr == 0)
                            lastm = (r0 + rpb >= NR and rr == rpb - 1)
                            oap = PS[pb][:, rr * NO * n:(rr + 1) * NO * n].rearrange("p (o j) -> p o j", o=NO)
                            mm(oap, Z[0:64, r_, :], EA[0:64, :, r_:n_in:NR], True, False,
                               r=(Zkeys if first else []) + ["EA"], w=[("ps", pb)])
                            mm(oap, Z[0:64, NR + r_, :], EB[0:64, :, r_:n_in:NR], False, True,
                               r=(Zkeys if lastm else []) + ["EB"], w=[("ps", pb)])
                        src_ap = PS[pb][:, 0:rpb * NO * n].rearrange("p (r o j) -> p o j r", o=NO, j=n)
                        dst_ap = Ov[:, :, :, r0:r0 + rpb]
                        if cnt[0] % 2 == 0:
                            op("act", _act(dst_ap, src_ap, AF.Identity), r=[("ps", pb)], w=[("O", r0)])
                        else:
                            op("dve", lambda e, src_ap=src_ap, dst_ap=dst_ap: e.tensor_copy(out=dst_ap, in_=src_ap),
                               r=[("ps", pb)], w=[("O", r0)])
                        cnt[0] += 1
                    consume(cc, O, Okeys)
            return run

        def fft_plan(nm, n_in, NO, out_dt, nsrc):
            if cd[nm + "_D0"].shape[1] // 2 == 128:
                return fft_plan_merged(nm, n_in, NO, out_dt, nsrc)
            return fft_plan_split(nm, n_in, NO, out_dt, nsrc)

        hT = alloc([8, T], BF16)
        mark_hT = kb.top
        norm_phase(cur_res, 0, hT, list(range(NT)))
        kb.barrier(reset_to=mark_hT)
        if stop_after == ("norm", l):
            dbg_hT = kb.dram("dbg_hT", [128, 8 * T], BF16, out=True)
            dma("sp", dbg_hT, hT.rearrange("p a b -> p (a b)"), r=[])
            return None

        mark_wi = kb.top
        wiA = alloc([8, 1024], BF16)
        for c8_ in range(8):
            dma("pool", wiA[:, :, c8_ * 128:(c8_ + 1) * 128],
                w_in[l].rearrange("(k p) f -> p k f", p=128)[:, :, 768 + c8_ * 128:768 + (c8_ + 1) * 128], w=[("wiA", c8_)])
        stage = [alloc([T], F32), alloc([T], F32)]
        x1T = alloc([2, T], F32)
        Ub = alloc([T + 4], BF16)
        dgs = [alloc([3, 128], BF16), alloc([3, 128], BF16)]
        op("dve", lambda e: e.memset(Ub, 0.0), w=["Ub"])
        gl = groups[1:] if last else groups
        for c8 in range(8):
            stg = stage[c8 % 2]
            col = 768 + c8 * 128
            if c8 >= 2:
                diag_build(dgs[c8 % 2], P_hcw + (c8 - 2) * 3, c8 % 2)
            for gi, (t0, n) in enumerate(gl):
                pb = kb.psn()
                for k in range(8):
                    mm(PS[pb][:, 0:n], wiA[:, k, col - 768:col - 768 + 128], hT[:, k, t0:t0 + n], k == 0, k == 7,
                       r=[("wiA", c8)] + [("hT", t) for t in range(t0 // 128, (t0 + n) // 128)], w=[("ps", pb)])
                if c8 < 2:
                    op("act", _act(stg[:, t0:t0 + n], PS[pb][:, 0:n], AF.Identity), r=[("ps", pb)], w=[("stage", c8 % 2)])
                else:
                    pc0 = padcol(t0)
                    op("act", _act(Ub[:, pc0:pc0 + n], PS[pb][:, 0:n], AF.Identity), r=[("ps", pb)], w=["Ub"])
            if c8 < 2:
                dma("sp", FT[c8 * 128:(c8 + 1) * 128, :], stg, r=[("stage", c8 % 2)], w=[("src", "FT")])
                continue
            hc = c8 - 2
            for gi, (t0, n) in enumerate(gl):
                pb = kb.psn()
                pc0 = padcol(t0)
                for k in range(3):
                    mm(PS[pb][:, 0:n], dgs[c8 % 2][:, k, :], Ub[:, pc0 - 1 + k:pc0 - 1 + k + n], k == 0, k == 2,
                       r=[("dg", c8 % 2), "Ub"], w=[("ps", pb)])
                bcol = pp[:, P_hcb + hc:P_hcb + hc + 1]
                if hc < 2:
                    op("act", _act(stg[:, t0:t0 + n], PS[pb][:, 0:n], AF.Identity, bias=bcol, scale=1.0),
                       r=[("ps", pb)], w=[("stage", c8 % 2)])
                elif hc < 4:
                    op("act", _act(x1T[:, hc - 2, t0:t0 + n], PS[pb][:, 0:n], AF.Identity, bias=bcol, scale=1.0),
                       r=[("ps", pb)], w=[("x1T", hc - 2)])
                else:
                    op("dve", lambda e, stg=stg, pb=pb, t0=t0, n=n, bcol=bcol, hc=hc: e.scalar_tensor_tensor(
                        out=stg[:, t0:t0 + n], in0=PS[pb][:, 0:n], scalar=bcol, in1=x1T[:, hc - 4, t0:t0 + n],
                        op0=ALU.add, op1=ALU.mult), r=[("ps", pb), ("x1T", hc - 4)], w=[("stage", c8 % 2)])
            if hc < 2:
                dma("sp", x0T[hc * 128:(hc + 1) * 128, :], stg, r=[("stage", c8 % 2)], w=[("x0T", hc)])
            elif hc >= 4:
                dma("sp", uT[(hc - 4) * 128:(hc - 3) * 128, :], stg, r=[("stage", c8 % 2)], w=[("src", "uT")])
        kb.barrier(reset_to=mark_wi)
        if stop_after == ("wina", l):
            return None

        QT = alloc([2, 2, T], BF16)
        KT = alloc([2, T], BF16)
        Va = alloc([NT, 2, 128], BF16)
        mark_qkv = kb.top
        wi = alloc([8, 768], BF16)
        for kk_ in range(4):
            dma("pool", wi[:, 2 * kk_:2 * kk_ + 2, :], w_in[l].rearrange("(k p) f -> p k f", p=128)[:, 2 * kk_:2 * kk_ + 2, 0:768], w=[("wi", kk_)])
        gqk = alloc([640], F32)
        dma("sp", gqk, rowsd[l, 4, 0:640].partition_broadcast(128), w=["gqk"])
        op("dve", lambda e: e.memset(Va, 1.0), w=["Va1"])
        rt = [alloc([128], F32) for _ in range(4)]
        sq = [alloc([640], F32), alloc([640], F32)]
        ss = alloc([NT, 3, 10], F32)
        qn = [alloc([640], F32), alloc([640], F32)]
        t1 = [alloc([640], F32), alloc([640], F32)]
        t2 = [alloc([640], F32), alloc([640], F32)]
        qkr = [alloc([768], F32), alloc([768], F32)]
        def winb_steps(t):
            b = t % 2
            rtt = rt[t % 4]
            pq, pk = kb.psn(), kb.psn()
            st_ = []

            def s_mm():
                for k in range(8):
                    mm(PS[pq], hT[:, k, t * 128:(t + 1) * 128], wi[:, k, 0:512], k == 0, k == 7, r=[("hT", t), ("wi", k // 2)], w=[("ps", pq)])
                for k in range(8):
                    mm(PS[pk][:, 0:256], hT[:, k, t * 128:(t + 1) * 128], wi[:, k, 512:768], k == 0, k == 7,
                       r=[("hT", t), ("wi", k // 2)], w=[("ps", pk)])
                dma("act", rtt, cd["ropetab"][t * 128:(t + 1) * 128, :], w=[("rt", t % 4)])
            st_.append(s_mm)
            st_.append(lambda: op("act", _act(sq[b][:, 0:512], PS[pq], AF.Square), r=[("ps", pq)], w=[("sq", b)]))
            st_.append(lambda: op("act", _act(sq[b][:, 512:640], PS[pk][:, 0:128], AF.Square), r=[("ps", pk)], w=[("sq", b)]))
            st_.append(lambda: op("dve", lambda e: e.tensor_reduce(out=ss[:, t, 0, :], in_=sq[b].rearrange("p (h d) -> p h d", d=64),
                                                                   axis=AX.X, op=ALU.add), r=[("sq", b)], w=[("ss0", t)]))
            st_.append(lambda: op("act", _act(ss[:, t, 1, :], ss[:, t, 0, :], AF.Sqrt, bias=epsc[:, 0:1], scale=1.0 / 64),
                                  r=[("ss0", t)], w=[("ss1", t)]))
            st_.append(lambda: op("act", _act(Va[:, t, :, 0:64], PS[pk][:, 128:256].rearrange("p (h d) -> p h d", h=2), AF.Identity),
                                  r=[("ps", pk), "Va1"], w=[("Va", t)]))
            st_.append(lambda: op("dve", lambda e: e.reciprocal(out=ss[:, t, 2, :], in_=ss[:, t, 1, :]), r=[("ss1", t)], w=[("ss2", t)]))
            st_.append(lambda: op("dve", lambda e: e.tensor_tensor(
                out=qn[b][:, 0:512].rearrange("p (h d) -> p h d", d=64), in0=PS[pq].rearrange("p (h d) -> p h d", d=64),
                in1=ss[:, t, 2, 0:8].unsqueeze(2).to_broadcast([128, 8, 64]), op=ALU.mult),
                r=[("ps", pq), ("ss2", t)], w=[("qn", b)]))
            st_.append(lambda: op("dve", lambda e: e.tensor_tensor(
                out=qn[b][:, 512:640].rearrange("p (h d) -> p h d", d=64), in0=PS[pk][:, 0:128].rearrange("p (h d) -> p h d", d=64),
                in1=ss[:, t, 2, 8:10].unsqueeze(2).to_broadcast([128, 2, 64]), op=ALU.mult),
                r=[("ps", pk), ("ss2", t), ("Va", t)], w=[("qn", b)]))
            st_.append(lambda: op("dve", lambda e: e.tensor_tensor(out=qn[b], in0=qn[b], in1=gqk, op=ALU.mult),
                                  r=[("qn", b), "gqk"], w=[("qn", b)]))
            qv = qn[b].rearrange("p (h d) -> p h d", d=64)
            st_.append(lambda: op("dve", lambda e: e.tensor_tensor(
                out=t1[b].rearrange("p (h d) -> p h d", d=64), in0=qv,
                in1=rtt[:, 0:64].unsqueeze(1).to_broadcast([128, 10, 64]), op=ALU.mult),
                r=[("qn", b), ("rt", t % 4)], w=[("t1", b)]))
            q4 = qn[b].rearrange("p (h a d) -> p h a d", a=2, d=16)
            t24 = t2[b].rearrange("p (h a d) -> p h a d", a=2, d=16)
            s4 = rtt[:, 64:128].rearrange("p (c a d) -> p c a d", a=2, d=16)

            def s_rope():
                for a in range(2):
                    for rc in range(2):
                        op("dve", lambda e, a=a, rc=rc: e.tensor_tensor(
                            out=t24[:, rc:20:2, a, :], in0=q4[:, rc:20:2, 1 - a, :],
                            in1=s4[:, rc, a, :].unsqueeze(1).to_broadcast([128, 10, 16]), op=ALU.mult),
                           r=[("qn", b), ("rt", t % 4)], w=[("t2", b)])
            st_.append(s_rope)

            def s_add():
                for h in range(2):
                    op("dve", lambda e, h=h: e.tensor_tensor(
                        out=qkr[b][:, h * 256:(h + 1) * 256].rearrange("p (e a d) -> p a e d", e=2, a=2),
                        in0=t1[b][:, h * 256:(h + 1) * 256].rearrange("p (a e d) -> p a e d", a=2, e=2),
                        in1=t2[b][:, h * 256:(h + 1) * 256].rearrange("p (a e d) -> p a e d", a=2, e=2), op=ALU.add),
                       r=[("t1", b), ("t2", b)], w=[("qkr", b)])
                for dup in range(2):
                    op("dve", lambda e, dup=dup: e.tensor_tensor(
                        out=qkr[b][:, 512:768].rearrange("p (h u d) -> p h u d", u=2, d=64)[:, :, dup, :],
                        in0=t1[b][:, 512:640].rearrange("p (h d) -> p h d", d=64),
                        in1=t2[b][:, 512:640].rearrange("p (h d) -> p h d", d=64), op=ALU.add),
                       r=[("t1", b), ("t2", b)], w=[("qkr", b)])
            st_.append(s_add)

            def s_tr():
                p1, p2 = pq, pk
                for slot in range(4):
                    src = qkr[b][:, slot * 128:(slot + 1) * 128]
                    op("pe", lambda e, src=src, slot=slot: e.transpose(PS[p1][:, slot * 128:(slot + 1) * 128], src, ident),
                       r=[("qkr", b), "ident"], w=[("ps", p1)])
                for h in range(2):
                    src = qkr[b][:, 512 + h * 128:512 + (h + 1) * 128]
                    op("pe", lambda e, src=src, h=h: e.transpose(PS[p2][:, h * 128:(h + 1) * 128], src, ident),
                       r=[("qkr", b), "ident"], w=[("ps", p2)])
                op("act", _act(QT.rearrange("p h e t -> p (h e) t")[:, :, t * 128:(t + 1) * 128],
                               PS[p1].rearrange("p (s q) -> p s q", q=128), AF.Identity), r=[("ps", p1)], w=[("QT", t)])
                op("dve", lambda e: e.tensor_copy(out=KT[:, :, t * 128:(t + 1) * 128],
                                                  in_=PS[p2][:, 0:256].rearrange("p (s q) -> p s q", q=128)),
                   r=[("ps", p2)], w=[("KT", t)])
            st_.append(s_tr)
            return st_

        pairs = [[winb_steps(t), winb_steps(t + 1)] for t in range(0, NT, 2)]
        for sl in pairs[0]:
            sl[0]()
        for pi, pr in enumerate(pairs):
            if pi + 1 < len(pairs):
                for sl in pairs[pi + 1]:
                    sl[0]()
            run_interleaved([sl[1:] for sl in pr])
        kb.barrier(reset_to=mark_qkv)
        if stop_after == ("winb", l):
            d1 = kb.dram("dbg_QT", [128, 4 * T], BF16, out=True)
            d2 = kb.dram("dbg_KT", [128, 2 * T], BF16, out=True)
            dma("sp", d1, QT.rearrange("p h e t -> p (h e t)"))
            dma("sp", d2, KT.rearrange("p h t -> p (h t)"))
            return None

        PT = [alloc([1024], BF16) for _ in range(4)]
        rden = [alloc([512], F32) for _ in range(2)]
        accs = [alloc([512], F32) for _ in range(2)]
        ao = [alloc([2, 512], BF16) for _ in range(2)]
        Va2 = alloc([NT, 2, 128], BF16)
        op("act", _act(Va2[:, :, :, 0:64], Va[:, :, :, 64:128], AF.Identity), w=["Va2"])
        op("act", _act(Va2[:, :, :, 64:128], Va[:, :, :, 0:64], AF.Identity), w=["Va2"])
        pti = 0
        si = 0
        acci = 0
        PSA = kb.psall
        qgroups = ([] if last else [[0, 1]]) + [[2 + 4 * i + j for j in range(4)] for i in range(8)]
        for h in range(2):
            for gi, qg in enumerate(qgroups):
                aob = ao[gi % 2]
                for qi, qt in enumerate(qg):
                    ktiles = list(range(0, 2)) if qt < 2 else list(range(0, NT))
                    nsup = len(ktiles) // 2
                    pacc, pacc2 = (0, 1)
                    co = 0
                    acci += 1
                    spair = {}

                    def s_op(ki, qt=qt, h=h):
                        kt0 = ktiles[2 * ki]
                        nonlocal si
                        b0 = 2 + 2 * (si % 3)
                        si += 1
                        spair[ki] = b0
                        for rep_ in range(ATT_DUP):
                            for sub in range(2):
                                kt = kt0 + sub
                                for a in (range(2) if rep_ == 0 else range(ATT_DUPA)):
                                    mm(PS[b0 + a][:, sub * 256:(sub + 1) * 256].rearrange("p (e q) -> p e q", e=2),
                                       KT[a * 64:(a + 1) * 64, h, kt * 128:(kt + 1) * 128],
                                       QT[a * 64:(a + 1) * 64, h, :, qt * 128:(qt + 1) * 128], True, True,
                                       r=[("KT", kt), ("QT", qt)], w=[("ps", b0 + a)])

                    s_op(0)
                    if nsup > 1:
                        s_op(1)
                    for ki in range(nsup):
                        if ki + 2 < nsup:
                            s_op(ki + 2)
                        b0 = spair[ki]
                        pt = PT[pti % 4]
                        op("act", _act(pt.rearrange("p (a c) -> p a c", a=2),
                                       PSA[:, b0 * 512:(b0 + 2) * 512].rearrange("p (a c) -> p a c", a=2),
                                       AF.Exp, scale=0.125),
                           r=[("ps", b0), ("ps", b0 + 1)], w=[("PT", pti % 4)])
                        ptv = pt.rearrange("p (a s e q) -> p a s e q", a=2, s=2, e=2)
                        for sub in range(2):
                            kt = ktiles[2 * ki] + sub
                            fst = (ki == 0 and sub == 0)
                            lst = (ki == nsup - 1 and sub == 1)
                            mm(PS[pacc][:, 0:256].rearrange("p (s q) -> p s q", q=128), Va[:, kt, h, :], ptv[:, :, sub, 0, :],
                               fst, lst, r=[("Va", kt), ("PT", pti % 4)], w=[("acc", pacc, co)])
                            mm(PS[pacc2][:, 0:256].rearrange("p (s q) -> p s q", q=128), Va2[:, kt, h, :], ptv[:, :, sub, 1, :],
                               fst, lst, r=["Va2", ("PT", pti % 4)], w=[("acc", pacc2, co)])
                        pti += 1
                    rd = rden[qi % 2]
                    ac = accs[qi % 2]
                    op("dve", lambda e, ac=ac: e.tensor_copy(out=ac[0:64, 0:256], in_=PS[0][0:64, 0:256]),
                       r=[("acc", 0, 0)], w=[("accs", qi % 2)])
                    op("dve", lambda e, ac=ac: e.tensor_copy(out=ac[0:64, 256:512], in_=PS[0][64:128, 0:256]),
                       r=[("acc", 0, 0)], w=[("accs", qi % 2)])
                    op("dve", lambda e, ac=ac: e.tensor_copy(out=ac[64:128, 0:256], in_=PS[1][64:128, 0:256]),
                       r=[("acc", 1, 0)], w=[("accs", qi % 2)])
                    op("dve", lambda e, ac=ac: e.tensor_copy(out=ac[64:128, 256:512], in_=PS[1][0:64, 0:256]),
                       r=[("acc", 1, 0)], w=[("accs", qi % 2)])
                    op("dve", lambda e, rd=rd, ac=ac: e.reciprocal(out=rd[:, 0:256], in_=ac[:, 256:512]),
                       r=[("accs", qi % 2)], w=[("rden", qi % 2)])
                    op("dve", lambda e, rd=rd, ac=ac, aob=aob, qi=qi: e.tensor_tensor(
                        out=aob[:, :, qi * 128:(qi + 1) * 128],
                        in0=ac[:, 0:256].rearrange("p (s q) -> p s q", q=128),
                        in1=rd[:, 0:256].rearrange("p (s q) -> p s q", q=128), op=ALU.mult),
                       r=[("accs", qi % 2), ("rden", qi % 2)], w=[("ao", gi % 2)])
                ncol = 128 * len(qg)
                tok0 = qg[0] * 128
                dma("sp", mixT[h * 256:(h + 1) * 256, tok0:tok0 + ncol].rearrange("(c p) t -> p c t", p=128),
                    aob[:, :, 0:ncol], r=[("ao", gi % 2)], w=[("mixT", "attn", h, gi)])
        kb.barrier(reset_to=layer_top)
        if stop_after == ("attn", l):
            return None

        segs = [("L", L, C)] if last else [("L", L, C), ("C", C, 0)]
        def do_seg(sn, n_in, tok0):
            small = (sn == "C")

            def seg_barrier():
                if not small:
                    kb.barrier(reset_to=layer_top)
            kb.top = layer_top
            HI = n_in // 64
            zT = alloc([n_in], F32)
            w1 = alloc([64], F32)
            w2 = alloc([64], F32)
            w3 = alloc([512], F32)
            frb = alloc([4], F32)
            h1 = alloc([n_in], F32)
            h2 = alloc([n_in], F32)
            hfb = alloc([4, n_in], F32)
            dec = alloc([2, n_in], F32)
            nrm = alloc([2, 8], F32)
            arg = [alloc([512], F32), alloc([512], F32)]
            arg2 = [alloc([512], F32), alloc([512], F32)]
            dma("sp", zT[0:33], cd["zT_" + sn], w=["zT"])
            dma("sp", w1[0:33], hy_w1[l], w=["w1"])
            dma("sp", w2[0:64], hy_w2[l], w=["w2"])
            dma("sp", w3[0:64], hy_w3[l], w=["w3"])
            dma("act", dec, cd["decT_" + sn].rearrange("(c p) t -> p c t", p=128), w=["dec"])
            op("dve", lambda e: e.tensor_tensor(out=frb[0:64, 0:1], in0=pp[0:64, P_hy:P_hy + 1], in1=pp[0:64, P_hy + 1:P_hy + 2], op=ALU.mult),
               r=["pp"], w=["frb"])
            op("dve", lambda e: e.tensor_tensor(out=frb[0:64, 1:2], in0=pp[0:64, P_hy + 2:P_hy + 3], in1=pp[0:64, P_hy + 3:P_hy + 4], op=ALU.mult),
               r=["pp"], w=["frb"])
            ncg = max(1, n_in // 512)
            cw = min(512, n_in)
            for li, (wt, kdim, src_, dst_, frc) in enumerate([(w1, 33, zT, h1, P_hy + 1), (w2, 64, h1, h2, P_hy + 3)]):
                for g in range(ncg):
                    pb = kb.psn()
                    mm(PS[pb][0:64, 0:cw], wt[0:kdim, 0:64], src_[0:kdim, g * cw:(g + 1) * cw], True, True,
                       r=["w1", "w2", "zT", ("h", li, g)], w=[("ps", pb)])
                    ab = arg[g % 2]
                    op("dve", lambda e, ab=ab, pb=pb, frc=frc, li=li: e.tensor_scalar(
                        out=ab[0:64, 0:cw], in0=PS[pb][0:64, 0:cw], scalar1=pp[0:64, frc:frc + 1], scalar2=frb[0:64, li:li + 1],
                        op0=ALU.mult, op1=ALU.add), r=[("ps", pb), "frb", "pp"], w=[("arg", g % 2)])
                    a2 = arg2[g % 2]
                    MAGIC = 12582912.0
                    op("dve", lambda e, ab=ab, a2=a2: e.tensor_scalar(out=a2[0:64, 0:cw], in0=ab[0:64, 0:cw], scalar1=1.0 / TWO_PI,
                                                                      scalar2=MAGIC, op0=ALU.mult, op1=ALU.add),
                       r=[("arg", g % 2)], w=[("arg2", g % 2)])
                    op("dve", lambda e, a2=a2: e.tensor_scalar(out=a2[0:64, 0:cw], in0=a2[0:64, 0:cw], scalar1=-MAGIC,
                                                               scalar2=-TWO_PI, op0=ALU.add, op1=ALU.mult),
                       r=[("arg2", g % 2)], w=[("arg2", g % 2)])
                    op("dve", lambda e, ab=ab, a2=a2: e.tensor_tensor(out=ab[0:64, 0:cw], in0=ab[0:64, 0:cw], in1=a2[0:64, 0:cw], op=ALU.add),
                       r=[("arg", g % 2), ("arg2", g % 2)], w=[("arg", g % 2)])
                    op("dve", lambda e, ab=ab: e.tensor_scalar(out=ab[0:64, 0:cw], in0=ab[0:64, 0:cw], scalar1=-3.14159,
                                                               scalar2=3.14159, op0=ALU.max, op1=ALU.min),
                       r=[("arg", g % 2)], w=[("arg", g % 2)])
                    op("act", _act(dst_[0:64, g * cw:(g + 1) * cw], ab[0:64, 0:cw], AF.Sin),
                       r=[("arg", g % 2)], w=[("h", li + 1, g)])
            for c4 in range(4):
                for g in range(ncg):
                    pb = kb.psn()
                    mm(PS[pb][:, 0:cw], w3[0:64, c4 * 128:(c4 + 1) * 128], h2[0:64, g * cw:(g + 1) * cw], True, True,
                       r=["w3", ("h", 2, g)], w=[("ps", pb)])
                    op("dve", lambda e, c4=c4, g=g, pb=pb: e.tensor_tensor(
                        out=hfb[:, c4, g * cw:(g + 1) * cw], in0=PS[pb][:, 0:cw], in1=dec[:, c4 % 2, g * cw:(g + 1) * cw], op=ALU.mult),
                       r=[("ps", pb), "dec"], w=[("hfb", c4)])
            for c2 in range(2):
                op("dve", lambda e, c2=c2: e.memset(hfb[:, 2 + c2, 0:1], 0.0), r=[("hfb", 2 + c2)], w=[("hfb", 2 + c2)])
            op("dve", lambda e: e.memset(nrm, 0.0), w=[("nrm", 0), ("nrm", 1)])
            for c2 in range(2):
                for q, c4 in enumerate((c2, 2 + c2)):
                    op("act", _act(zT, hfb[:, c4, :], AF.Abs, accum=nrm[:, c2, q:q + 1]),
                       r=[("hfb", c4), ("nrm", c2)], w=["zT", ("nrm", c2)])
                op("dve", lambda e, c2=c2: e.tensor_tensor(out=nrm[:, c2, 2:3], in0=nrm[:, c2, 0:1], in1=nrm[:, c2, 1:2], op=ALU.add),
                   r=[("nrm", c2)], w=[("nrm", c2)])
                op("dve", lambda e, c2=c2: e.reciprocal(out=nrm[:, c2, 3:4], in_=nrm[:, c2, 2:3]), r=[("nrm", c2)], w=[("nrm", c2)])
                op("dve", lambda e, c2=c2: e.tensor_scalar(out=hfb[:, 2 + c2, :], in0=hfb[:, 2 + c2, :], scalar1=nrm[:, c2, 3:4], scalar2=None,
                                                           op0=ALU.mult), r=[("hfb", 2 + c2), ("nrm", c2)], w=[("hfb", 2 + c2)])
                op("dve", lambda e, c2=c2: e.scalar_tensor_tensor(out=h2, in0=hfb[:, c2, :], scalar=nrm[:, c2, 3:4], in1=hfb[:, 2 + c2, :],
                                                                  op0=ALU.mult, op1=ALU.add),
                   r=[("hfb", c2), ("hfb", 2 + c2), ("nrm", c2)], w=["hps"] + [("h", 2, g) for g in range(ncg)])
                dma("sp", hpm[sn][0, c2 * 128:(c2 + 1) * 128, :], h2, r=["hps"], w=[("src", "hp" + sn)])
                op("dve", lambda e, c2=c2: e.scalar_tensor_tensor(out=h1, in0=hfb[:, c2, :], scalar=nrm[:, c2, 3:4], in1=hfb[:, 2 + c2, :],
                                                                  op0=ALU.mult, op1=ALU.subtract),
                   r=[("hfb", c2), ("hfb", 2 + c2), ("nrm", c2)], w=["h1buf"] + [("h", 1, g) for g in range(ncg)])
                dma("act", hpm[sn][1, c2 * 128:(c2 + 1) * 128, :], h1, r=["h1buf"], w=[("src", "hm" + sn)])
            seg_barrier()
            hw = min(512, n_in)
            gbuf = alloc([2, hw], F32)
            ybuf = alloc([2, hw], F32)
            ytmp = alloc([hw], F32)
            run_hf = fft_plan("hf" + sn.lower(), n_in, 2, F32, 1)
            for which in range(2):
                def consume_g(cc, O, Okeys, which=which):
                    dma("sp", Gd[sn][which, cc * 128:(cc + 1) * 128, :], O[:, which, :], r=Okeys, w=[("G", which, cc)])
                run_hf([hpm[sn][which]], consume_g, srckeys=[("src", ("hp" if which == 0 else "hm") + sn)])

            gbufs = [gbuf, alloc([2, hw], F32)]
            ybufs = [ybuf, alloc([2, hw], F32)]
            nblk = n_in // hw
            gcnt = [0]

            def g_load(cc, hh):
                i = gcnt[0]
                gcnt[0] += 1
                fs = slice(hh * hw, (hh + 1) * hw)
                dma("sp", gbufs[i % 2], Gd[sn][:, cc * 128:(cc + 1) * 128, fs].rearrange("w c f -> c w f"),
                    r=[("G", 0, cc), ("G", 1, cc)], w=[("gbuf", i % 2)])
                return i % 2

            def consume_u(cc, O, Okeys):
                nxt = g_load(cc, 0)
                for hh in range(nblk):
                    fs = slice(hh * hw, (hh + 1) * hw)
                    gi_ = nxt
                    if hh + 1 < nblk:
                        nxt = g_load(cc, hh + 1)
                    gb_ = gbufs[gi_]
                    yb_ = ybufs[gi_]
                    gk = ("gbuf", gi_)
                    op("dve", lambda e, fs=fs, gb_=gb_, yb_=yb_: e.tensor_tensor(out=yb_[:, 0, :], in0=O[:, 0, fs], in1=gb_[:, 0, :], op=ALU.mult),
                       r=Okeys + [gk], w=[("yb0", gi_)])
                    op("dve", lambda e, fs=fs, gb_=gb_: e.tensor_tensor(out=ytmp, in0=O[:, 1, fs], in1=gb_[:, 1, :], op=ALU.mult),
                       r=Okeys + [gk], w=["ytmp"])
                    op("dve", lambda e, yb_=yb_: e.tensor_tensor(out=yb_[:, 0, :], in0=yb_[:, 0, :], in1=ytmp, op=ALU.subtract),
                       r=[("yb0", gi_), "ytmp"], w=[("yb0", gi_)])
                    op("dve", lambda e, fs=fs, gb_=gb_, yb_=yb_: e.tensor_tensor(out=yb_[:, 1, :], in0=O[:, 0, fs], in1=gb_[:, 1, :], op=ALU.mult),
                       r=Okeys + [gk], w=[("yb1", gi_)])
                    op("dve", lambda e, fs=fs, gb_=gb_: e.tensor_tensor(out=ytmp, in0=O[:, 1, fs], in1=gb_[:, 0, :], op=ALU.mult),
                       r=Okeys + [gk, ("yb0", gi_)], w=["ytmp"])
                    op("dve", lambda e, yb_=yb_: e.tensor_tensor(out=yb_[:, 1, :], in0=yb_[:, 1, :], in1=ytmp, op=ALU.add),
                       r=[("yb1", gi_), "ytmp"], w=[("yb1", gi_)])
                    dma("act", Yd[sn][:, cc * 128:(cc + 1) * 128, fs].rearrange("w c f -> c w f"), yb_,
                        r=[("yb0", gi_), ("yb1", gi_)], w=[("Ysrc", cc)])
            run_hf([uT[:, tok0:tok0 + n_in]], consume_u)
            seg_barrier()
            ub = alloc([n_in], F32)
            x0b = alloc([n_in], F32)
            hyo = alloc([n_in], BF16)

            ubs = [ub, alloc([n_in], F32)]
            x0bs = [x0b, alloc([n_in], F32)]
            for cc_ in range(2):
                dma("sp", ubs[cc_], uT[cc_ * 128:(cc_ + 1) * 128, tok0:tok0 + n_in], w=[("ub", cc_)])
                dma("sp", x0bs[cc_], x0T[cc_ * 128:(cc_ + 1) * 128, tok0:tok0 + n_in], w=[("x0b", cc_)])

            def consume_y(cc, O, Okeys):
                ub_, x0_ = ubs[cc], x0bs[cc]
                op("dve", lambda e, cc=cc, ub_=ub_: e.scalar_tensor_tensor(out=ub_, in0=ub_, scalar=pp[:, P_hb + cc:P_hb + cc + 1], in1=O[:, 0, :],
                                                                           op0=ALU.mult, op1=ALU.add), r=[("ub", cc), "pp"] + Okeys, w=[("ub", cc)])
                op("dve", lambda e, ub_=ub_, x0_=x0_: e.tensor_tensor(out=hyo, in0=ub_, in1=x0_, op=ALU.mult), r=[("ub", cc), ("x0b", cc)], w=["hyo"])
                dma("sp", mixT[768 + cc * 128:768 + (cc + 1) * 128, tok0:tok0 + n_in], hyo, r=["hyo"], w=[("mixT", "hy", cc, tok0)])
            fft_plan("hi" + sn.lower(), n_in, 1, F32, 2)([Yd[sn][0], Yd[sn][1]], consume_y, srckeys=[("Ysrc", 0), ("Ysrc", 1)])
            seg_barrier()

            wf = alloc([2, 256], F32)
            dcb = alloc([2, 128], F32)
            Wx = alloc([2, 2, 256], BF16)
            dma("sp", wf, w_four[l].rearrange("(c p) n -> p c n", p=128), w=["wf"])
            dma("sp", dcb, cd["dcds"], w=["dcb"])
            fsc = 1.0 if sn == "L" else 4.0
            for o_ in range(2):
                for cc in range(2):
                    pb = kb.psn()
                    mm(PS[pb][:, 0:256], dcb[:, o_, :], wf[:, cc, :], True, True, r=["wf", "dcb"], w=[("ps", pb)])
                    op("act", _act(Wx[:, o_, cc, :], PS[pb][:, 0:256], AF.Identity, scale=fsc), r=[("ps", pb)], w=["Wx"])
            Ok = alloc([2, 2, n_in], BF16)
            fo = alloc([n_in], BF16)

            def consume_f(cc, O, Okeys):
                op("act", _act(Ok[:, cc, :, :], O, AF.Identity), r=Okeys, w=[("Ok", cc)])
            fft_plan("f" + sn.lower(), n_in, 2, BF16, 1)([FT[:, tok0:tok0 + n_in]], consume_f)
            for nchk in range(2):
                for g in range(ncg):
                    pb = kb.psn()
                    i = 0
                    for cc in range(2):
                        for o_ in range(2):
                            mm(PS[pb][:, 0:cw], Wx[:, o_, cc, nchk * 128:(nchk + 1) * 128], Ok[:, cc, o_, g * cw:(g + 1) * cw],
                               i == 0, i == 3, r=["Wx", ("Ok", cc)], w=[("ps", pb)])
                            i += 1
                    op("act", _act(fo[:, g * cw:(g + 1) * cw], PS[pb][:, 0:cw], AF.Identity), r=[("ps", pb)], w=["fo"])
                dma("sp", mixT[512 + nchk * 128:512 + (nchk + 1) * 128, tok0:tok0 + n_in], fo, r=["fo"], w=[("mixT", "f", nchk, tok0)])
            seg_barrier()
            if small:
                kb.barrier(reset_to=layer_top)

        for (sn_, n_, t_) in segs:
            do_seg(sn_, n_, t_)
        if stop_after == ("mix", l):
            return None

        dst_res = res[(2 * l) % 3]
        wo = alloc([8, D], BF16)
        for kk_ in range(4):
            dma("pool", wo[:, 2 * kk_:2 * kk_ + 2, :], w_out[l].rearrange("(k p) f -> p k f", p=128)[:, 2 * kk_:2 * kk_ + 2, :], w=[("wo", kk_)])
        mx = [alloc([8, 512], BF16), alloc([8, 512], BF16)]
        ebufs = ([alloc([1024], F32), alloc([1024], F32)], [alloc([1024], F32), alloc([1024], F32)],
                 alloc([2, 8], F32), [alloc([512], F32), alloc([512], F32)])
        outs = []
        def mx_load(gi):
            t0, n = gl[gi]
            dma("pool", mx[gi % 2][:, :, 0:n], mixT[:, t0:t0 + n].rearrange("(c p) t -> p c t", p=128), w=[("mx", gi % 2)])
        mx_load(0)
        for gi, (t0, n) in enumerate(gl):
            mb = mx[gi % 2]
            if gi + 1 < len(gl):
                mx_load(gi + 1)
            for tp in range(0, n // 128, 2):
                sls = []
                for ti in (tp, tp + 1):
                    t = t0 // 128 + ti
                    pbs = [kb.psn(), kb.psn()]
                    for hb in range(2):
                        for k in range(8):
                            mm(PS[pbs[hb]], mb[:, k, ti * 128:(ti + 1) * 128], wo[:, k, hb * 512:(hb + 1) * 512], k == 0, k == 7,
                               r=[("mx", gi % 2), ("wo", k // 2)], w=[("ps", pbs[hb])])
                    sls.append(epilogue_steps(pbs, t, 0, cur_res, dst_res[t * 128:(t + 1) * 128, :], ebufs))
                run_interleaved(sls)
        kb.barrier(reset_to=layer_top)
        cur_res = dst_res
        if stop_after == ("wout", l):
            return None

        fT = alloc([8, T], BF16)
        mark_f = kb.top
        tiles = list(range(2, NT)) if last else list(range(NT))
        norm_phase(cur_res, 1, fT, tiles)
        kb.barrier(reset_to=mark_f)
        wu = [alloc([2, 8, 128], BF16), alloc([2, 8, 128], BF16)]
        Ug = [[alloc([T + 4], BF16), alloc([T + 4], BF16)], [alloc([T + 4], BF16), alloc([T + 4], BF16)]]
        Cb = [alloc([T + 4], F32), alloc([T + 4], F32)]
        SG = alloc([T + 4], F32)
        gst = [alloc([T + 4], BF16), alloc([T + 4], BF16)]
        for jb_ in range(2):
            for gv_ in range(2):
                op("dve", lambda e, jb_=jb_, gv_=gv_: e.memset(Ug[jb_][gv_], 0.0), w=[("Ug", jb_, gv_)])
        wuv = w_up[l].rearrange("(k p) f -> p k f", p=128)
        W_ = T + 2

        def conv_steps(j):
            jb = j % 2
            cw_ = lambda gv, k: pp[:, P_fcw + (gv * 22 + j) * 3 + k:P_fcw + (gv * 22 + j) * 3 + k + 1]
            cb_ = lambda gv: pp[:, P_fcb + gv * 22 + j:P_fcb + gv * 22 + j + 1]
            st_ = []
            ugk = lambda gv: [("Ug", jb, gv, gi_) for gi_ in range(len(gl))] + [("Ug", jb, gv)]
            for gv in range(2):
                st_.append(lambda gv=gv: op("act", _act(Cb[gv][:, 1:1 + W_], Ug[jb][gv][:, 1:1 + W_], AF.Identity,
                                                        bias=cb_(gv), scale=cw_(gv, 1)),
                                            r=ugk(gv), w=[("Cb", gv)]))
            for gv in range(2):
                for k in (0, 2):
                    st_.append(lambda gv=gv, k=k: op("dve", lambda e: e.scalar_tensor_tensor(
                        out=Cb[gv][:, 1:1 + W_], in0=Ug[jb][gv][:, k:k + W_], scalar=cw_(gv, k), in1=Cb[gv][:, 1:1 + W_],
                        op0=ALU.mult, op1=ALU.add), r=ugk(gv) + [("Cb", gv)], w=[("Cb", gv)]))
            st_.append(lambda: op("act", _act(SG[:, 1:1 + W_], Cb[0][:, 1:1 + W_], AF.Silu), r=[("Cb", 0)], w=["SG"]))
            st_.append(lambda: op("dve", lambda e: e.tensor_tensor(out=gst[jb][:, 1:1 + W_], in0=Cb[1][:, 1:1 + W_], in1=SG[:, 1:1 + W_],
                                                                   op=ALU.mult), r=[("Cb", 1), "SG"], w=[("gst", jb)]))

            def s_out():
                if not last:
                    dma("sp", gT[j * 128:(j + 1) * 128, 0:256], gst[jb][:, 1:257], r=[("gst", jb)], w=[("gT", j, 0)])
                dma("sp", gT[j * 128:(j + 1) * 128, 256:T], gst[jb][:, 259:259 + L], r=[("gst", jb)], w=[("gT", j, 1)])
            st_.append(s_out)
            return st_

        pending = []
        for j in range(22):
            jb = j % 2
            for gv in range(2):
                c0 = gv * DFF + j * 128
                dma("pool", wu[jb][:, gv, :, :], wuv[:, :, c0:c0 + 128], w=[("wu", jb, gv)])
                for gi, (t0, n) in enumerate(gl):
                    pb = kb.psn()
                    for k in range(8):
                        mm(PS[pb][:, 0:n], wu[jb][:, gv, k, :], fT[:, k, t0:t0 + n], k == 0, k == 7,
                           r=[("wu", jb, gv)] + [("hT", t) for t in range(t0 // 128, (t0 + n) // 128)], w=[("ps", pb)])
                    pc0 = padcol(t0)
                    op("act", _act(Ug[jb][gv][:, pc0:pc0 + n], PS[pb][:, 0:n], AF.Identity), r=[("ps", pb), ("Ug", jb, gv)], w=[("Ug", jb, gv, gi)])
                    if pending and gi % 2 == 1:
                        pending.pop(0)()
            while pending:
                pending.pop(0)()
            pending = conv_steps(j)
        while pending:
            pending.pop(0)()
        kb.barrier(reset_to=layer_top)
        wd = alloc([22, D], BF16)
        for kk_ in range(11):
            dma("pool", wd[:, 2 * kk_:2 * kk_ + 2, :], w_down[l].rearrange("(k p) f -> p k f", p=128)[:, 2 * kk_:2 * kk_ + 2, :], w=[("wd", kk_)])
        gb = [alloc([22, 512], BF16), alloc([22, 512], BF16)]
        ebufs = ([alloc([1024], F32), alloc([1024], F32)], [alloc([1024], F32), alloc([1024], F32)],
                 alloc([2, 8], F32), [alloc([512], F32), alloc([512], F32)])
        final_ops = []
        def gb_load(gi):
            t0, n = gl[gi]
            dma("pool", gb[gi % 2][:, :, 0:n], gT[:, t0:t0 + n].rearrange("(c p) t -> p c t", p=128),
                r=[("gT", j, 0) for j in range(22)] + [("gT", j, 1) for j in range(22)], w=[("gb", gi % 2)])
        gb_load(0)
        for gi, (t0, n) in enumerate(gl):
            g_ = gb[gi % 2]
            if gi + 1 < len(gl):
                gb_load(gi + 1)
            for tp in range(0, n // 128, 2):
                sls = []
                for ti in (tp, tp + 1):
                    t = t0 // 128 + ti
                    pbs = [kb.psn(), kb.psn()]
                    for hb in range(2):
                        for k in range(22):
                            mm(PS[pbs[hb]], g_[:, k, ti * 128:(ti + 1) * 128], wd[:, k, hb * 512:(hb + 1) * 512], k == 0, k == 21,
                               r=[("gb", gi % 2), ("wd", k // 2)], w=[("ps", pbs[hb])])
                    if last:
                        dst = yout[(t - 2) * 128:(t - 1) * 128, :]
                    else:
                        dst = res[(2 * l + 1) % 3][t * 128:(t + 1) * 128, :]
                    sls.append(epilogue_steps(pbs, t, 1, cur_res, dst, ebufs))
                run_interleaved(sls)
        kb.barrier(reset_to=layer_top)
        cur_res = res[(2 * l + 1) % 3]

        return cur_res

    for l in range(nlayers):
        cur_res = do_layer(l, cur_res)
        if cur_res is None:
            break
    S.emit(nc, [S.bar])
    return kb


_KB = None


def make_in_maps(inp):
    cs = host_consts()
    rows = pack_rows(inp)
    maps = []
    f = lambda a: np.ascontiguousarray(a, dtype=np.float32)
    shared = {k: f(inp[k]) for k in ["w_mod", "w_in", "w_out", "w_up", "w_down", "w_fourier", "hy_w1", "hy_w2", "hy_w3"]}
    for b in range(8):
        m = {"xb": f(inp["x"][b]), "ctxb": f(inp["ctx"][b]), "pp": pack_pp(inp, b), "rows": rows}
        m.update(shared)
        m.update(cs)
        maps.append(m)
    return maps


def kernel(**inputs):
    global _KB
    inp = {k: np.asarray(v) for k, v in inputs.items()}
    if _KB is None:
        _KB = build()
    maps = make_in_maps(inp)
    r = run_bass_kernel_spmd(_KB.nc, maps, core_ids=list(range(8)))
    return np.stack([np.asarray(r.results[b]["y"], dtype=np.float32) for b in range(8)], axis=0)
```

```python
import math
import contextlib
import numpy as np
import ml_dtypes
import concourse.bass as bass
import concourse.mybir as mybir
from concourse.bass_utils import run_bass_kernel_spmd

F32 = mybir.dt.float32
BF16 = mybir.dt.bfloat16
AF = mybir.ActivationFunctionType
ALU = mybir.AluOpType
AX = mybir.AxisListType

D = 1024
L = 4096
C = 256
T = L + C
NT = T // 128
DFF = 2816
INW = 1792
EPS = 1e-6
PL = 270
NPP = 16 + 2 * PL
TWO_PI = 2.0 * math.pi
ATT_DUP = 1


class _Op:
    __slots__ = ("eng", "fn", "deps", "idx", "dma", "marked", "mark_no", "sem", "val", "qn")


class Sched:
    ENGS = ["pe", "act", "dve", "pool", "sp"]
    KDMA = 8

    def __init__(self):
        self.ops = []
        self.last_w = {}
        self.readers = {}
        self.bar = None
        self.last_eng = {}
        self.dmas_since = []

    def add(self, eng, fn, reads=(), writes=(), dma=False):
        op = _Op()
        op.eng, op.fn, op.dma = eng, fn, dma
        op.idx = len(self.ops)
        op.marked = False
        deps = {}
        for k in reads:
            w = self.last_w.get(k)
            if w is not None:
                deps[w.idx] = w
        for k in writes:
            w = self.last_w.get(k)
            if w is not None:
                deps[w.idx] = w
            for r in self.readers.get(k, ()):
                deps[r.idx] = r
        if eng == "pe" and not dma:
            deps = {i: d for i, d in deps.items() if not (d.eng == "pe" and not d.dma)}
        if self.bar is not None:
            deps[self.bar.idx] = self.bar
        op.deps = list(deps.values())
        for k in reads:
            self.readers.setdefault(k, []).append(op)
        for k in writes:
            self.last_w[k] = op
            self.readers[k] = []
        self.ops.append(op)
        if dma:
            self.dmas_since.append(op)
        else:
            self.last_eng[eng] = op
        return op

    def barrier(self, fn):
        op = _Op()
        op.eng, op.fn, op.dma = "pool", fn, False
        op.idx = len(self.ops)
        op.marked = False
        deps = list(self.last_eng.values()) + list(self.dmas_since)
        if self.bar is not None:
            deps.append(self.bar)
        op.deps = deps
        self.ops.append(op)
        self.bar = op
        self.last_eng = {"pool": op}
        self.dmas_since = []
        self.last_w = {}
        self.readers = {}
        return op

    def emit(self, nc, final_deps):
        ops = self.ops
        for op in ops:
            for d in op.deps:
                d.marked = True
        for d in final_deps:
            d.marked = True
        cnt = {e: 0 for e in self.ENGS}
        qcnt = {e: 0 for e in self.ENGS}
        for op in ops:
            if op.dma:
                op.qn = qcnt[op.eng]
                qcnt[op.eng] += 1
            elif op.marked:
                cnt[op.eng] += 1
                op.mark_no = cnt[op.eng]
        with contextlib.ExitStack() as st:
            csem = {e: st.enter_context(nc.semaphore("c_" + e)) for e in ["pe", "act", "dve", "pool"]}
            dsem = {}
            for e in self.ENGS:
                if qcnt[e] > 0:
                    dsem[e] = [st.enter_context(nc.semaphore("d_%s_%d" % (e, i))) for i in range(self.KDMA)]
            for op in ops:
                if op.dma:
                    op.sem = dsem[op.eng][op.qn % self.KDMA]
                    op.val = 16 * (op.qn // self.KDMA + 1)
                elif op.marked:
                    op.sem = csem[op.eng]
                    op.val = op.mark_no
            block = st.enter_context(nc.Block())
            engobj = {"pe": nc.tensor, "act": nc.scalar, "dve": nc.vector, "pool": nc.gpsimd, "sp": nc.sync}
            KD = self.KDMA

            def make(e):
                def body(_h):
                    seen = {}
                    eo = engobj[e]

                    def wait(sem, val):
                        key = id(sem)
                        if seen.get(key, 0) >= val:
                            return
                        seen[key] = val
                        eo.wait_ge(sem, val)

                    for op in ops:
                        if op.eng != e:
                            continue
                        need = {}
                        for d in op.deps:
                            k = id(d.sem)
                            if k not in need or need[k][1] < d.val:
                                need[k] = (d.sem, d.val)
                        if op.dma and op.qn >= KD:
                            k = id(op.sem)
                            v = op.val - 16
                            if k not in need or need[k][1] < v:
                                need[k] = (op.sem, v)
                        for sem, val in need.values():
                            wait(sem, val)
                        if op.fn is None:
                            continue
                        ins = op.fn(eo)
                        if op.dma:
                            ins.then_inc(op.sem, 16)
                        elif op.marked:
                            ins.then_inc(op.sem, 1)
                    if e == "sp":
                        for d in final_deps:
                            wait(d.sem, d.val)
                return body

            block.tensor(make("pe"))
            block.scalar(make("act"))
            block.vector(make("dve"))
            block.gpsimd(make("pool"))
            block.sync(make("sp"))


def _fft_tables(kind, n_in):
    HI = n_in // 64
    lo = np.arange(64, dtype=np.float64)[:, None]
    hi = np.arange(HI, dtype=np.float64)[:, None]
    out = {}
    if kind == "four":
        NR = HI
        r = np.arange(NR, dtype=np.float64)[None, :]
        al = TWO_PI * r * hi / HI
        out["D0"] = np.concatenate([np.cos(al), -np.sin(al)], axis=1)
        k = np.arange(n_in, dtype=np.float64)[None, :]
        ph = TWO_PI * k * lo / n_in
        out["EA"] = np.stack([np.cos(ph), np.sin(ph)], axis=1)
        out["EB"] = np.stack([np.sin(ph), -np.cos(ph)], axis=1)
    elif kind == "hfwd":
        NR = 2 * HI
        r = np.arange(NR, dtype=np.float64)[None, :]
        al = TWO_PI * (2 * r + 1) * hi * 64 / (4 * n_in)
        out["D0"] = np.concatenate([np.cos(al), -np.sin(al)], axis=1)
        f = np.arange(n_in, dtype=np.float64)[None, :]
        ph = TWO_PI * (2 * f + 1) * lo / (4 * n_in)
        out["EA"] = np.stack([np.cos(ph), np.sin(ph)], axis=1)
        out["EB"] = np.stack([np.sin(ph), -np.cos(ph)], axis=1)
    elif kind == "hinv":
        NR = 2 * HI
        r = np.arange(NR, dtype=np.float64)[None, :]
        be = TWO_PI * hi * r / NR
        out["D0"] = np.concatenate([np.cos(be), np.sin(be)], axis=1)
        out["D1"] = np.concatenate([np.sin(be), -np.cos(be)], axis=1)
        t = np.arange(n_in, dtype=np.float64)[None, :]
        ps = TWO_PI * (2 * lo + 1) * t / (4 * n_in)
        sc = 2.0 / (2 * n_in)
        out["EA"] = (sc * np.cos(ps))[:, None, :]
        out["EB"] = (-sc * np.sin(ps))[:, None, :]
    return {k: np.ascontiguousarray(v, dtype=np.float32) for k, v in out.items()}, NR


_CONST = None


def host_consts():
    global _CONST
    if _CONST is not None:
        return _CONST
    cs = {}
    f32 = np.float32
    inv = (np.float32(10000.0) ** (-(np.arange(0, 32, 2, dtype=f32)) / f32(32))).astype(f32)
    row = np.repeat(np.arange(64, dtype=f32), 64)
    col = np.tile(np.arange(64, dtype=f32), 64)
    ar = (row[:, None] * inv[None, :]).astype(f32).astype(np.float64)
    ac = (col[:, None] * inv[None, :]).astype(f32).astype(np.float64)
    cos64 = np.concatenate([np.cos(ar), np.cos(ar), np.cos(ac), np.cos(ac)], axis=1)
    sin64 = np.concatenate([-np.sin(ar), np.sin(ar), -np.sin(ac), np.sin(ac)], axis=1)
    tab = np.zeros((T, 128), f32)
    tab[:C, :64] = 1.0
    tab[C:, :64] = cos64
    tab[C:, 64:] = sin64
    cs["ropetab"] = tab
    for nm, kind, n in [("fl", "four", L), ("fc", "four", C), ("hfl", "hfwd", L), ("hfc", "hfwd", C),
                        ("hil", "hinv", L), ("hic", "hinv", C)]:
        tb, NR = _fft_tables(kind, n)
        for k, v in tb.items():
            cs[nm + "_" + k] = v.astype(ml_dtypes.bfloat16)
    for nm, n in [("L", L), ("C", C)]:
        t = np.linspace(0.0, 1.0, n, dtype=f32)[:, None]
        bands = np.linspace(1e-4, 15, 16, dtype=f32)
        ang = ((f32(TWO_PI) * np.arange(n, dtype=f32)[:, None]) / f32(n) * bands[None, :]).astype(f32)
        z = np.concatenate([t, np.cos(ang.astype(np.float64)), -np.sin(ang.astype(np.float64))], axis=-1)
        cs["zT_" + nm] = np.ascontiguousarray(z.T, dtype=f32)
        deltas = np.abs(np.linspace(math.log(1e-2) / 1.5, math.log(1e-2) / 0.3, 256, dtype=f32))
        dec = np.exp(-(t * deltas[None, :]).astype(f32).astype(np.float64))
        cs["decT_" + nm] = np.ascontiguousarray(dec.T, dtype=f32)
    j = np.arange(64, dtype=np.float64)
    b = TWO_PI * np.outer(j, j) / 64.0
    blk = np.zeros((128, 2, 128), np.float64)
    for g in range(2):
        blk[g * 64:(g + 1) * 64, 0, g * 64:(g + 1) * 64] = np.cos(b) / 512.0
        blk[g * 64:(g + 1) * 64, 1, g * 64:(g + 1) * 64] = -np.sin(b) / 512.0
    cs["dcds"] = blk.astype(f32)
    _CONST = cs
    return cs


def pack_pp(inp, b):
    pp = np.zeros((128, NPP), np.float32)
    pp[:, 0:16:2] = inp["c"][b].reshape(8, 128).T
    pp[:, 1:16:2] = inp["c_ctx"].reshape(8, 128).T
    for l in range(2):
        o = 16 + l * PL
        pp[:, o:o + 48] = inp["b_mod"][l].reshape(48, 128).T; o += 48
        pp[:, o:o + 8] = inp["g_pre_mix"][l].reshape(8, 128).T; o += 8
        pp[:, o:o + 8] = inp["g_pre_ffn"][l].reshape(8, 128).T; o += 8
        pp[:, o:o + 18] = inp["w_hy_conv"][l].reshape(3, 6, 128).transpose(2, 1, 0).reshape(128, 18); o += 18
        pp[:, o:o + 6] = inp["b_hy_conv"][l].reshape(6, 128).T; o += 6
        pp[:, o:o + 2] = inp["hy_bias"][l].reshape(2, 128).T; o += 2
        pp[:, o:o + 132] = inp["w_ffn_conv"][l].reshape(3, 44, 128).transpose(2, 1, 0).reshape(128, 132); o += 132
        pp[:, o:o + 44] = inp["b_ffn_conv"][l].reshape(44, 128).T; o += 44
        pp[0:64, o] = inp["hy_b1"][l]; pp[0:64, o + 1] = inp["hy_fr1"][l]
        pp[0:64, o + 2] = inp["hy_b2"][l]; pp[0:64, o + 3] = inp["hy_fr2"][l]
    return pp


def pack_rows(inp):
    rows = np.zeros((2, 6, 1024), np.float32)
    for l in range(2):
        rows[l, 0] = inp["b_mod"][l][2048:3072]
        rows[l, 1] = inp["b_mod"][l][5120:6144]
        rows[l, 2] = inp["g_post_mix"][l]
        rows[l, 3] = inp["g_post_ffn"][l]
        rows[l, 4, 0:512] = np.tile(inp["g_q"][l], 8)
        rows[l, 4, 512:640] = np.tile(inp["g_k"][l], 2)
    return rows


class KB:
    AW = 52900

    def __init__(self, dbg=()):
        self.nc = bass.Bass("TRN2", target_bir_lowering=False)
        nc = self.nc
        self.S = Sched()
        self.arena = nc.alloc_sbuf_tensor("arena", [128, self.AW], F32)
        self.top = 0
        self.psall = nc.alloc_psum_tensor("psall", [128, 4096], F32)[:]
        self.ps = [self.psall[:, i * 512:(i + 1) * 512] for i in range(8)]
        self.psi = 0
        self.dbg = set(dbg)
        self.drams = {}
        self.nbar = 0

    def alloc(self, shape, dtype=F32, parts=128):
        n = 1
        for s in shape:
            n *= s
        esz = 4 if dtype == F32 else 2
        words = (n * esz + 3) // 4
        words = (words + 7) // 8 * 8
        w0 = self.top
        self.top += words
        assert self.top <= self.AW, "SBUF arena overflow %d" % self.top
        ap = self.arena[0:parts, w0:w0 + words]
        if dtype != F32:
            ap = ap.bitcast(dtype)
        ap = ap[:, 0:n]
        if len(shape) == 2:
            ap = ap.rearrange("p (a b) -> p a b", a=shape[0])
        elif len(shape) == 3:
            ap = ap.rearrange("p (a b c) -> p a b c", a=shape[0], b=shape[1])
        elif len(shape) == 4:
            ap = ap.rearrange("p (a b c d) -> p a b c d", a=shape[0], b=shape[1], c=shape[2])
        return ap

    def dram_in(self, name, shape, dtype=F32):
        t = self.nc.dram_tensor(name, list(shape), dtype, kind="ExternalInput").ap()
        self.drams[name] = t
        return t

    def dram(self, name, shape, dtype=F32, out=False):
        kind = "ExternalOutput" if (out or name in self.dbg) else "Internal"
        t = self.nc.dram_tensor(name, list(shape), dtype, kind=kind).ap()
        self.drams[name] = t
        return t

    def psn(self):
        i = self.psi
        self.psi = (self.psi + 1) % 8
        return i

    def op(self, eng, fn, r=(), w=()):
        return self.S.add(eng, fn, reads=r, writes=w)

    def dma(self, eng, out, in_, r=(), w=()):
        return self.S.add(eng, lambda e: e.dma_start(out=out, in_=in_), reads=r, writes=w, dma=True)

    def mm(self, out, lhsT, rhs, start, stop, r=(), w=()):
        return self.S.add("pe", lambda e: e.matmul(out, lhsT=lhsT, rhs=rhs, start=start, stop=stop), reads=r, writes=w)

    def barrier(self, reset_to=None):
        d = self.bdummy
        self.S.barrier(lambda e: e.memset(d, 0.0))
        if reset_to is not None:
            self.top = reset_to


def _act(out, in_, func, bias=None, scale=None, accum=None):
    kw = {}
    if bias is not None:
        kw["bias"] = bias
    if scale is not None:
        kw["scale"] = scale
    if accum is not None:
        kw["accum_out"] = accum
    return lambda e: e.activation(out=out, in_=in_, func=func, **kw)


def build(nlayers=2, dbg=(), stop_after=None):
    kb = KB(dbg)
    nc, S = kb.nc, kb.S
    op, dma, mm, alloc = kb.op, kb.dma, kb.mm, kb.alloc
    PS = kb.ps
    cs = host_consts()

    xb = kb.dram_in("xb", [L, D])
    ctxb = kb.dram_in("ctxb", [C, D])
    ppd = kb.dram_in("pp", [128, NPP])
    rowsd = kb.dram_in("rows", [2, 6, 1024])
    w_mod = kb.dram_in("w_mod", [2, D, 6 * D])
    w_in = kb.dram_in("w_in", [2, D, INW])
    w_out = kb.dram_in("w_out", [2, D, D])
    w_up = kb.dram_in("w_up", [2, D, 2 * DFF])
    w_down = kb.dram_in("w_down", [2, DFF, D])
    w_four = kb.dram_in("w_fourier", [2, 256, 256])
    hy_w1 = kb.dram_in("hy_w1", [2, 33, 64])
    hy_w2 = kb.dram_in("hy_w2", [2, 64, 64])
    hy_w3 = kb.dram_in("hy_w3", [2, 64, 512])
    cd = {k: kb.dram_in(k, v.shape, BF16 if v.dtype == ml_dtypes.bfloat16 else F32) for k, v in cs.items()}
    yout = kb.dram("y", [L, D], out=True)
    res = [kb.dram("res%d" % i, [T, D]) for i in range(3)]
    mixT = kb.dram("mixT", [D, T], BF16)
    FT = kb.dram("FT", [256, T])
    uT = kb.dram("uT", [256, T])
    x0T = kb.dram("x0T", [256, T])
    hpm = {"L": kb.dram("hpmL", [2, 256, L]), "C": kb.dram("hpmC", [2, 256, C])}
    Gd = {"L": kb.dram("GL", [2, 256, L]), "C": kb.dram("GC", [2, 256, C])}
    Yd = {"L": kb.dram("YL", [2, 256, L]), "C": kb.dram("YC", [2, 256, C])}
    gT = kb.dram("gT", [DFF, T], BF16)

    ident = alloc([128], F32)
    kb.bdummy = alloc([8], F32)
    pp = alloc([NPP], F32)
    epsc = alloc([4], F32)
    base_top = kb.top

    op("pool", lambda e: e.memset(ident, 0.0), w=["ident"])
    op("pool", lambda e: e.affine_select(out=ident, in_=ident, pattern=[[-1, 128]], compare_op=ALU.not_equal,
                                         fill=1.0, base=0, channel_multiplier=1), r=["ident"], w=["ident"])
    op("pool", lambda e: e.memset(epsc[:, 0:1], EPS), w=["epsc"])
    op("pool", lambda e: e.memset(epsc[:, 1:2], -math.pi), w=["epsc"])
    dma("sp", pp, ppd, w=["pp"])
    kb.barrier()

    def tile_src(srcs, t):
        if srcs == "in":
            return ctxb[t * 128:(t + 1) * 128, :] if t < 2 else xb[(t - 2) * 128:(t - 1) * 128, :]
        return srcs[t * 128:(t + 1) * 128, :]

    groups = [(0, 256)] + [(256 + 512 * i, 512) for i in range(8)]

    def padcol(tok):
        return 1 + tok if tok < 256 else 3 + tok

    cur_res = "in"
    def do_layer(l, cur_res):
        last = (l == 1)
        po = 16 + l * PL
        P_bmod, P_gpm, P_gpf = po, po + 48, po + 56
        P_hcw, P_hcb, P_hb, P_fcw, P_fcb, P_hy = po + 64, po + 82, po + 88, po + 90, po + 222, po + 266
        kb.top = base_top
        modc = alloc([48, 2], F32)
        AB = alloc([2, 2, 2, 8], F32)
        GA = alloc([2, 2, 1024], F32)
        layer_top = kb.top

        sc = alloc([8, 2], BF16)
        rep = alloc([2, 8, 128], BF16)
        ones = alloc([128], F32)
        rowsb = alloc([4, 1024], F32)
        wm = [alloc([8, 1024], BF16), alloc([8, 1024], BF16), alloc([8, 1024], BF16)]
        scf = alloc([8, 2], F32)
        op("act", _act(scf, pp[:, 0:16].rearrange("p (j w) -> p j w", w=2), AF.Silu), r=["pp"], w=["scf"])
        op("dve", lambda e: e.tensor_copy(out=sc, in_=scf), r=["scf"], w=["sc"])
        op("pool", lambda e: e.memset(ones, 1.0), w=["ones"])
        for wh in range(2):
            for j in range(8):
                op("act", _act(rep[:, wh, j, :], ones, AF.Identity, scale=scf[:, j, wh:wh + 1]), r=["scf", "ones"], w=["rep"])
        dma("sp", rowsb.rearrange("p a b -> p (a b)"), rowsd[l, 0:4, :].rearrange("r f -> (r f)").partition_broadcast(128), w=["rowsb"])
        wmv = w_mod[l].rearrange("(k p) f -> p k f", p=128)
        pm = 0
        gab = [1]
        for pc in range(6):
            wb = wm[pc % 3]
            dma("pool", wb, wmv[:, :, pc * 1024:(pc + 1) * 1024], w=[("wm", pc % 3)])
            for f in range(8):
                fc = pc * 8 + f
                for k in range(8):
                    mm(PS[pm][:, 2 * fc:2 * fc + 2], wb[:, k, f * 128:(f + 1) * 128], sc[:, k, :], k == 0, k == 7,
                       r=[("wm", pc % 3), "sc"], w=[("ps", pm)])
            if pc in (2, 5):
                sub = 0 if pc == 2 else 1
                for wh in range(2):
                    for hf in range(2):
                        pb = gab[0]
                        gab[0] = gab[0] % 7 + 1
                        for k in range(8):
                            mm(PS[pb], rep[:, wh, k, :], wb[:, k, hf * 512:(hf + 1) * 512], k == 0, k == 7,
                               r=[("wm", pc % 3), "rep"], w=[("ps", pb)])
                        gsl = GA[:, sub, wh, hf * 512:(hf + 1) * 512]
                        op("dve", lambda e, gsl=gsl, pb=pb, sub=sub, hf=hf: e.tensor_tensor(
                            out=gsl, in0=PS[pb], in1=rowsb[:, sub, hf * 512:(hf + 1) * 512], op=ALU.add),
                           r=[("ps", pb), "rowsb"], w=[("GA", sub, wh, hf)])
                        op("dve", lambda e, gsl=gsl, sub=sub, hf=hf: e.tensor_tensor(
                            out=gsl, in0=gsl, in1=rowsb[:, 2 + sub, hf * 512:(hf + 1) * 512], op=ALU.mult),
                           r=["rowsb", ("GA", sub, wh, hf)], w=[("GA", sub, wh, hf)])
        op("dve", lambda e: e.tensor_tensor(out=modc, in0=PS[pm][:, 0:96].rearrange("p (f w) -> p f w", w=2),
                                            in1=pp[:, P_bmod:P_bmod + 48].unsqueeze(2).to_broadcast([128, 48, 2]),
                                            op=ALU.add), r=[("ps", pm), "pp"], w=["modc"])
        for sub in range(2):
            gcol = P_gpm if sub == 0 else P_gpf
            shc, scc = (0, 8) if sub == 0 else (24, 32)
            for wh in range(2):
                a_ap = AB[:, sub, wh, 0, :]
                b_ap = AB[:, sub, wh, 1, :]
                op("dve", lambda e, a_ap=a_ap, scc=scc, wh=wh: e.tensor_scalar(
                    out=a_ap, in0=modc[:, scc:scc + 8, wh], scalar1=1.0, scalar2=None, op0=ALU.add),
                   r=["modc"], w=[("AB", sub, wh, 0)])
                op("dve", lambda e, a_ap=a_ap, gcol=gcol: e.tensor_tensor(out=a_ap, in0=a_ap, in1=pp[:, gcol:gcol + 8], op=ALU.mult),
                   r=[("AB", sub, wh, 0), "pp"], w=[("AB", sub, wh, 0)])
                op("dve", lambda e, b_ap=b_ap, shc=shc, wh=wh: e.tensor_copy(out=b_ap, in_=modc[:, shc:shc + 8, wh]),
                   r=["modc"], w=[("AB", sub, wh, 1)])
        kb.barrier(reset_to=layer_top)
        if stop_after == ("mod", l):
            return None

        def norm_phase(src, sub, hT, tiles):
            xt = [alloc([1024], F32), alloc([1024], F32), alloc([1024], F32)]
            xs = [alloc([1024], F32), alloc([1024], F32), alloc([1024], F32)]
            junk = alloc([1024], F32)
            st = alloc([34, 4], F32)

            def stage_a(i, t):
                b = i % 3
                dma("sp", xt[b], tile_src(src, t), w=[("xt", b)])
                op("pool", lambda e, t=t: e.memset(st[:, t, 0:1], 0.0), w=[("st", t)])
                op("act", _act(junk, xt[b], AF.Square, accum=st[:, t, 0:1]), r=[("xt", b), ("st", t)], w=["junk", ("st", t)])
                op("act", _act(st[:, t, 1:2], st[:, t, 0:1], AF.Sqrt, bias=epsc[:, 0:1], scale=1.0 / D),
                   r=[("st", t)], w=[("st1", t)])
                op("dve", lambda e, t=t: e.reciprocal(out=st[:, t, 2:3], in_=st[:, t, 1:2]), r=[("st1", t)], w=[("st2", t)])
                op("dve", lambda e, t=t, b=b: e.tensor_scalar(out=xs[b], in0=xt[b], scalar1=st[:, t, 2:3], scalar2=None,
                                                              op0=ALU.mult), r=[("xt", b), ("st2", t)], w=[("xs", b)])

            def stage_b(i, t):
                wh = 1 if t < 2 else 0
                b = i % 3
                for hb in range(2):
                    pb = kb.psn()
                    for jj in range(4):
                        j = hb * 4 + jj
                        op("pe", lambda e, pb=pb, jj=jj, j=j, b=b: e.transpose(PS[pb][:, jj * 128:(jj + 1) * 128],
                                                                               xs[b][:, j * 128:(j + 1) * 128], ident),
                           r=[("xs", b), "ident"], w=[("ps", pb)])
                    for jj in range(4):
                        j = hb * 4 + jj
                        if hb == 0:
                            op("act", _act(hT[:, j, t * 128:(t + 1) * 128], PS[pb][:, jj * 128:(jj + 1) * 128], AF.Identity,
                                           bias=AB[:, sub, wh, 1, j:j + 1], scale=AB[:, sub, wh, 0, j:j + 1]),
                               r=[("ps", pb)], w=[("hT", t, j)])
                        else:
                            op("dve", lambda e, pb=pb, jj=jj, j=j, t=t, wh=wh: e.tensor_scalar(
                                out=hT[:, j, t * 128:(t + 1) * 128], in0=PS[pb][:, jj * 128:(jj + 1) * 128],
                                scalar1=AB[:, sub, wh, 0, j:j + 1], scalar2=AB[:, sub, wh, 1, j:j + 1], op0=ALU.mult, op1=ALU.add),
                               r=[("ps", pb)], w=[("hT", t, j)])

            stage_a(0, tiles[0])
            if len(tiles) > 1:
                stage_a(1, tiles[1])
            for i, t in enumerate(tiles):
                if i + 2 < len(tiles):
                    stage_a(i + 2, tiles[i + 2])
                stage_b(i, t)

        def epilogue_steps(pbs, t, sub, src, dst_ap, bufs):
            xt2, tmp, st2, junk2 = bufs
            wh = 1 if t < 2 else 0
            b = t % 2
            steps = []
            steps.append(lambda: dma("sp", xt2[b], tile_src(src, t), w=[("xt2", b)]))
            steps.append(lambda: op("dve", lambda e: e.memset(st2[:, b, 0:2], 0.0), w=[("st2a", b)]))
            for hb in range(2):
                steps.append(lambda hb=hb: op("act", _act(junk2[b], PS[pbs[hb]], AF.Square, accum=st2[:, b, hb:hb + 1]),
                                              r=[("ps", pbs[hb]), ("st2a", b)], w=[("junk2", b), ("st2a", b)]))
            steps.append(lambda: op("dve", lambda e: e.tensor_tensor(out=st2[:, b, 2:3], in0=st2[:, b, 0:1], in1=st2[:, b, 1:2], op=ALU.add),
                                    r=[("st2a", b)], w=[("st2b", b)]))
            steps.append(lambda: op("act", _act(st2[:, b, 3:4], st2[:, b, 2:3], AF.Sqrt, bias=epsc[:, 0:1], scale=1.0 / D),
                                    r=[("st2b", b)], w=[("st2c", b)]))
            steps.append(lambda: op("dve", lambda e: e.reciprocal(out=st2[:, b, 4:5], in_=st2[:, b, 3:4]), r=[("st2c", b)], w=[("st2d", b)]))
            for hb in range(2):
                steps.append(lambda hb=hb: op("dve", lambda e: e.scalar_tensor_tensor(
                    out=tmp[b][:, hb * 512:(hb + 1) * 512], in0=PS[pbs[hb]], scalar=st2[:, b, 4:5],
                    in1=GA[:, sub, wh, hb * 512:(hb + 1) * 512], op0=ALU.mult, op1=ALU.mult),
                    r=[("ps", pbs[hb]), ("st2d", b)], w=[("tmp", b)]))
            steps.append(lambda: op("dve", lambda e: e.tensor_tensor(out=tmp[b], in0=tmp[b], in1=xt2[b], op=ALU.add),
                                    r=[("tmp", b), ("xt2", b)], w=[("tmp", b)]))
            steps.append(lambda: dma("sp", dst_ap, tmp[b], r=[("tmp", b)], w=[("res", t)]))
            return steps

        def run_interleaved(step_lists):
            n = max(len(sl) for sl in step_lists)
            for k in range(n):
                for sl in step_lists:
                    if k < len(sl):
                        sl[k]()

        def diag_build(dg, col0, j):
            for k in range(3):
                op("dve", lambda e, k=k: e.tensor_scalar(out=dg[:, k, :], in0=ident, scalar1=pp[:, col0 + k:col0 + k + 1],
                                                         scalar2=None, op0=ALU.mult), r=["ident", "pp"], w=[("dg", j)])

        def fft_plan_merged(nm, n_in, NO, out_dt, nsrc):
            HI = n_in // 64
            NR = cd[nm + "_D0"].shape[1] // 2
            NF1 = 2 * NR
            n = n_in // NR
            assert HI == 64
            KS = HI * nsrc
            Dts = alloc([NF1], BF16)
            for i in range(nsrc):
                dma("sp", Dts[i * HI:(i + 1) * HI], cd[nm + "_D%d" % i], w=[("Dt", i)])
            E2 = alloc([NO, n_in], BF16)
            dma("sp", E2[0:64], cd[nm + "_EA"], w=["E2a"])
            dma("act", E2[64:128], cd[nm + "_EB"], w=["E2b"])
            XX = [alloc([128, 64], BF16) for _ in range(2)]
            xcnt = [0]
            Z = alloc([NR, 128], BF16)
            O = alloc([NO, n_in], out_dt)
            cpb = min(128, 512 // NF1)
            rpb = min(NR, 512 // (NO * n))
            Zkeys = [("Z", c0, hh) for c0 in range(0, 128, cpb) for hh in range(2)]
            Okeys = [("O", r0) for r0 in range(0, NR, rpb)]
            cnt = [0]

            def run(srcs, consume, srckeys=(), nchunks=2):
                for cc in range(nchunks):
                    xb = xcnt[0] % 2
                    xcnt[0] += 1
                    X = XX[xb]
                    for i in range(nsrc):
                        for q4 in range(4):
                            dma("pool", X[i * HI:(i + 1) * HI, q4 * 32:(q4 + 1) * 32, :],
                                srcs[i][cc * 128 + q4 * 32:cc * 128 + (q4 + 1) * 32, :].rearrange("c (h q) -> h c q", q=64),
                                r=list(srckeys), w=[("X", xb, i)])
                    for c0 in range(0, 128, cpb):
                        pb = kb.psn()
                        for ch in range(c0, c0 + cpb):
                            mm(PS[pb][0:64, (ch - c0) * NF1:(ch - c0 + 1) * NF1], X[0:KS, ch, :], Dts[0:KS, :],
                               True, True, r=[("X", xb, i) for i in range(nsrc)] + [("Dt", i) for i in range(nsrc)], w=[("ps", pb)])
                        psv = PS[pb][0:64, 0:cpb * NF1].rearrange("p (c h f) -> p h f c", h=2, f=NR)
                        op("act", _act(Z[0:64, :, c0:c0 + cpb], psv[:, 0, :, :], AF.Identity), r=[("ps", pb)], w=[("Z", c0, 0)])
                        src_ap = psv[:, 1, :, :]
                        dst_ap = Z[64:128, :, c0:c0 + cpb]
                        op("dve", lambda e, src_ap=src_ap, dst_ap=dst_ap: e.tensor_copy(out=dst_ap, in_=src_ap),
                           r=[("ps", pb), ("Z", c0, 0)], w=[("Z", c0, 1)])
                    Ov = O.rearrange("p o (j r) -> p o j r", r=NR)
                    for r0 in range(0, NR, rpb):
                        pb = kb.psn()
                        for rr in range(rpb):
                            r_ = r0 + rr
                            first = (r0 == 0 and rr == 0)
                            lastm = (r0 + rpb >= NR and rr == rpb - 1)
                            oap = PS[pb][:, rr * NO * n:(rr + 1) * NO * n].rearrange("p (o j) -> p o j", o=NO)
                            mm(oap, Z[:, r_, :], E2[:, :, r_:n_in:NR], True, True,
                               r=(Zkeys if (first or lastm) else []) + ["E2a", "E2b"], w=[("ps", pb)])
                        src_ap = PS[pb][:, 0:rpb * NO * n].rearrange("p (r o j) -> p o j r", o=NO, j=n)
                        dst_ap = Ov[:, :, :, r0:r0 + rpb]
                        op("act", _act(dst_ap, src_ap, AF.Identity), r=[("ps", pb)], w=[("O", r0)])
                    consume(cc, O, Okeys)
            return run

        def fft_plan_split(nm, n_in, NO, out_dt, nsrc):
            HI = n_in // 64
            NR = cd[nm + "_D0"].shape[1] // 2
            NF1 = 2 * NR
            n = n_in // NR
            Dt = []
            for i in range(nsrc):
                dt_ = alloc([NF1], BF16)
                dma("sp", dt_[0:HI], cd[nm + "_D%d" % i], w=[("Dt", i)])
                Dt.append(dt_)
            EA = alloc([NO, n_in], BF16)
            EB = alloc([NO, n_in], BF16)
            dma("sp", EA[0:64], cd[nm + "_EA"], w=["EA"])
            dma("act", EB[0:64], cd[nm + "_EB"], w=["EB"])
            X = [alloc([128, 64], BF16) for _ in range(nsrc)]
            Z = alloc([NF1, 128], BF16)
            O = alloc([NO, n_in], out_dt)
            cpb = min(128, 512 // NF1)
            rpb = min(NR, 512 // (NO * n))
            Zkeys = [("Z", c0) for c0 in range(0, 128, cpb)]
            Okeys = [("O", r0) for r0 in range(0, NR, rpb)]
            cnt = [0]

            def run(srcs, consume, srckeys=(), nchunks=2):
                for cc in range(nchunks):
                    for i in range(nsrc):
                        for q4 in range(4):
                            dma("pool", X[i][0:HI, q4 * 32:(q4 + 1) * 32, :],
                                srcs[i][cc * 128 + q4 * 32:cc * 128 + (q4 + 1) * 32, :].rearrange("c (h q) -> h c q", q=64),
                                r=list(srckeys), w=[("X", i)])
                    for c0 in range(0, 128, cpb):
                        pb = kb.psn()
                        for ch in range(c0, c0 + cpb):
                            for i in range(nsrc):
                                mm(PS[pb][0:64, (ch - c0) * NF1:(ch - c0 + 1) * NF1], X[i][0:HI, ch, :], Dt[i][0:HI, :],
                                   i == 0, i == nsrc - 1, r=[("X", i), ("Dt", i)], w=[("ps", pb)])
                        src_ap = PS[pb][0:64, 0:cpb * NF1].rearrange("p (c f) -> p f c", f=NF1)
                        dst_ap = Z[0:64, :, c0:c0 + cpb]
                        if cnt[0] % 2 == 0:
                            op("act", _act(dst_ap, src_ap, AF.Identity), r=[("ps", pb)], w=[("Z", c0)])
                        else:
                            op("dve", lambda e, src_ap=src_ap, dst_ap=dst_ap: e.tensor_copy(out=dst_ap, in_=src_ap),
                               r=[("ps", pb)], w=[("Z", c0)])
                        cnt[0] += 1
                    Ov = O.rearrange("p o (j r) -> p o j r", r=NR)
                    for r0 in range(0, NR, rpb):
                        pb = kb.psn()
                        for rr in range(rpb):
                            r_ = r0 + rr
                            first = (r0 == 0 and rr == 0)
                            lastm = (r0 + rpb >= NR and rr == rpb - 1)
                            oap = PS[pb][:, rr * NO * n:(rr + 1) * NO * n].rearrange("p (o j) -> p o j", o=NO)
                            mm(oap, Z[0:64, r_, :], EA[0:64, :, r_:n_in:NR], True, False,
                               r=(Zkeys if first else []) + ["EA"], w=[("ps", pb)])
                            mm(oap, Z[0:64, NR + r_, :], EB[0:64, :, r_:n_in:NR], False, True,
                               r=(Zkeys if lastm else []) + ["EB"], w=[("ps", pb)])
                        src_ap = PS[pb][:, 0:rpb * NO * n].rearrange("p (r o j) -> p o j r", o=NO, j=n)
                        dst_ap = Ov[:, :, :, r0:r0 + rpb]
                        if cnt[0] % 2 == 0:
                            op("act", _act(dst_ap, src_ap, AF.Identity), r=[("ps", pb)], w=[("O", r0)])
                        else:
                            op("dve", lambda e, src_ap=src_ap, dst_ap=dst_ap: e.tensor_copy(out=dst_ap, in_=src_ap),
                               r=[("ps", pb)], w=[("O", r0)])
                        cnt[0] += 1
                    consume(cc, O, Okeys)
            return run

        def fft_plan(nm, n_in, NO, out_dt, nsrc):
            if cd[nm + "_D0"].shape[1] // 2 == 128:
                return fft_plan_merged(nm, n_in, NO, out_dt, nsrc)
            return fft_plan_split(nm, n_in, NO, out_dt, nsrc)

        hT = alloc([8, T], BF16)
        mark_hT = kb.top
        norm_phase(cur_res, 0, hT, list(range(NT)))
        kb.barrier(reset_to=mark_hT)
        if stop_after == ("norm", l):
            dbg_hT = kb.dram("dbg_hT", [128, 8 * T], BF16, out=True)
            dma("sp", dbg_hT, hT.rearrange("p a b -> p (a b)"), r=[])
            return None

        mark_wi = kb.top
        wiA = alloc([8, 1024], BF16)
        for c8_ in range(8):
            dma("pool", wiA[:, :, c8_ * 128:(c8_ + 1) * 128],
                w_in[l].rearrange("(k p) f -> p k f", p=128)[:, :, 768 + c8_ * 128:768 + (c8_ + 1) * 128], w=[("wiA", c8_)])
        stage = [alloc([T], F32), alloc([T], F32)]
        x1T = alloc([2, T], F32)
        Ub = alloc([T + 4], BF16)
        dgs = [alloc([3, 128], BF16), alloc([3, 128], BF16)]
        op("dve", lambda e: e.memset(Ub, 0.0), w=["Ub"])
        gl = groups[1:] if last else groups
        for c8 in range(8):
            stg = stage[c8 % 2]
            col = 768 + c8 * 128
            if c8 >= 2:
                diag_build(dgs[c8 % 2], P_hcw + (c8 - 2) * 3, c8 % 2)
            for gi, (t0, n) in enumerate(gl):
                pb = kb.psn()
                for k in range(8):
                    mm(PS[pb][:, 0:n], wiA[:, k, col - 768:col - 768 + 128], hT[:, k, t0:t0 + n], k == 0, k == 7,
                       r=[("wiA", c8)] + [("hT", t) for t in range(t0 // 128, (t0 + n) // 128)], w=[("ps", pb)])
                if c8 < 2:
                    op("act", _act(stg[:, t0:t0 + n], PS[pb][:, 0:n], AF.Identity), r=[("ps", pb)], w=[("stage", c8 % 2)])
                else:
                    pc0 = padcol(t0)
                    op("act", _act(Ub[:, pc0:pc0 + n], PS[pb][:, 0:n], AF.Identity), r=[("ps", pb)], w=["Ub"])
            if c8 < 2:
                dma("sp", FT[c8 * 128:(c8 + 1) * 128, :], stg, r=[("stage", c8 % 2)], w=[("src", "FT")])
                continue
            hc = c8 - 2
            for gi, (t0, n) in enumerate(gl):
                pb = kb.psn()
                pc0 = padcol(t0)
                for k in range(3):
                    mm(PS[pb][:, 0:n], dgs[c8 % 2][:, k, :], Ub[:, pc0 - 1 + k:pc0 - 1 + k + n], k == 0, k == 2,
                       r=[("dg", c8 % 2), "Ub"], w=[("ps", pb)])
                bcol = pp[:, P_hcb + hc:P_hcb + hc + 1]
                if hc < 2:
                    op("act", _act(stg[:, t0:t0 + n], PS[pb][:, 0:n], AF.Identity, bias=bcol, scale=1.0),
                       r=[("ps", pb)], w=[("stage", c8 % 2)])
                elif hc < 4:
                    op("act", _act(x1T[:, hc - 2, t0:t0 + n], PS[pb][:, 0:n], AF.Identity, bias=bcol, scale=1.0),
                       r=[("ps", pb)], w=[("x1T", hc - 2)])
                else:
                    op("dve", lambda e, stg=stg, pb=pb, t0=t0, n=n, bcol=bcol, hc=hc: e.scalar_tensor_tensor(
                        out=stg[:, t0:t0 + n], in0=PS[pb][:, 0:n], scalar=bcol, in1=x1T[:, hc - 4, t0:t0 + n],
                        op0=ALU.add, op1=ALU.mult), r=[("ps", pb), ("x1T", hc - 4)], w=[("stage", c8 % 2)])
            if hc < 2:
                dma("sp", x0T[hc * 128:(hc + 1) * 128, :], stg, r=[("stage", c8 % 2)], w=[("x0T", hc)])
            elif hc >= 4:
                dma("sp", uT[(hc - 4) * 128:(hc - 3) * 128, :], stg, r=[("stage", c8 % 2)], w=[("src", "uT")])
        kb.barrier(reset_to=mark_wi)
        if stop_after == ("wina", l):
            return None

        QT = alloc([2, 2, T], BF16)
        KT = alloc([2, T], BF16)
        Va = alloc([NT, 2, 128], BF16)
        mark_qkv = kb.top
        wi = alloc([8, 768], BF16)
        for kk_ in range(4):
            dma("pool", wi[:, 2 * kk_:2 * kk_ + 2, :], w_in[l].rearrange("(k p) f -> p k f", p=128)[:, 2 * kk_:2 * kk_ + 2, 0:768], w=[("wi", kk_)])
        gqk = alloc([640], F32)
        dma("sp", gqk, rowsd[l, 4, 0:640].partition_broadcast(128), w=["gqk"])
        op("dve", lambda e: e.memset(Va, 1.0), w=["Va1"])
        rt = [alloc([128], F32) for _ in range(4)]
        sq = [alloc([640], F32), alloc([640], F32)]
        ss = alloc([NT, 3, 10], F32)
        qn = [alloc([640], F32), alloc([640], F32)]
        t1 = [alloc([640], F32), alloc([640], F32)]
        t2 = [alloc([640], F32), alloc([640], F32)]
        qkr = [alloc([768], F32), alloc([768], F32)]
        def winb_steps(t):
            b = t % 2
            rtt = rt[t % 4]
            pq, pk = kb.psn(), kb.psn()
            st_ = []

            def s_mm():
                for k in range(8):
                    mm(PS[pq], hT[:, k, t * 128:(t + 1) * 128], wi[:, k, 0:512], k == 0, k == 7, r=[("hT", t), ("wi", k // 2)], w=[("ps", pq)])
                for k in range(8):
                    mm(PS[pk][:, 0:256], hT[:, k, t * 128:(t + 1) * 128], wi[:, k, 512:768], k == 0, k == 7,
                       r=[("hT", t), ("wi", k // 2)], w=[("ps", pk)])
                dma("act", rtt, cd["ropetab"][t * 128:(t + 1) * 128, :], w=[("rt", t % 4)])
            st_.append(s_mm)
            st_.append(lambda: op("act", _act(sq[b][:, 0:512], PS[pq], AF.Square), r=[("ps", pq)], w=[("sq", b)]))
            st_.append(lambda: op("act", _act(sq[b][:, 512:640], PS[pk][:, 0:128], AF.Square), r=[("ps", pk)], w=[("sq", b)]))
            st_.append(lambda: op("dve", lambda e: e.tensor_reduce(out=ss[:, t, 0, :], in_=sq[b].rearrange("p (h d) -> p h d", d=64),
                                                                   axis=AX.X, op=ALU.add), r=[("sq", b)], w=[("ss0", t)]))
            st_.append(lambda: op("act", _act(ss[:, t, 1, :], ss[:, t, 0, :], AF.Sqrt, bias=epsc[:, 0:1], scale=1.0 / 64),
                                  r=[("ss0", t)], w=[("ss1", t)]))
            st_.append(lambda: op("act", _act(Va[:, t, :, 0:64], PS[pk][:, 128:256].rearrange("p (h d) -> p h d", h=2), AF.Identity),
                                  r=[("ps", pk), "Va1"], w=[("Va", t)]))
            st_.append(lambda: op("dve", lambda e: e.reciprocal(out=ss[:, t, 2, :], in_=ss[:, t, 1, :]), r=[("ss1", t)], w=[("ss2", t)]))
            st_.append(lambda: op("dve", lambda e: e.tensor_tensor(
                out=qn[b][:, 0:512].rearrange("p (h d) -> p h d", d=64), in0=PS[pq].rearrange("p (h d) -> p h d", d=64),
                in1=ss[:, t, 2, 0:8].unsqueeze(2).to_broadcast([128, 8, 64]), op=ALU.mult),
                r=[("ps", pq), ("ss2", t)], w=[("qn", b)]))
            st_.append(lambda: op("dve", lambda e: e.tensor_tensor(
                out=qn[b][:, 512:640].rearrange("p (h d) -> p h d", d=64), in0=PS[pk][:, 0:128].rearrange("p (h d) -> p h d", d=64),
                in1=ss[:, t, 2, 8:10].unsqueeze(2).to_broadcast([128, 2, 64]), op=ALU.mult),
                r=[("ps", pk), ("ss2", t), ("Va", t)], w=[("qn", b)]))
            st_.append(lambda: op("dve", lambda e: e.tensor_tensor(out=qn[b], in0=qn[b], in1=gqk, op=ALU.mult),
                                  r=[("qn", b), "gqk"], w=[("qn", b)]))
            qv = qn[b].rearrange("p (h d) -> p h d", d=64)
            st_.append(lambda: op("dve", lambda e: e.tensor_tensor(
                out=t1[b].rearrange("p (h d) -> p h d", d=64), in0=qv,
                in1=rtt[:, 0:64].unsqueeze(1).to_broadcast([128, 10, 64]), op=ALU.mult),
                r=[("qn", b), ("rt", t % 4)], w=[("t1", b)]))
            q4 = qn[b].rearrange("p (h a d) -> p h a d", a=2, d=16)
            t24 = t2[b].rearrange("p (h a d) -> p h a d", a=2, d=16)
            s4 = rtt[:, 64:128].rearrange("p (c a d) -> p c a d", a=2, d=16)

            def s_rope():
                q5 = qn[b].rearrange("p (h c a d) -> p h c a d", c=2, a=2, d=16)
                t25 = t2[b].rearrange("p (h c a d) -> p h c a d", c=2, a=2, d=16)
                for a in range(2):
                    op("dve", lambda e, a=a: e.tensor_tensor(
                        out=t25[:, :, :, a, :], in0=q5[:, :, :, 1 - a, :],
                        in1=s4[:, :, a, :].unsqueeze(1).to_broadcast([128, 10, 2, 16]), op=ALU.mult),
                       r=[("qn", b), ("rt", t % 4)], w=[("t2", b)])
            st_.append(s_rope)

            def s_add():
                for h in range(2):
                    op("dve", lambda e, h=h: e.tensor_tensor(
                        out=qkr[b][:, h * 256:(h + 1) * 256].rearrange("p (e a d) -> p a e d", e=2, a=2),
                        in0=t1[b][:, h * 256:(h + 1) * 256].rearrange("p (a e d) -> p a e d", a=2, e=2),
                        in1=t2[b][:, h * 256:(h + 1) * 256].rearrange("p (a e d) -> p a e d", a=2, e=2), op=ALU.add),
                       r=[("t1", b), ("t2", b)], w=[("qkr", b)])
                for dup in range(2):
                    op("dve", lambda e, dup=dup: e.tensor_tensor(
                        out=qkr[b][:, 512:768].rearrange("p (h u d) -> p h u d", u=2, d=64)[:, :, dup, :],
                        in0=t1[b][:, 512:640].rearrange("p (h d) -> p h d", d=64),
                        in1=t2[b][:, 512:640].rearrange("p (h d) -> p h d", d=64), op=ALU.add),
                       r=[("t1", b), ("t2", b)], w=[("qkr", b)])
            st_.append(s_add)

            def s_tr():
                p1, p2 = pq, pk
                for slot in range(4):
                    src = qkr[b][:, slot * 128:(slot + 1) * 128]
                    op("pe", lambda e, src=src, slot=slot: e.transpose(PS[p1][:, slot * 128:(slot + 1) * 128], src, ident),
                       r=[("qkr", b), "ident"], w=[("ps", p1)])
                for h in range(2):
                    src = qkr[b][:, 512 + h * 128:512 + (h + 1) * 128]
                    op("pe", lambda e, src=src, h=h: e.transpose(PS[p2][:, h * 128:(h + 1) * 128], src, ident),
                       r=[("qkr", b), "ident"], w=[("ps", p2)])
                op("act", _act(QT.rearrange("p h e t -> p (h e) t")[:, :, t * 128:(t + 1) * 128],
                               PS[p1].rearrange("p (s q) -> p s q", q=128), AF.Identity), r=[("ps", p1)], w=[("QT", t)])
                op("act", _act(KT[:, :, t * 128:(t + 1) * 128], PS[p2][:, 0:256].rearrange("p (s q) -> p s q", q=128), AF.Identity),
                   r=[("ps", p2)], w=[("KT", t)])
            st_.append(s_tr)
            return st_

        pairs = [[winb_steps(t), winb_steps(t + 1)] for t in range(0, NT, 2)]
        for sl in pairs[0]:
            sl[0]()
        for pi, pr in enumerate(pairs):
            if pi + 1 < len(pairs):
                for sl in pairs[pi + 1]:
                    sl[0]()
            run_interleaved([sl[1:] for sl in pr])
        kb.barrier(reset_to=mark_qkv)
        if stop_after == ("winb", l):
            d1 = kb.dram("dbg_QT", [128, 4 * T], BF16, out=True)
            d2 = kb.dram("dbg_KT", [128, 2 * T], BF16, out=True)
            dma("sp", d1, QT.rearrange("p h e t -> p (h e t)"))
            dma("sp", d2, KT.rearrange("p h t -> p (h t)"))
            return None

        PT = [alloc([1024], BF16) for _ in range(4)]
        rden = [alloc([512], F32) for _ in range(2)]
        accs = [alloc([512], F32) for _ in range(2)]
        ao = [alloc([2, 512], BF16) for _ in range(2)]
        Va2 = alloc([NT, 2, 128], BF16)
        op("act", _act(Va2[:, :, :, 0:64], Va[:, :, :, 64:128], AF.Identity), w=["Va2"])
        op("act", _act(Va2[:, :, :, 64:128], Va[:, :, :, 0:64], AF.Identity), w=["Va2"])
        pti = 0
        si = 0
        acci = 0
        PSA = kb.psall
        qgroups = ([] if last else [[0, 1]]) + [[2 + 4 * i + j for j in range(4)] for i in range(8)]
        for h in range(2):
            for gi, qg in enumerate(qgroups):
                aob = ao[gi % 2]
                for qi, qt in enumerate(qg):
                    ktiles = list(range(0, 2)) if qt < 2 else list(range(0, NT))
                    nsup = len(ktiles) // 2
                    pacc, pacc2 = (0, 1)
                    co = 0
                    acci += 1
                    spair = {}

                    def s_op(ki, qt=qt, h=h):
                        kt0 = ktiles[2 * ki]
                        nonlocal si
                        b0 = 2 + 2 * (si % 3)
                        si += 1
                        spair[ki] = b0
                        for rep_ in range(ATT_DUP):
                            for sub in range(2):
                                kt = kt0 + sub
                                for a in (range(2) if rep_ == 0 else range(ATT_DUPA)):
                                    mm(PS[b0 + a][:, sub * 256:(sub + 1) * 256].rearrange("p (e q) -> p e q", e=2),
                                       KT[a * 64:(a + 1) * 64, h, kt * 128:(kt + 1) * 128],
                                       QT[a * 64:(a + 1) * 64, h, :, qt * 128:(qt + 1) * 128], True, True,
                                       r=[("KT", kt), ("QT", qt)], w=[("ps", b0 + a)])

                    s_op(0)
                    if nsup > 1:
                        s_op(1)
                    for ki in range(nsup):
                        if ki + 2 < nsup:
                            s_op(ki + 2)
                        b0 = spair[ki]
                        pt = PT[pti % 4]
                        op("act", _act(pt.rearrange("p (a c) -> p a c", a=2),
                                       PSA[:, b0 * 512:(b0 + 2) * 512].rearrange("p (a c) -> p a c", a=2),
                                       AF.Exp, scale=0.125),
                           r=[("ps", b0), ("ps", b0 + 1)], w=[("PT", pti % 4)])
                        ptv = pt.rearrange("p (a s e q) -> p a s e q", a=2, s=2, e=2)
                        for sub in range(2):
                            kt = ktiles[2 * ki] + sub
                            fst = (ki == 0 and sub == 0)
                            lst = (ki == nsup - 1 and sub == 1)
                            mm(PS[pacc][:, 0:256].rearrange("p (s q) -> p s q", q=128), Va[:, kt, h, :], ptv[:, :, sub, 0, :],
                               fst, lst, r=[("Va", kt), ("PT", pti % 4)], w=[("acc", pacc, co)])
                            mm(PS[pacc2][:, 0:256].rearrange("p (s q) -> p s q", q=128), Va2[:, kt, h, :], ptv[:, :, sub, 1, :],
                               fst, lst, r=["Va2", ("PT", pti % 4)], w=[("acc", pacc2, co)])
                        pti += 1
                    rd = rden[qi % 2]
                    ac = accs[qi % 2]
                    op("dve", lambda e, ac=ac: e.tensor_copy(out=ac[0:64, 0:256], in_=PS[0][0:64, 0:256]),
                       r=[("acc", 0, 0)], w=[("accs", qi % 2)])
                    op("dve", lambda e, ac=ac: e.tensor_copy(out=ac[0:64, 256:512], in_=PS[0][64:128, 0:256]),
                       r=[("acc", 0, 0)], w=[("accs", qi % 2)])
                    op("dve", lambda e, ac=ac: e.tensor_copy(out=ac[64:128, 0:256], in_=PS[1][64:128, 0:256]),
                       r=[("acc", 1, 0)], w=[("accs", qi % 2)])
                    op("dve", lambda e, ac=ac: e.tensor_copy(out=ac[64:128, 256:512], in_=PS[1][0:64, 0:256]),
                       r=[("acc", 1, 0)], w=[("accs", qi % 2)])
                    op("dve", lambda e, rd=rd, ac=ac: e.reciprocal(out=rd[:, 0:256], in_=ac[:, 256:512]),
                       r=[("accs", qi % 2)], w=[("rden", qi % 2)])
                    op("dve", lambda e, rd=rd, ac=ac, aob=aob, qi=qi: e.tensor_tensor(
                        out=aob[:, :, qi * 128:(qi + 1) * 128],
                        in0=ac[:, 0:256].rearrange("p (s q) -> p s q", q=128),
                        in1=rd[:, 0:256].rearrange("p (s q) -> p s q", q=128), op=ALU.mult),
                       r=[("accs", qi % 2), ("rden", qi % 2)], w=[("ao", gi % 2)])
                ncol = 128 * len(qg)
                tok0 = qg[0] * 128
                dma("sp", mixT[h * 256:(h + 1) * 256, tok0:tok0 + ncol].rearrange("(c p) t -> p c t", p=128),
                    aob[:, :, 0:ncol], r=[("ao", gi % 2)], w=[("mixT", "attn", h, gi)])
        kb.barrier(reset_to=layer_top)
        if stop_after == ("attn", l):
            return None

        segs = [("L", L, C)] if last else [("L", L, C), ("C", C, 0)]
        def do_seg(sn, n_in, tok0):
            small = (sn == "C")

            def seg_barrier():
                if not small:
                    kb.barrier(reset_to=layer_top)
            kb.top = layer_top
            HI = n_in // 64
            zT = alloc([n_in], F32)
            w1 = alloc([64], F32)
            w2 = alloc([64], F32)
            w3 = alloc([512], F32)
            frb = alloc([4], F32)
            h1 = alloc([n_in], F32)
            h2 = alloc([n_in], F32)
            hfb = alloc([4, n_in], F32)
            dec = alloc([2, n_in], F32)
            nrm = alloc([2, 8], F32)
            arg = [alloc([512], F32), alloc([512], F32)]
            arg2 = [alloc([512], F32), alloc([512], F32)]
            dma("sp", zT[0:33], cd["zT_" + sn], w=["zT"])
            dma("sp", w1[0:33], hy_w1[l], w=["w1"])
            dma("sp", w2[0:64], hy_w2[l], w=["w2"])
            dma("sp", w3[0:64], hy_w3[l], w=["w3"])
            dma("act", dec, cd["decT_" + sn].rearrange("(c p) t -> p c t", p=128), w=["dec"])
            op("dve", lambda e: e.tensor_tensor(out=frb[0:64, 0:1], in0=pp[0:64, P_hy:P_hy + 1], in1=pp[0:64, P_hy + 1:P_hy + 2], op=ALU.mult),
               r=["pp"], w=["frb"])
            op("dve", lambda e: e.tensor_tensor(out=frb[0:64, 1:2], in0=pp[0:64, P_hy + 2:P_hy + 3], in1=pp[0:64, P_hy + 3:P_hy + 4], op=ALU.mult),
               r=["pp"], w=["frb"])
            ncg = max(1, n_in // 512)
            cw = min(512, n_in)
            for li, (wt, kdim, src_, dst_, frc) in enumerate([(w1, 33, zT, h1, P_hy + 1), (w2, 64, h1, h2, P_hy + 3)]):
                for g in range(ncg):
                    pb = kb.psn()
                    mm(PS[pb][0:64, 0:cw], wt[0:kdim, 0:64], src_[0:kdim, g * cw:(g + 1) * cw], True, True,
                       r=["w1", "w2", "zT", ("h", li, g)], w=[("ps", pb)])
                    ab = arg[g % 2]
                    op("dve", lambda e, ab=ab, pb=pb, frc=frc, li=li: e.tensor_scalar(
                        out=ab[0:64, 0:cw], in0=PS[pb][0:64, 0:cw], scalar1=pp[0:64, frc:frc + 1], scalar2=frb[0:64, li:li + 1],
                        op0=ALU.mult, op1=ALU.add), r=[("ps", pb), "frb", "pp"], w=[("arg", g % 2)])
                    a2 = arg2[g % 2]
                    MAGIC = 12582912.0
                    op("dve", lambda e, ab=ab, a2=a2: e.tensor_scalar(out=a2[0:64, 0:cw], in0=ab[0:64, 0:cw], scalar1=1.0 / TWO_PI,
                                                                      scalar2=MAGIC, op0=ALU.mult, op1=ALU.add),
                       r=[("arg", g % 2)], w=[("arg2", g % 2)])
                    op("dve", lambda e, a2=a2: e.tensor_scalar(out=a2[0:64, 0:cw], in0=a2[0:64, 0:cw], scalar1=-MAGIC,
                                                               scalar2=-TWO_PI, op0=ALU.add, op1=ALU.mult),
                       r=[("arg2", g % 2)], w=[("arg2", g % 2)])
                    op("dve", lambda e, ab=ab, a2=a2: e.tensor_tensor(out=ab[0:64, 0:cw], in0=ab[0:64, 0:cw], in1=a2[0:64, 0:cw], op=ALU.add),
                       r=[("arg", g % 2), ("arg2", g % 2)], w=[("arg", g % 2)])
                    op("dve", lambda e, ab=ab: e.tensor_scalar(out=ab[0:64, 0:cw], in0=ab[0:64, 0:cw], scalar1=-3.14159,
                                                               scalar2=3.14159, op0=ALU.max, op1=ALU.min),
                       r=[("arg", g % 2)], w=[("arg", g % 2)])
                    op("act", _act(dst_[0:64, g * cw:(g + 1) * cw], ab[0:64, 0:cw], AF.Sin),
                       r=[("arg", g % 2)], w=[("h", li + 1, g)])
            for c4 in range(4):
                for g in range(ncg):
                    pb = kb.psn()
                    mm(PS[pb][:, 0:cw], w3[0:64, c4 * 128:(c4 + 1) * 128], h2[0:64, g * cw:(g + 1) * cw], True, True,
                       r=["w3", ("h", 2, g)], w=[("ps", pb)])
                    op("dve", lambda e, c4=c4, g=g, pb=pb: e.tensor_tensor(
                        out=hfb[:, c4, g * cw:(g + 1) * cw], in0=PS[pb][:, 0:cw], in1=dec[:, c4 % 2, g * cw:(g + 1) * cw], op=ALU.mult),
                       r=[("ps", pb), "dec"], w=[("hfb", c4)])
            for c2 in range(2):
                op("dve", lambda e, c2=c2: e.memset(hfb[:, 2 + c2, 0:1], 0.0), r=[("hfb", 2 + c2)], w=[("hfb", 2 + c2)])
            op("dve", lambda e: e.memset(nrm, 0.0), w=[("nrm", 0), ("nrm", 1)])
            for c2 in range(2):
                for q, c4 in enumerate((c2, 2 + c2)):
                    op("act", _act(zT, hfb[:, c4, :], AF.Abs, accum=nrm[:, c2, q:q + 1]),
                       r=[("hfb", c4), ("nrm", c2)], w=["zT", ("nrm", c2)])
                op("dve", lambda e, c2=c2: e.tensor_tensor(out=nrm[:, c2, 2:3], in0=nrm[:, c2, 0:1], in1=nrm[:, c2, 1:2], op=ALU.add),
                   r=[("nrm", c2)], w=[("nrm", c2)])
                op("dve", lambda e, c2=c2: e.reciprocal(out=nrm[:, c2, 3:4], in_=nrm[:, c2, 2:3]), r=[("nrm", c2)], w=[("nrm", c2)])
                op("dve", lambda e, c2=c2: e.tensor_scalar(out=hfb[:, 2 + c2, :], in0=hfb[:, 2 + c2, :], scalar1=nrm[:, c2, 3:4], scalar2=None,
                                                           op0=ALU.mult), r=[("hfb", 2 + c2), ("nrm", c2)], w=[("hfb", 2 + c2)])
                op("dve", lambda e, c2=c2: e.scalar_tensor_tensor(out=h2, in0=hfb[:, c2, :], scalar=nrm[:, c2, 3:4], in1=hfb[:, 2 + c2, :],
                                                                  op0=ALU.mult, op1=ALU.add),
                   r=[("hfb", c2), ("hfb", 2 + c2), ("nrm", c2)], w=["hps"] + [("h", 2, g) for g in range(ncg)])
                dma("sp", hpm[sn][0, c2 * 128:(c2 + 1) * 128, :], h2, r=["hps"], w=[("src", "hp" + sn)])
                op("dve", lambda e, c2=c2: e.scalar_tensor_tensor(out=h1, in0=hfb[:, c2, :], scalar=nrm[:, c2, 3:4], in1=hfb[:, 2 + c2, :],
                                                                  op0=ALU.mult, op1=ALU.subtract),
                   r=[("hfb", c2), ("hfb", 2 + c2), ("nrm", c2)], w=["h1buf"] + [("h", 1, g) for g in range(ncg)])
                dma("act", hpm[sn][1, c2 * 128:(c2 + 1) * 128, :], h1, r=["h1buf"], w=[("src", "hm" + sn)])
            seg_barrier()
            hw = min(512, n_in)
            gbuf = alloc([2, hw], F32)
            ybuf = alloc([2, hw], F32)
            ytmp = alloc([hw], F32)
            run_hf = fft_plan("hf" + sn.lower(), n_in, 2, F32, 1)
            for which in range(2):
                def consume_g(cc, O, Okeys, which=which):
                    dma("sp", Gd[sn][which, cc * 128:(cc + 1) * 128, :], O[:, which, :], r=Okeys, w=[("G", which, cc)])
                run_hf([hpm[sn][which]], consume_g, srckeys=[("src", ("hp" if which == 0 else "hm") + sn)])

            gbufs = [gbuf, alloc([2, hw], F32)]
            ybufs = [ybuf, alloc([2, hw], F32)]
            nblk = n_in // hw
            gcnt = [0]

            def g_load(cc, hh):
                i = gcnt[0]
                gcnt[0] += 1
                fs = slice(hh * hw, (hh + 1) * hw)
                dma("sp", gbufs[i % 2], Gd[sn][:, cc * 128:(cc + 1) * 128, fs].rearrange("w c f -> c w f"),
                    r=[("G", 0, cc), ("G", 1, cc)], w=[("gbuf", i % 2)])
                return i % 2

            def consume_u(cc, O, Okeys):
                nxt = g_load(cc, 0)
                for hh in range(nblk):
                    fs = slice(hh * hw, (hh + 1) * hw)
                    gi_ = nxt
                    if hh + 1 < nblk:
                        nxt = g_load(cc, hh + 1)
                    gb_ = gbufs[gi_]
                    yb_ = ybufs[gi_]
                    gk = ("gbuf", gi_)
                    op("dve", lambda e, fs=fs, gb_=gb_, yb_=yb_: e.tensor_tensor(out=yb_[:, 0, :], in0=O[:, 0, fs], in1=gb_[:, 0, :], op=ALU.mult),
                       r=Okeys + [gk], w=[("yb0", gi_)])
                    op("dve", lambda e, fs=fs, gb_=gb_: e.tensor_tensor(out=ytmp, in0=O[:, 1, fs], in1=gb_[:, 1, :], op=ALU.mult),
                       r=Okeys + [gk], w=["ytmp"])
                    op("dve", lambda e, yb_=yb_: e.tensor_tensor(out=yb_[:, 0, :], in0=yb_[:, 0, :], in1=ytmp, op=ALU.subtract),
                       r=[("yb0", gi_), "ytmp"], w=[("yb0", gi_)])
                    op("dve", lambda e, fs=fs, gb_=gb_, yb_=yb_: e.tensor_tensor(out=yb_[:, 1, :], in0=O[:, 0, fs], in1=gb_[:, 1, :], op=ALU.mult),
                       r=Okeys + [gk], w=[("yb1", gi_)])
                    op("dve", lambda e, fs=fs, gb_=gb_: e.tensor_tensor(out=ytmp, in0=O[:, 1, fs], in1=gb_[:, 0, :], op=ALU.mult),
                       r=Okeys + [gk, ("yb0", gi_)], w=["ytmp"])
                    op("dve", lambda e, yb_=yb_: e.tensor_tensor(out=yb_[:, 1, :], in0=yb_[:, 1, :], in1=ytmp, op=ALU.add),
                       r=[("yb1", gi_), "ytmp"], w=[("yb1", gi_)])
                    dma("act", Yd[sn][:, cc * 128:(cc + 1) * 128, fs].rearrange("w c f -> c w f"), yb_,
                        r=[("yb0", gi_), ("yb1", gi_)], w=[("Ysrc", cc)])
            run_hf([uT[:, tok0:tok0 + n_in]], consume_u)
            seg_barrier()
            ub = alloc([n_in], F32)
            x0b = alloc([n_in], F32)
            hyo = alloc([n_in], BF16)

            ubs = [ub, alloc([n_in], F32)]
            x0bs = [x0b, alloc([n_in], F32)]
            for cc_ in range(2):
                dma("sp", ubs[cc_], uT[cc_ * 128:(cc_ + 1) * 128, tok0:tok0 + n_in], w=[("ub", cc_)])
                dma("sp", x0bs[cc_], x0T[cc_ * 128:(cc_ + 1) * 128, tok0:tok0 + n_in], w=[("x0b", cc_)])

            def consume_y(cc, O, Okeys):
                ub_, x0_ = ubs[cc], x0bs[cc]
                op("dve", lambda e, cc=cc, ub_=ub_: e.scalar_tensor_tensor(out=ub_, in0=ub_, scalar=pp[:, P_hb + cc:P_hb + cc + 1], in1=O[:, 0, :],
                                                                           op0=ALU.mult, op1=ALU.add), r=[("ub", cc), "pp"] + Okeys, w=[("ub", cc)])
                op("dve", lambda e, ub_=ub_, x0_=x0_: e.tensor_tensor(out=hyo, in0=ub_, in1=x0_, op=ALU.mult), r=[("ub", cc), ("x0b", cc)], w=["hyo"])
                dma("sp", mixT[768 + cc * 128:768 + (cc + 1) * 128, tok0:tok0 + n_in], hyo, r=["hyo"], w=[("mixT", "hy", cc, tok0)])
            fft_plan("hi" + sn.lower(), n_in, 1, F32, 2)([Yd[sn][0], Yd[sn][1]], consume_y, srckeys=[("Ysrc", 0), ("Ysrc", 1)])
            seg_barrier()

            wf = alloc([2, 256], F32)
            dcb = alloc([2, 128], F32)
            Wx = alloc([2, 2, 256], BF16)
            dma("sp", wf, w_four[l].rearrange("(c p) n -> p c n", p=128), w=["wf"])
            dma("sp", dcb, cd["dcds"], w=["dcb"])
            fsc = 1.0 if sn == "L" else 4.0
            for o_ in range(2):
                for cc in range(2):
                    pb = kb.psn()
                    mm(PS[pb][:, 0:256], dcb[:, o_, :], wf[:, cc, :], True, True, r=["wf", "dcb"], w=[("ps", pb)])
                    op("act", _act(Wx[:, o_, cc, :], PS[pb][:, 0:256], AF.Identity, scale=fsc), r=[("ps", pb)], w=["Wx"])
            Ok = alloc([2, 2, n_in], BF16)
            fo = alloc([n_in], BF16)

            def consume_f(cc, O, Okeys):
                op("act", _act(Ok[:, cc, :, :], O, AF.Identity), r=Okeys, w=[("Ok", cc)])
            fft_plan("f" + sn.lower(), n_in, 2, BF16, 1)([FT[:, tok0:tok0 + n_in]], consume_f)
            for nchk in range(2):
                for g in range(ncg):
                    pb = kb.psn()
                    i = 0
                    for cc in range(2):
                        for o_ in range(2):
                            mm(PS[pb][:, 0:cw], Wx[:, o_, cc, nchk * 128:(nchk + 1) * 128], Ok[:, cc, o_, g * cw:(g + 1) * cw],
                               i == 0, i == 3, r=["Wx", ("Ok", cc)], w=[("ps", pb)])
                            i += 1
                    op("act", _act(fo[:, g * cw:(g + 1) * cw], PS[pb][:, 0:cw], AF.Identity), r=[("ps", pb)], w=["fo"])
                dma("sp", mixT[512 + nchk * 128:512 + (nchk + 1) * 128, tok0:tok0 + n_in], fo, r=["fo"], w=[("mixT", "f", nchk, tok0)])
            seg_barrier()
            if small:
                kb.barrier(reset_to=layer_top)

        for (sn_, n_, t_) in segs:
            do_seg(sn_, n_, t_)
        if stop_after == ("mix", l):
            return None

        dst_res = res[(2 * l) % 3]
        wo = alloc([8, D], BF16)
        for kk_ in range(4):
            dma("pool", wo[:, 2 * kk_:2 * kk_ + 2, :], w_out[l].rearrange("(k p) f -> p k f", p=128)[:, 2 * kk_:2 * kk_ + 2, :], w=[("wo", kk_)])
        mx = [alloc([8, 512], BF16), alloc([8, 512], BF16)]
        ebufs = ([alloc([1024], F32), alloc([1024], F32)], [alloc([1024], F32), alloc([1024], F32)],
                 alloc([2, 8], F32), [alloc([512], F32), alloc([512], F32)])
        outs = []
        def mx_load(gi):
            t0, n = gl[gi]
            dma("pool", mx[gi % 2][:, :, 0:n], mixT[:, t0:t0 + n].rearrange("(c p) t -> p c t", p=128), w=[("mx", gi % 2)])
        mx_load(0)
        for gi, (t0, n) in enumerate(gl):
            mb = mx[gi % 2]
            if gi + 1 < len(gl):
                mx_load(gi + 1)
            for tp in range(0, n // 128, 2):
                sls = []
                for ti in (tp, tp + 1):
                    t = t0 // 128 + ti
                    pbs = [kb.psn(), kb.psn()]
                    for hb in range(2):
                        for k in range(8):
                            mm(PS[pbs[hb]], mb[:, k, ti * 128:(ti + 1) * 128], wo[:, k, hb * 512:(hb + 1) * 512], k == 0, k == 7,
                               r=[("mx", gi % 2), ("wo", k // 2)], w=[("ps", pbs[hb])])
                    sls.append(epilogue_steps(pbs, t, 0, cur_res, dst_res[t * 128:(t + 1) * 128, :], ebufs))
                run_interleaved(sls)
        kb.barrier(reset_to=layer_top)
        cur_res = dst_res
        if stop_after == ("wout", l):
            return None

        fT = alloc([8, T], BF16)
        mark_f = kb.top
        tiles = list(range(2, NT)) if last else list(range(NT))
        norm_phase(cur_res, 1, fT, tiles)
        kb.barrier(reset_to=mark_f)
        wu = [alloc([2, 8, 128], BF16), alloc([2, 8, 128], BF16)]
        Ug = [[alloc([T + 4], BF16), alloc([T + 4], BF16)], [alloc([T + 4], BF16), alloc([T + 4], BF16)]]
        Cb = [alloc([T + 4], F32), alloc([T + 4], F32)]
        SG = alloc([T + 4], BF16)
        TAB = [alloc([T + 4], BF16), alloc([T + 4], BF16)]
        gst = [alloc([T + 4], BF16), alloc([T + 4], BF16)]
        for jb_ in range(2):
            for gv_ in range(2):
                op("dve", lambda e, jb_=jb_, gv_=gv_: e.memset(Ug[jb_][gv_], 0.0), w=[("Ug", jb_, gv_)])
        wuv = w_up[l].rearrange("(k p) f -> p k f", p=128)
        W_ = T + 2

        def conv_steps(j):
            jb = j % 2
            cw_ = lambda gv, k: pp[:, P_fcw + (gv * 22 + j) * 3 + k:P_fcw + (gv * 22 + j) * 3 + k + 1]
            cb_ = lambda gv: pp[:, P_fcb + gv * 22 + j:P_fcb + gv * 22 + j + 1]
            st_ = []
            ugk = lambda gv: [("Ug", jb, gv, gi_) for gi_ in range(len(gl))] + [("Ug", jb, gv)]
            for gv in range(2):
                st_.append(lambda gv=gv: op("act", _act(Cb[gv][:, 1:1 + W_], Ug[jb][gv][:, 1:1 + W_], AF.Identity,
                                                        bias=cb_(gv), scale=cw_(gv, 1)),
                                            r=ugk(gv), w=[("Cb", gv)]))
            for gv in range(2):
                for k in (0, 2):
                    st_.append(lambda gv=gv, k=k: op("dve", lambda e: e.scalar_tensor_tensor(
                        out=Cb[gv][:, 1:1 + W_], in0=Ug[jb][gv][:, k:k + W_], scalar=cw_(gv, k), in1=Cb[gv][:, 1:1 + W_],
                        op0=ALU.mult, op1=ALU.add), r=ugk(gv) + [("Cb", gv)], w=[("Cb", gv)]))
            st_.append(lambda: op("act", _act(SG[:, 1:1 + W_], Cb[0][:, 1:1 + W_], AF.Silu), r=[("Cb", 0)], w=["SG"]))
            st_.append(lambda: op("dve", lambda e: e.tensor_tensor(out=gst[jb][:, 1:1 + W_], in0=Cb[1][:, 1:1 + W_], in1=SG[:, 1:1 + W_],
                                                                   op=ALU.mult), r=[("Cb", 1), "SG"], w=[("gst", jb)]))

            def s_out():
                if not last:
                    dma("sp", gT[j * 128:(j + 1) * 128, 0:256], gst[jb][:, 1:257], r=[("gst", jb)], w=[("gT", j, 0)])
                dma("sp", gT[j * 128:(j + 1) * 128, 256:T], gst[jb][:, 259:259 + L], r=[("gst", jb)], w=[("gT", j, 1)])
            st_.append(s_out)
            return st_

        pending = []
        for j in range(22):
            jb = j % 2
            for gv in range(2):
                c0 = gv * DFF + j * 128
                dma("pool", wu[jb][:, gv, :, :], wuv[:, :, c0:c0 + 128], w=[("wu", jb, gv)])
                for gi, (t0, n) in enumerate(gl):
                    pb = kb.psn()
                    for k in range(8):
                        mm(PS[pb][:, 0:n], wu[jb][:, gv, k, :], fT[:, k, t0:t0 + n], k == 0, k == 7,
                           r=[("wu", jb, gv)] + [("hT", t) for t in range(t0 // 128, (t0 + n) // 128)], w=[("ps", pb)])
                    pc0 = padcol(t0)
                    op("act", _act(Ug[jb][gv][:, pc0:pc0 + n], PS[pb][:, 0:n], AF.Identity), r=[("ps", pb), ("Ug", jb, gv)], w=[("Ug", jb, gv, gi)])
                    if pending and gi % 2 == 1:
                        pending.pop(0)()
            while pending:
                pending.pop(0)()
            pending = conv_steps(j)
        while pending:
            pending.pop(0)()
        kb.barrier(reset_to=layer_top)
        wd = alloc([22, D], BF16)
        for kk_ in range(11):
            dma("pool", wd[:, 2 * kk_:2 * kk_ + 2, :], w_down[l].rearrange("(k p) f -> p k f", p=128)[:, 2 * kk_:2 * kk_ + 2, :], w=[("wd", kk_)])
        gb = [alloc([22, 512], BF16), alloc([22, 512], BF16)]
        ebufs = ([alloc([1024], F32), alloc([1024], F32)], [alloc([1024], F32), alloc([1024], F32)],
                 alloc([2, 8], F32), [alloc([512], F32), alloc([512], F32)])
        final_ops = []
        def gb_load(gi):
            t0, n = gl[gi]
            dma("pool", gb[gi % 2][:, :, 0:n], gT[:, t0:t0 + n].rearrange("(c p) t -> p c t", p=128),
                r=[("gT", j, 0) for j in range(22)] + [("gT", j, 1) for j in range(22)], w=[("gb", gi % 2)])
        gb_load(0)
        for gi, (t0, n) in enumerate(gl):
            g_ = gb[gi % 2]
            if gi + 1 < len(gl):
                gb_load(gi + 1)
            for tp in range(0, n // 128, 2):
                sls = []
                for ti in (tp, tp + 1):
                    t = t0 // 128 + ti
                    pbs = [kb.psn(), kb.psn()]
                    for hb in range(2):
                        for k in range(22):
                            mm(PS[pbs[hb]], g_[:, k, ti * 128:(ti + 1) * 128], wd[:, k, hb * 512:(hb + 1) * 512], k == 0, k == 21,
                               r=[("gb", gi % 2), ("wd", k // 2)], w=[("ps", pbs[hb])])
                    if last:
                        dst = yout[(t - 2) * 128:(t - 1) * 128, :]
                    else:
                        dst = res[(2 * l + 1) % 3][t * 128:(t + 1) * 128, :]
                    sls.append(epilogue_steps(pbs, t, 1, cur_res, dst, ebufs))
                run_interleaved(sls)
        kb.barrier(reset_to=layer_top)
        cur_res = res[(2 * l + 1) % 3]

        return cur_res

    for l in range(nlayers):
        cur_res = do_layer(l, cur_res)
        if cur_res is None:
            break
    S.emit(nc, [S.bar])
    return kb


_KB = None


def make_in_maps(inp):
    cs = host_consts()
    rows = pack_rows(inp)
    maps = []
    f = lambda a: np.ascontiguousarray(a, dtype=np.float32)
    shared = {k: f(inp[k]) for k in ["w_mod", "w_in", "w_out", "w_up", "w_down", "w_fourier", "hy_w1", "hy_w2", "hy_w3"]}
    for b in range(8):
        m = {"xb": f(inp["x"][b]), "ctxb": f(inp["ctx"][b]), "pp": pack_pp(inp, b), "rows": rows}
        m.update(shared)
        m.update(cs)
        maps.append(m)
    return maps


def kernel(**inputs):
    global _KB
    inp = {k: np.asarray(v) for k, v in inputs.items()}
    if _KB is None:
        _KB = build()
    maps = make_in_maps(inp)
    r = run_bass_kernel_spmd(_KB.nc, maps, core_ids=list(range(8)))
    return np.stack([np.asarray(r.results[b]["y"], dtype=np.float32) for b in range(8)], axis=0)
```
